# Optimizing a Trainium2 kernel written in Bass

```python
import math
import jax, jax.numpy as jnp
from jax import lax
import numpy as np

D_MODEL = 1024
BATCH = 1
SEQ = 16384
DEPTH = 1
DEC_BATCH = 16
DEC_SEQ = 2048
PAST_LEN = 128

BLK = 128
WINDOW = 128
HA = 8
HKV_A = 2
G_A = HA // HKV_A
DH_A = 64
WIDTH_A = HA * DH_A
NUM_BUCKETS = 32
MAX_DISTANCE = 128
HB = 8
Q_LORA = 256
KV_LORA = 128
NOPE = 64
ROPE = 32
DV = 64
WIDTH_B = HB * DV
ROPE_THETA = 10000.0
EPS = 1e-6
SPLITS = [WIDTH_A, HKV_A * DH_A, HKV_A * DH_A, WIDTH_A, Q_LORA, KV_LORA, ROPE, WIDTH_B, D_MODEL, D_MODEL]
IN_COLS = sum(SPLITS)

kernel_name = "hybrid_wingqa_mla_gated_encoder"


def rmsnorm(x, g):
    xf = x.astype(jnp.float32)
    y = xf * lax.rsqrt(jnp.mean(xf * xf, axis=-1, keepdims=True) + EPS)
    return (y * g.astype(jnp.float32)).astype(x.dtype)


def t5_bucket(rel):
    nb = NUM_BUCKETS // 2
    max_exact = nb // 2
    ret = (rel > 0).astype(jnp.int32) * nb
    n = jnp.abs(rel)
    nf = jnp.maximum(n, 1).astype(jnp.float32)
    large = max_exact + (jnp.log(nf / max_exact) / math.log(MAX_DISTANCE / max_exact)
                         * (nb - max_exact)).astype(jnp.int32)
    large = jnp.minimum(large, nb - 1)
    return ret + jnp.where(n < max_exact, n, large)


def rope(t):
    S, R = t.shape[1], t.shape[-1]
    half = R // 2
    inv = jnp.power(ROPE_THETA, -jnp.arange(half, dtype=jnp.float32) / half)
    ang = jnp.arange(S, dtype=jnp.float32)[:, None] * inv[None, :]
    cos = jnp.cos(ang)[None, :, None, :]
    sin = jnp.sin(ang)[None, :, None, :]
    tf = t.astype(jnp.float32)
    t1, t2 = tf[..., :half], tf[..., half:]
    return jnp.concatenate([t1 * cos - t2 * sin, t1 * sin + t2 * cos], axis=-1).astype(t.dtype)


def window_gqa(q, k, v, sink, rel_bias):
    B, S = q.shape[0], q.shape[1]
    nb = S // BLK
    qb = q.reshape(B, nb, BLK, HKV_A, G_A, DH_A)

    def windows(t):
        tp = jnp.pad(t, ((0, 0), (BLK, BLK), (0, 0), (0, 0))).reshape(B, nb + 2, BLK, HKV_A, DH_A)
        return jnp.concatenate([tp[:, :-2], tp[:, 1:-1], tp[:, 2:]], axis=2)

    kw, vw = windows(k), windows(v)
    s = jnp.einsum('bnqhgd,bnkhd->bnhgqk', qb, kw,
                   preferred_element_type=jnp.float32) * (DH_A ** -0.5)
    qi = jnp.arange(BLK, dtype=jnp.int32)[:, None]
    kj = jnp.arange(3 * BLK, dtype=jnp.int32)[None, :]
    rel = kj - BLK - qi
    bias = rel_bias.astype(jnp.float32)[t5_bucket(rel)]
    bias = bias.transpose(2, 0, 1).reshape(HKV_A, G_A, BLK, 3 * BLK)
    kabs = jnp.arange(nb, dtype=jnp.int32)[:, None] * BLK + kj - BLK
    valid = (jnp.abs(rel) <= WINDOW)[None] & ((kabs >= 0) & (kabs < S))[:, None, :]
    s = jnp.where(valid[None, :, None, None], s + bias[None, None], -1e30)
    sk = sink.astype(jnp.float32).reshape(HKV_A, G_A)[:, :, None]
    m = jnp.maximum(jnp.max(s, axis=-1), sk)
    p = jnp.exp(s - m[..., None])
    denom = jnp.sum(p, axis=-1) + jnp.exp(sk - m)
    p = (p / denom[..., None]).astype(v.dtype)
    o = jnp.einsum('bnhgqk,bnkhd->bnqhgd', p, vw)
    return o.reshape(B, S, WIDTH_A)


def mla_attn(qn, qr, kn, kr, v):
    B, S = qn.shape[0], qn.shape[1]
    nb = S // BLK
    scale = (NOPE + ROPE) ** -0.5

    def to_blocks(t):
        return t.reshape(B, nb, BLK, *t.shape[2:]).swapaxes(0, 1)

    def one(blk):
        qn_b, qr_b = blk
        s = (jnp.einsum('bqhd,bkhd->bhqk', qn_b, kn, preferred_element_type=jnp.float32)
             + jnp.einsum('bqhr,bkr->bhqk', qr_b, kr, preferred_element_type=jnp.float32)) * scale
        p = jax.nn.softmax(s, axis=-1).astype(v.dtype)
        return jnp.einsum('bhqk,bkhd->bqhd', p, v)

    o = lax.map(one, (to_blocks(qn), to_blocks(qr)))
    return o.swapaxes(0, 1).reshape(B, S, WIDTH_B)


def hybrid_layer(x, ln_g, w_in, b_gate, sink_a, q_norm_g, kv_norm_g, w_uq, w_ukv,
                 w_proj_a, w_proj_b, w_out, rel_bias):
    B, S, _ = x.shape
    h = rmsnorm(x, ln_g)
    u = h @ w_in
    idx = [sum(SPLITS[:i + 1]) for i in range(len(SPLITS) - 1)]
    qa, ka, va, za, cq, ckv, kr, zb, ga, gb = jnp.split(u, idx, axis=-1)
    ya = window_gqa(qa.reshape(B, S, HA, DH_A), ka.reshape(B, S, HKV_A, DH_A),
                    va.reshape(B, S, HKV_A, DH_A), sink_a, rel_bias)
    oa = (ya * jax.nn.silu(za)) @ w_proj_a
    qf = (rmsnorm(cq, q_norm_g) @ w_uq).reshape(B, S, HB, NOPE + ROPE)
    qn, qr = qf[..., :NOPE], rope(qf[..., NOPE:])
    kvf = (rmsnorm(ckv, kv_norm_g) @ w_ukv).reshape(B, S, HB, NOPE + DV)
    kn, vb = kvf[..., :NOPE], kvf[..., NOPE:]
    krr = rope(kr[:, :, None, :])[:, :, 0, :]
    yb = mla_attn(qn, qr, kn, krr, vb)
    ob = (yb * jax.nn.silu(zb)) @ w_proj_b
    merged = jax.nn.sigmoid(ga + b_gate[:D_MODEL]) * oa + jax.nn.sigmoid(gb + b_gate[D_MODEL:]) * ob
    return x + merged @ w_out


def setup_inputs(seed: int = 0) -> dict:
    key = jax.random.key(seed)
    ks = jax.random.split(key, 16)
    f = jnp.float32
    nrm = lambda k, shape, s: jax.random.normal(k, shape, f) * s
    return {
        "x_prompt": nrm(ks[0], (BATCH, SEQ, D_MODEL), 1.0),
        "x_sample": nrm(ks[1], (DEC_BATCH, DEC_SEQ, D_MODEL), 1.0),
        "ln_g": 1.0 + nrm(ks[2], (DEPTH, D_MODEL), 0.02),
        "w_in": nrm(ks[3], (DEPTH, D_MODEL, IN_COLS), D_MODEL ** -0.5),
        "b_gate": nrm(ks[4], (DEPTH, 2 * D_MODEL), 0.1),
        "sink_a": nrm(ks[5], (DEPTH, HA), 0.5),
        "q_norm_g": 1.0 + nrm(ks[6], (DEPTH, Q_LORA), 0.02),
        "kv_norm_g": 1.0 + nrm(ks[7], (DEPTH, KV_LORA), 0.02),
        "w_uq": nrm(ks[8], (DEPTH, Q_LORA, HB * (NOPE + ROPE)), Q_LORA ** -0.5),
        "w_ukv": nrm(ks[9], (DEPTH, KV_LORA, HB * (NOPE + DV)), KV_LORA ** -0.5),
        "w_proj_a": nrm(ks[10], (DEPTH, WIDTH_A, D_MODEL), WIDTH_A ** -0.5),
        "w_proj_b": nrm(ks[11], (DEPTH, WIDTH_B, D_MODEL), WIDTH_B ** -0.5),
        "w_out": nrm(ks[12], (DEPTH, D_MODEL, D_MODEL), D_MODEL ** -0.5),
        "rel_bias": nrm(ks[13], (NUM_BUCKETS, HA), 0.3),
        "final_g": 1.0 + nrm(ks[14], (D_MODEL,), 0.02),
    }


def reference(x_prompt, x_sample, ln_g, w_in, b_gate, sink_a, q_norm_g, kv_norm_g, w_uq, w_ukv,
              w_proj_a, w_proj_b, w_out, rel_bias, final_g):
    def trunk(x):
        for l in range(DEPTH):
            x = hybrid_layer(x, ln_g[l], w_in[l], b_gate[l], sink_a[l], q_norm_g[l], kv_norm_g[l],
                             w_uq[l], w_ukv[l], w_proj_a[l], w_proj_b[l], w_out[l], rel_bias)
        return rmsnorm(x, final_g)

    y_prompt = trunk(x_prompt)
    y_sample = trunk(x_sample)
    return (y_prompt, y_sample)
```

```python
import math
import contextlib
import numpy as np
import concourse.bass as bass
import concourse.mybir as mybir
from concourse.bass_utils import run_bass_kernel_spmd

F32 = mybir.dt.float32
BF16 = mybir.dt.bfloat16
AF = mybir.ActivationFunctionType
ALU = mybir.AluOpType

ENGS = ("pe", "act", "dve", "pool", "sp")
SYNC_SAME = ("act", "dve", "pool")

D = 1024
HA, HKV, DH = 8, 2, 64
HB, QL, KVL, NOPE, ROPE, DV = 8, 256, 128, 64, 32, 64
NBUCK, MAXD = 32, 128
EPS = 1e-6
C_QA, C_KA, C_VA, C_ZA, C_CQ, C_CKV, C_KR, C_ZB, C_GA, C_GB, C_END = 0, 512, 640, 768, 1280, 1536, 1664, 1696, 2208, 3232, 4256


class Cfg:
    def __init__(self, NC=8, SEQ=16384, DB=16, DSEQ=2048, QP=1024, CH=512):
        self.NC, self.SEQ, self.DB, self.DSEQ, self.QP, self.CH = NC, SEQ, DB, DSEQ, QP, CH
        self.OWNP = SEQ // NC
        self.NS = DB // NC
        self.segs = [(SEQ, self.OWNP)] + [(DSEQ, DSEQ)] * self.NS
        self.passes = []
        for s, (T, own) in enumerate(self.segs):
            assert own % QP == 0 and T % CH == 0 and QP % CH == 0
            for hf in range(own // QP):
                self.passes.append((s, hf))
        self.NPASS = len(self.passes)
        self.NBQ = QP // 128
        self.TK = SEQ + self.NS * DSEQ
        self.TMAX = max(SEQ, DSEQ)
        self.kbase = [0] + [SEQ + i * DSEQ for i in range(self.NS)]


class Prog:
    def __init__(self, nc):
        self.nc = nc
        self.ops = []

    def op(self, eng, fn, reads=(), writes=(), dma=None):
        self.ops.append((eng, fn, tuple(reads), tuple(writes), dma))

    def emit(self, final_wait_eng="sp"):
        nc, ops = self.nc, self.ops
        n = len(ops)
        last_w, readers = {}, {}
        deps = [None] * n
        raw = [None] * n
        for i, (eng, fn, rds, wrs, dma) in enumerate(ops):
            d = set()
            rw = set()
            for r in rds:
                j = last_w.get(r)
                if j is not None:
                    d.add(j)
                    rw.add(j)
            raw[i] = rw
            for w in wrs:
                j = last_w.get(w)
                if j is not None:
                    d.add(j)
                rl = readers.get(w)
                if rl:
                    d.update(rl)
            for r in rds:
                readers.setdefault(r, []).append(i)
            for w in wrs:
                last_w[w] = i
                readers[w] = []
            d.discard(i)
            deps[i] = d
        need_sig = [False] * n
        for i in range(n):
            e = ops[i][0]
            for j in deps[i]:
                if ops[j][4] is None and (ops[j][0] != e or e in SYNC_SAME):
                    need_sig[j] = True
        dma_keys = []
        seenk = set()
        for o in ops:
            if o[4] is not None and o[4] not in seenk:
                seenk.add(o[4])
                dma_keys.append(o[4])
        with contextlib.ExitStack() as st:
            eng_sems = {e: st.enter_context(nc.semaphore("s_" + e)) for e in ENGS}
            dma_sems = {k: st.enter_context(nc.semaphore("d_%d" % i)) for i, k in enumerate(dma_keys)}
            cnt = {e: 0 for e in ENGS}
            dcnt = {k: 0 for k in dma_keys}
            sig = [None] * n
            for i, (eng, fn, rds, wrs, dma) in enumerate(ops):
                if dma is not None:
                    dcnt[dma] += 16
                    sig[i] = (dma_sems[dma], dcnt[dma], ("d", dma))
                elif need_sig[i]:
                    cnt[eng] += 1
                    sig[i] = (eng_sems[eng], cnt[eng], ("e", eng))
            per_eng = {e: [] for e in ENGS}
            for i, o in enumerate(ops):
                per_eng[o[0]].append(i)
            block = st.enter_context(nc.Block())

            def make_body(e):
                def body(eng):
                    seen = {}
                    for i in per_eng[e]:
                        _, fn, _, _, dma = ops[i]
                        need = {}
                        for j in deps[i]:
                            s = sig[j]
                            if s is None:
                                continue
                            if ops[j][4] is None and ops[j][0] == e and e not in SYNC_SAME:
                                continue
                            sem, val, key = s
                            if seen.get(key, 0) >= val:
                                continue
                            if key not in need or need[key][1] < val:
                                need[key] = (sem, val)
                        for key, (sem, val) in need.items():
                            eng.wait_ge(sem, val)
                            seen[key] = val
                        ins = fn(eng)
                        if sig[i] is not None:
                            ins.then_inc(sig[i][0], 16 if dma is not None else 1)
                    if e == final_wait_eng:
                        for k in dma_keys:
                            if seen.get(("d", k), 0) < dcnt[k]:
                                eng.wait_ge(dma_sems[k], dcnt[k])
                        for e2 in ENGS:
                            if e2 != e and cnt[e2] > 0 and seen.get(("e", e2), 0) < cnt[e2]:
                                eng.wait_ge(eng_sems[e2], cnt[e2])
                return body

            block.tensor(make_body("pe"))
            block.scalar(make_body("act"))
            block.vector(make_body("dve"))
            block.gpsimd(make_body("pool"))
            block.sync(make_body("sp"))
        return n


def t5_bucket_np(rel):
    nb = NBUCK // 2
    max_exact = nb // 2
    ret = (rel > 0).astype(np.int32) * nb
    n = np.abs(rel)
    nf = np.maximum(n, 1).astype(np.float32)
    large = max_exact + (np.log(nf / max_exact) / math.log(MAXD / max_exact) * (nb - max_exact)).astype(np.int32)
    large = np.minimum(large, nb - 1)
    return ret + np.where(n < max_exact, n, large)


def build_program(cfg):
    nc = bass.Bass("TRN2", target_bir_lowering=False)
    P = Prog(nc)
    CH, QP, NBQ, NPASS, TK, TMAX = cfg.CH, cfg.QP, cfg.NBQ, cfg.NPASS, cfg.TK, cfg.TMAX
    NBW = NBQ + 2
    NGQ = QP // CH
    NBC = CH // 128
    TREG = max(cfg.SEQ, cfg.DSEQ)

    def din(name, shape, dt=F32):
        return nc.dram_tensor(name, list(shape), dt, kind="ExternalInput").ap()

    xp = din("xp", [cfg.SEQ, D])
    xs = din("xs", [cfg.NS * cfg.DSEQ, D])
    xq = din("xq", [NPASS * NBW * 128, D])
    flags = din("flags", [NPASS * 2])
    ropek = din("ropek", [64, TMAX])
    ropeq = din("ropeq", [NPASS, 64, QP])
    w_in = din("w_in", [D, C_END])
    ln_g = din("ln_g", [128, 8])
    b_gate = din("b_gate", [128, 16])
    sink_a = din("sink_a", [8])
    q_norm_g = din("q_norm_g", [128, 2])
    kv_norm_g = din("kv_norm_g", [128, 1])
    w_uq = din("w_uq", [QL, HB * 96])
    w_ukv = din("w_ukv", [KVL, HB * 128])
    w_proj_a = din("w_proj_a", [512, D])
    w_proj_b = din("w_proj_b", [512, D])
    w_out = din("w_out", [D, D])
    rel_bias = din("rel_bias", [NBUCK, HA])
    final_g = din("final_g", [D])
    onehot = din("onehot", [NBUCK, 512])
    validv = din("validv", [8, 512])
    yout = nc.dram_tensor("y", [NPASS * QP, D], F32, kind="ExternalOutput").ap()

    NWD = 24576 + 4096 * 3 + 8192
    wd_img = nc.dram_tensor("wd_img", [128, NWD], BF16, kind="Internal").ap()
    kvc = nc.dram_tensor("kvc", [160, TK], BF16, kind="Internal").ap()
    evec_t = nc.dram_tensor("evec", [8, 512], F32, kind="Internal")
    etoe_t = nc.dram_tensor("etoe", [8 * 128 * 512], F32, kind="Internal")

    sb = nc.alloc_sbuf_tensor
    O_CN, O_KT, O_V = 0, TREG, 2 * TREG
    NV = (TREG // 128) * 96
    O_QT = O_V + NV
    NREG = max(O_QT + 2 * QP, NWD)
    RG = sb("RG", [128, NREG], BF16)
    cnT = RG[:, O_CN:O_CN + TREG]
    KT = RG[:, O_KT:O_KT + TREG]
    Vb = RG[:, O_V:O_V + NV].rearrange("p (t c) -> p t c", c=96)
    QTs = [RG[:, O_QT + i * QP:O_QT + (i + 1) * QP] for i in range(2)]
    WQA = RG[:, 0:4096].rearrange("p (k c) -> p k c", k=8)
    WG = RG[:, 4096:28672].rearrange("p (k c) -> p k c", k=8)
    WPA = RG[:, 28672:32768].rearrange("p (k c) -> p k c", k=4)
    WPB = RG[:, 32768:36864].rearrange("p (k c) -> p k c", k=4)
    WO = RG[:, 36864:45056].rearrange("p (k c) -> p k c", k=8)

    hTo = sb("hTo", [128, 8 * QP], BF16)
    hTo3 = hTo[:, :].rearrange("p (k c) -> p k c", k=8)
    HR = sb("HR", [128, max(2048, 2 * QP)], BF16)
    hTh = [HR[:, i * 1024:(i + 1) * 1024] for i in range(2)]
    cqnT = sb("cqnT", [128, 2 * QP], BF16)
    cqn3 = cqnT[:, :].rearrange("p (k c) -> p k c", k=2)
    ybT = sb("ybT", [128, 4 * QP], BF16)
    yb3 = ybT[:, :].rearrange("p (k c) -> p k c", k=4)
    kaT = sb("kaT", [128, NBW * 128], BF16)
    vaw = sb("vaw", [128, NBW * 2 * 128], BF16)
    vaw4 = vaw[:, :].rearrange("p (b g c) -> p b g c", b=NBW, g=2)
    WA = sb("WA", [128, 8 * 192], BF16)
    WA3 = WA[:, :].rearrange("p (k c) -> p k c", k=8)
    WB = sb("WB", [128, 8 * 512], BF16)
    WB3 = WB[:, :].rearrange("p (k c) -> p k c", k=8)
    WUQ = sb("WUQ", [128, 2 * 8 * 128], BF16)
    WUQ4 = WUQ[:, :].rearrange("p (k h c) -> p k h c", k=2, h=8)
    WUKV = sb("WUKV", [128, 8 * 128], BF16)
    WUKV3 = WUKV[:, :].rearrange("p (h c) -> p h c", h=8)
    gcol = sb("gcol", [128, 8], F32)
    bg = sb("bg", [128, 16], F32)
    gq = sb("gq", [128, 2], F32)
    gkv = sb("gkv", [128, 1], F32)
    fgB = sb("fgB", [128, D], F32)
    epsT = sb("epsT", [128, 1], F32)
    ident = sb("ident", [128, 128], BF16)
    onesf = sb("onesf", [128, 128], F32)
    esk = sb("esk", [128, 8], F32)
    esinkT = sb("esinkT", [128, 2 * 512], BF16)
    esink3 = esinkT[:, :].rearrange("p (g c) -> p g c", g=2)
    Etab = sb("Etab", [128, 6 * 512], BF16)
    Etab4 = Etab[:, :].rearrange("p (g t c) -> p g t c", g=2, t=3)
    flg = sb("flg", [128, NPASS * 2], F32)
    ropq = HR[:, 0:2 * QP].bitcast(F32)
    NXT = 4
    xt = [sb("xt%d" % i, [128, D], F32) for i in range(NXT)]
    xsb = [sb("xsb%d" % i, [128, D], BF16) for i in range(2)]
    ss = [sb("ss%d" % i, [128, 4], F32) for i in range(NXT)]
    NF = 5
    f32t = [sb("f32t%d" % i, [128, 512], F32) for i in range(NF)]
    b16t = [sb("b16t%d" % i, [128, 512], BF16) for i in range(4)]
    QY = sb("QY", [128, 4096], BF16)
    qaT3 = QY[:, 0:2048].rearrange("p (j c) -> p j c", j=4)
    yaT3 = QY[:, 2048:4096].rearrange("p (j c) -> p j c", j=4)
    rpf = QY[:, 0:2048].bitcast(F32)
    rpk = [rpf[0:64, i * 512:(i + 1) * 512] for i in range(2)]
    cng = [QY[:, 2048 + i * 512:2048 + (i + 1) * 512] for i in range(2)]
    krg = [QY[0:32, 3072 + i * 512:3072 + (i + 1) * 512] for i in range(2)]
    mTb = sb("mTb", [128, 4 * 512], BF16)
    mTl = [QY[:, m * 512:(m + 1) * 512] for m in range(4)] + [mTb[:, m * 512:(m + 1) * 512] for m in range(4)]
    fence_t = sb("fence_t", [128, 2], F32)
    pb = [nc.alloc_psum_tensor("pb%d" % i, [128, 512], F32) for i in range(8)]
    print("sbuf bytes remaining:", nc.sbuf_bytes_remaining)

    st = {"bank": 0, "nb": 7, "f": 0, "b": 0, "xt": 0}
    NWARM = 2

    live_banks = set()

    def bank(hold=False):
        for _ in range(16):
            i = st["bank"] % st["nb"]
            st["bank"] += 1
            if "pb%d" % i not in live_banks:
                break
        else:
            raise RuntimeError("no free PSUM bank")
        if hold:
            live_banks.add("pb%d" % i)
        return pb[i], "pb%d" % i

    def release(k):
        live_banks.discard(k)

    def ftmp():
        i = st["f"] % NF
        st["f"] += 1
        return f32t[i], "f32t%d" % i

    def btmp():
        i = st["b"] % 4
        st["b"] += 1
        return b16t[i], "b16t%d" % i

    def I(eng, name, *args, reads=(), writes=(), dma=None, **kw):
        P.op(eng, lambda e: getattr(e, name)(*args, **kw), reads, writes, dma)

    def DMA(eng, out, in_, reads=(), writes=(), key=None):
        I(eng, "dma_start", out=out, in_=in_, reads=reads, writes=writes, dma=key)

    def MM(out, lhsT, rhs, start, stop, reads, writes):
        I("pe", "matmul", out, lhsT=lhsT, rhs=rhs, start=start, stop=stop, reads=reads, writes=writes)

    def ACT(out, in_, func, reads, writes, **kw):
        I("act", "activation", out=out, in_=in_, func=func, reads=reads, writes=writes, **kw)

    def TT(eng, out, in0, in1, op_, reads, writes):
        I(eng, "tensor_tensor", out=out, in0=in0, in1=in1, op=op_, reads=reads, writes=writes)

    def TS(eng, out, in0, s1, s2, op0, op1, reads, writes):
        if s2 is None:
            I(eng, "tensor_scalar", out=out, in0=in0, scalar1=s1, scalar2=None, op0=op0, reads=reads, writes=writes)
        else:
            I(eng, "tensor_scalar", out=out, in0=in0, scalar1=s1, scalar2=s2, op0=op0, op1=op1, reads=reads, writes=writes)

    def STT(eng, out, in0, scalar, in1, reads, writes):
        I(eng, "scalar_tensor_tensor", out=out, in0=in0, scalar=scalar, in1=in1, op0=ALU.mult, op1=ALU.mult, reads=reads, writes=writes)

    def CP(eng, out, in_, reads, writes):
        I(eng, "tensor_copy", out=out, in_=in_, reads=reads, writes=writes)

    def RCP(out, in_, reads, writes):
        I("dve", "reciprocal", out=out, in_=in_, reads=reads, writes=writes)

    I("pool", "memset", epsT[:], EPS, writes=["epsT"])
    I("pool", "memset", onesf[:], 1.0, writes=["onesf"])
    idf, idk = f32t[0], "f32t0"
    I("pool", "memset", idf[:, 0:128], 1.0, writes=[idk])
    I("pool", "affine_select", out=idf[:, 0:128], in_=idf[:, 0:128], pattern=[[-1, 128]], compare_op=ALU.is_equal,
      fill=0.0, base=0, channel_multiplier=1, reads=[idk], writes=[idk])
    CP("dve", ident[:], idf[:, 0:128], [idk], ["ident"])
    I("pool", "memset", vaw[:], 1.0, writes=["vaw_init"])
    for t_, d_, k_ in [(gcol, ln_g, "gcol"), (bg, b_gate, "bg"), (gq, q_norm_g, "gq"), (gkv, kv_norm_g, "gkv")]:
        DMA("sp", t_[:], d_[:, :], writes=[k_], key="ld_" + k_)
    DMA("sp", fgB[:], bass.AP(final_g.tensor, 0, [[0, 128], [1, D]]), writes=["fgB"], key="ld_fgB")
    DMA("sp", esk[:], bass.AP(sink_a.tensor, 0, [[0, 128], [1, 8]]), writes=["esk"], key="ld_esk")
    DMA("sp", flg[:], bass.AP(flags.tensor, 0, [[0, 128], [1, NPASS * 2]]), writes=["flg"], key="ld_flg")
    ACT(esk[:], esk[:], AF.Exp, ["esk"], ["esk"])
    for h in range(8):
        g, i = h // 4, h % 4
        CP("dve", esink3[:, g, i * 128:(i + 1) * 128], esk[:, h:h + 1].to_broadcast([128, 128]), ["esk"], ["esinkT"])

    rbt, oht, vvt, ev = f32t[1], f32t[2], f32t[3], f32t[4]
    DMA("sp", rbt[0:NBUCK, 0:8], rel_bias[:, :], writes=["f32t1"], key="ld_rb")
    DMA("sp", oht[0:NBUCK, :], onehot[:, :], writes=["f32t2"], key="ld_oh")
    DMA("sp", vvt[0:8, :], validv[:, :], writes=["f32t3"], key="ld_vv")
    MM(pb[0][0:8, :], rbt[0:NBUCK, 0:8], oht[0:NBUCK, :], True, True, ["f32t1", "f32t2"], ["pb0"])
    ACT(ev[0:8, :], pb[0][0:8, :], AF.Exp, ["pb0"], ["f32t4"])
    TT("dve", ev[0:8, :], ev[0:8, :], vvt[0:8, :], ALU.mult, ["f32t4", "f32t3"], ["f32t4"])
    DMA("sp", evec_t.ap()[:, :], ev[0:8, :], reads=["f32t4"], writes=["evec"], key="st_evec")
    DMA("sp", bass.AP(etoe_t, 0, [[128 * 512, 8], [512, 128], [1, 511]]), bass.AP(evec_t, 0, [[512, 8], [0, 128], [1, 511]]),
        reads=["evec"], writes=["etoe"], key="st_etoe")
    CT = [383, 255, 127]
    st["f"] = 0
    for g in range(2):
        for t in range(3):
            ft, fk = ftmp()
            DMA("sp", ft[:, :].rearrange("p (h c) -> p h c", h=4),
                bass.AP(etoe_t, 4 * g * 128 * 512 + CT[t], [[511, 128], [128 * 512, 4], [1, 128]]),
                reads=["etoe"], writes=[fk], key="ld_E%d%d" % (g, t))
            CP("dve", Etab4[:, g, t, :], ft[:, :], [fk], ["Etab"])

    def stage(src_ap, ncols):
        i = st["xt"] % NXT
        st["xt"] += 1
        DMA("sp", xt[i][:, 0:ncols], src_ap, writes=["xt%d" % i], key="ld_xt%d" % i)
        return xt[i], ["xt%d" % i, "gcol", "RG"]

    for k in range(8):
        rows = slice(k * 128, (k + 1) * 128)
        gk = gcol[:, k:k + 1]
        x_, rd = stage(w_in[rows, 0:768], 768)
        for hf in range(2):
            TS("dve", WQA[:, k, :].rearrange("p (j a c) -> p j a c", j=4, a=2)[:, :, hf, :],
               x_[:, hf * 256:(hf + 1) * 256].rearrange("p (j c) -> p j c", j=4), gk, None, ALU.mult, None, rd, ["WQA"])
        TS("dve", WB3[:, k, 256:512], x_[:, 512:768], gk, None, ALU.mult, None, rd, ["WB"])
        x_, rd = stage(w_in[rows, 768:1696], 928)
        TS("dve", WG[:, k, 0:512], x_[:, 0:512], gk, None, ALU.mult, None, rd, ["WG"])
        TS("dve", WB3[:, k, 0:256], x_[:, 512:768], gk, None, ALU.mult, None, rd, ["WB"])
        TS("dve", WA3[:, k, 0:160], x_[:, 768:928], gk, None, ALU.mult, None, rd, ["WA"])
        TS("dve", WA3[:, k, 176:192], x_[:, 896:912], gk, None, ALU.mult, None, rd, ["WA"])
        TS("dve", WA3[:, k, 160:176], x_[:, 912:928], gk, -1.0, ALU.mult, ALU.mult, rd, ["WA"])
        x_, rd = stage(w_in[rows, 1696:2720], 1024)
        TS("dve", WG[:, k, 512:1536], x_[:, 0:1024], gk, None, ALU.mult, None, rd, ["WG"])
        x_, rd = stage(w_in[rows, 2720:3744], 1024)
        TS("dve", WG[:, k, 1536:2560], x_[:, 0:1024], gk, None, ALU.mult, None, rd, ["WG"])
        x_, rd = stage(w_in[rows, 3744:4256], 512)
        TS("dve", WG[:, k, 2560:3072], x_[:, 0:512], gk, None, ALU.mult, None, rd, ["WG"])
    for k in range(2):
        x_, rd = stage(w_uq[k * 128:(k + 1) * 128, :], 768)
        x3 = x_[:, 0:768].rearrange("p (h c) -> p h c", h=8)
        CP("dve", WUQ4[:, k, :, 0:96], x3, rd, ["WUQ"])
        TS("dve", WUQ4[:, k, :, 96:112], x3[:, :, 80:96], -1.0, None, ALU.mult, None, rd, ["WUQ"])
        CP("dve", WUQ4[:, k, :, 112:128], x3[:, :, 64:80], rd, ["WUQ"])
    x_, rd = stage(w_ukv[:, :], 1024)
    CP("dve", WUKV[:, :], x_[:, :], rd, ["WUKV"])
    for k in range(4):
        x_, rd = stage(w_proj_a[k * 128:(k + 1) * 128, :], 1024)
        CP("dve", WPA[:, k, :], x_[:, :], rd, ["WPA"])
        x_, rd = stage(w_proj_b[k * 128:(k + 1) * 128, :], 1024)
        CP("dve", WPB[:, k, :], x_[:, :], rd, ["WPB"])
    for k in range(8):
        x_, rd = stage(w_out[k * 128:(k + 1) * 128, :], 1024)
        CP("dve", WO[:, k, :], x_[:, :], rd, ["WO"])
    wpieces = [(0, 4096), (4096, 16384), (16384, 28672), (28672, 36864), (36864, NWD)]
    NWP = len(wpieces)
    EARLY = [i for i in (0, 1) if wpieces[i][1] <= TREG]
    for i, (a, b) in enumerate(wpieces):
        DMA("sp", wd_img[:, a:b], RG[:, a:b], reads=["WG", "WQA", "WPA", "WPB", "WO", "RG"], writes=["wd_img%d" % i], key="st_wd%d" % i)

    dbg = getattr(cfg, "debug", False)

    def dump(name, ap, shape, dt, reads):
        if not dbg:
            return
        o = nc.dram_tensor(name, list(shape), dt, kind="ExternalOutput").ap()
        DMA("sp", o, ap, reads=reads, key="dbg_" + name)

    def fence():
        I("pool", "memset", fence_t[:], 0.0, writes=["RG", "fence_t"])

    def pipeline(tasks):
        active = []
        tasks = list(tasks)
        while tasks or active:
            if tasks:
                active.append(tasks.pop(0))
            nxt = []
            for g_ in active:
                try:
                    next(g_)
                    nxt.append(g_)
                except StopIteration:
                    pass
            active = nxt

    def load_norm_transpose(src_ap, dst_view, dst_keys, extra_reads=(), tail=None):
        i = st["xt"] % NXT
        st["xt"] += 1
        xtile, xk, sst, sk = xt[i], "xt%d" % i, ss[i], "ss%d" % i
        j = st["xt"] % 2
        xb, xbk = xsb[j], "xsb%d" % j
        DMA("sp", xtile[:, :], src_ap, writes=[xk], key="ld_" + xk)
        ACT(xb[:, :], xtile[:, :], AF.Square, [xk], [xbk, sk], accum_out=sst[:, 0:1])
        ACT(sst[:, 1:2], sst[:, 0:1], AF.Ln, [sk, "epsT"], [sk], scale=1.0 / D, bias=epsT[:, 0:1])
        ACT(sst[:, 2:3], sst[:, 1:2], AF.Exp, [sk], [sk], scale=-0.5)
        TS("dve", xb[:, :], xtile[:, :], sst[:, 2:3], None, ALU.mult, None, [xk, sk], [xbk])
        yield
        bk, bkk = bank(hold=True)
        bkb = bk[:, :].bitcast(BF16)
        for k in range(8):
            I("pe", "transpose", out=bkb[:, k * 128:(k + 1) * 128], in_=xb[:, k * 128:(k + 1) * 128], identity=ident[:],
              reads=[xbk, "ident"], writes=[bkk])
        for _ in range(NWARM):
            MM(pb[7][:, 0:512], ident[:], WB[:, 0:512], True, True, ["ident", "WB"], ["pb7"])
        yield
        CP("dve", dst_view, bkb[:, :].rearrange("p (k c) -> p k c", k=8), [bkk] + list(extra_reads), dst_keys)
        release(bkk)
        if tail is not None:
            yield
            r_ = tail()
            if r_ is not None:
                yield from r_

    if QP >= 2 * CH:
        hA = [hTo[:, i * 8 * CH:(i + 1) * 8 * CH].rearrange("p (k c) -> p k c", k=8) for i in range(2)]
    else:
        hAt = [sb("hA%d" % i, [128, 8 * CH], BF16) for i in range(2)]
        hA = [t_[:, :].rearrange("p (k c) -> p k c", k=8) for t_ in hAt]
    _ob = onesf[:, :].bitcast(BF16)
    onesb = bass.AP(_ob.tensor, _ob.offset + 1, [[_ob.ap[0][0], 128], [2, 128]])

    def phaseA_group_tail(grp, slot, tok0, pos0, hks):
        b1, b1k = bank(hold=True)
        for k in range(8):
            MM(b1[:, 0:CH], WA3[:, k, 0:128], hA[slot][:, k, :], k == 0, k == 7, hks + ["WA", "RG"], [b1k])
        b2, b2k = bank(hold=True)
        for k in range(8):
            MM(b2[0:64, 0:CH], WA3[:, k, 128:192], hA[slot][:, k, :], k == 0, k == 7, hks + ["WA", "RG"], [b2k])
        rp, rpkk = rpk[slot], "rpk%d" % slot
        DMA("sp", rp[:, 0:CH], ropek[:, pos0:pos0 + CH], reads=["RG"], writes=[rpkk], key="ld_" + rpkk)
        yield
        sq, sqk = btmp()
        ACT(sq[:, 0:CH], b1[:, 0:CH], AF.Square, [b1k], [sqk])
        b3, b3k = bank(hold=True)
        MM(b3[:, 0:CH], onesb, sq[:, 0:CH], True, True, [sqk, "onesf"], [b3k])
        ta, tak = ftmp()
        tb, tbk = ftmp()
        TT("dve", ta[0:32, 0:CH], b2[0:32, 0:CH], rp[0:32, 0:CH], ALU.mult, [b2k, rpkk, "RG"], [tak])
        TT("dve", tb[0:32, 0:CH], b2[32:64, 0:CH], rp[32:64, 0:CH], ALU.mult, [b2k, rpkk, "RG"], [tbk])
        release(b2k)
        yield
        rc, rck = ftmp()
        ACT(rc[:, 0:CH], b3[:, 0:CH], AF.Ln, [b3k, "epsT"], [rck], scale=1.0 / KVL, bias=epsT[:, 0:1])
        release(b3k)
        ACT(rc[:, 0:CH], rc[:, 0:CH], AF.Exp, [rck], [rck], scale=-0.5)
        kg, kgk = krg[slot], "krg%d" % slot
        TT("dve", kg[:, 0:CH], ta[0:32, 0:CH], tb[0:32, 0:CH], ALU.add, [tak, tbk, "RG"], [kgk])
        DMA("pool", kvc[128:160, tok0:tok0 + CH], kg[:, 0:CH], reads=[kgk, "RG"], writes=["kvck%d" % grp], key="st_" + kgk)
        yield
        cg, cgk = cng[slot], "cng%d" % slot
        STT("dve", cg[:, 0:CH], b1[:, 0:CH], gkv[:, 0:1], rc[:, 0:CH], [b1k, rck, "gkv", "RG"], [cgk])
        release(b1k)
        DMA("pool", kvc[0:128, tok0:tok0 + CH], cg[:, 0:CH], reads=[cgk, "RG"], writes=["kvcc%d" % grp], key="st_" + cgk)

    tasksA = []
    for grp in range(TK // CH):
        slot = grp % 2
        tok0 = grp * CH
        if tok0 < cfg.SEQ:
            src, r0, pos0 = xp, tok0, tok0
        else:
            src, r0 = xs, tok0 - cfg.SEQ
            pos0 = r0 % cfg.DSEQ
        hks = ["hA%d_%d" % (slot, tl) for tl in range(NBC)]
        for tl in range(NBC):
            tail = None
            if tl == NBC - 1:
                tail = (lambda grp=grp, slot=slot, tok0=tok0, pos0=pos0, hks=hks: phaseA_group_tail(grp, slot, tok0, pos0, hks))
            tasksA.append(load_norm_transpose(src[r0 + tl * 128: r0 + (tl + 1) * 128, :], hA[slot][:, :, tl * 128:(tl + 1) * 128],
                                              [hks[tl]], ["RG"], tail))
    pipeline(tasksA)

    dump("d_kvc", kvc[:, :], [160, TK], BF16, ["kvcc%d" % g_ for g_ in range(TK // CH)] + ["kvck%d" % g_ for g_ in range(TK // CH)])
    dump("d_E", Etab[:, :], [128, 6 * 512], BF16, ["Etab"])
    SC_A = DH ** -0.5
    SC_B = (NOPE + ROPE) ** -0.5
    ocnt = 0
    B_done = set()
    for p, (seg, hf) in enumerate(cfg.passes):
        T, own = cfg.segs[seg]
        kb = cfg.kbase[seg]
        NJ = T // CH
        NT = T // 128
        st["nb"] = 7
        fence()
        NLD = min(4, NJ)
        for i in range(NLD):
            j0, j1 = i * NJ // NLD, (i + 1) * NJ // NLD
            a, b = j0 * CH, j1 * CH
            kvr = ["kvcc%d" % ((kb + jj * CH) // CH) for jj in range(j0, j1)] + ["kvck%d" % ((kb + jj * CH) // CH) for jj in range(j0, j1)]
            DMA("pool", cnT[:, a:b], kvc[0:128, kb + a:kb + b], reads=["RG"] + kvr, writes=["cn%d" % j for j in range(j0, j1)], key="ld_cn%d" % i)
            DMA("pool", KT[64:96, a:b], kvc[128:160, kb + a:kb + b], reads=["RG"] + kvr, writes=["KTr%d" % j for j in range(j0, j1)], key="ld_kr%d" % i)
        I("pool", "memset", Vb[:, 0:NT, 64:96], 1.0, reads=["RG"], writes=["Vones"])
        def phaseB_tail(blk, hv, hkeys):
            bka, bkak = bank(hold=True)
            for k in range(8):
                MM(bka[:, 0:128], WB3[:, k, 256:384], hv[:, k, :], k == 0, k == 7, hkeys + ["WB"], [bkak])
            bkv, bkvk = bank(hold=True)
            for k in range(8):
                MM(bkv[:, 0:128], hv[:, k, :], WB3[:, k, 384:512], k == 0, k == 7, hkeys + ["WB"], [bkvk])
            yield
            CP("dve", kaT[:, blk * 128:(blk + 1) * 128], bka[:, 0:128], [bkak], ["kaT%d" % blk])
            CP("dve", vaw4[:, blk, :, 0:64], bkv[:, 0:128].rearrange("p (g c) -> p g c", g=2), [bkvk, "vaw_init"], ["vaw%d" % blk])
            release(bkak)
            release(bkvk)

        def make_B_tasks(pp):
          tasksB = []
          for blk in range(NBW):
            row0 = (pp * NBW + blk) * 128
            if blk == 0 or blk == NBW - 1:
                hi = 0 if blk == 0 else 1
                hv = hTh[hi].rearrange("p (k c) -> p k c", k=8)
                hkeys = ["hTh%d" % hi, "ropq"]
            else:
                hv = hTo3[:, :, (blk - 1) * 128: blk * 128]
                hkeys = ["hTo%d" % (blk - 1)]
            tasksB.append(load_norm_transpose(xq[row0:row0 + 128, :], hv, hkeys, ["RG"] if pp == 0 else [],
                                              (lambda blk=blk, hv=hv, hkeys=hkeys: phaseB_tail(blk, hv, hkeys))))
          return tasksB

        def emit_B_cq():
          for grp in range(NGQ):
              cols = slice(grp * CH, (grp + 1) * CH)
              hk = ["hTo%d" % b for b in range(grp * NBC, (grp + 1) * NBC)]
              bq = []
              for m in range(2):
                  bk, bkk = bank()
                  bq.append((bk, bkk))
                  for k in range(8):
                      MM(bk[:, 0:CH], WB3[:, k, m * 128:(m + 1) * 128], hTo3[:, k, cols], k == 0, k == 7, hk + ["WB"], [bkk])
              b3, b3k = bank()
              for m in range(2):
                  sq, sqk = btmp()
                  ACT(sq[:, 0:CH], bq[m][0][:, 0:CH], AF.Square, [bq[m][1]], [sqk])
                  MM(b3[:, 0:CH], onesb, sq[:, 0:CH], m == 0, m == 1, [sqk, "onesf"], [b3k])
              rc, rck = ftmp()
              ACT(rc[:, 0:CH], b3[:, 0:CH], AF.Ln, [b3k, "epsT"], [rck], scale=1.0 / QL, bias=epsT[:, 0:1])
              ACT(rc[:, 0:CH], rc[:, 0:CH], AF.Exp, [rck], [rck], scale=-0.5)
              for m in range(2):
                  STT("dve", cqn3[:, m, cols], bq[m][0][:, 0:CH], gq[:, m:m + 1], rc[:, 0:CH], [bq[m][1], rck, "gq"], ["cqn%d_%d" % (grp, m)])

        if p not in B_done:
            pipeline(make_B_tasks(p))
            emit_B_cq()
            B_done.add(p)

        if p == 0:
            dump("d_hTo", hTo[:, :], [128, 8 * QP], BF16, ["hTo%d" % b_ for b_ in range(NBQ)])
            dump("d_cqn", cqnT[:, :], [128, 2 * QP], BF16, ["cqn%d_%d" % (g_, m_) for g_ in range(NGQ) for m_ in range(2)])
            dump("d_ka", kaT[:, :], [128, NBW * 128], BF16, ["kaT%d" % b_ for b_ in range(NBW)])
            dump("d_va", vaw[:, :], [128, NBW * 2 * 128], BF16, ["vaw%d" % b_ for b_ in range(NBW)])
        DMA("sp", ropq[64:128, :], ropeq[p, :, :], writes=["ropq", "hTh0", "hTh1"], key="ld_ropq")
        st["nb"] = 2
        st["bank"] = 0
        ocount = 0
        def expK(h, j):
            bk, bkk = bank()
            MM(bk[0:64, 0:CH], WUKV3[:, h, 0:64], cnT[:, j * CH:(j + 1) * CH], True, True, ["cn%d" % j, "WUKV", "RG"], [bkk])
            CP("dve", KT[0:64, j * CH:(j + 1) * CH], bk[0:64, 0:CH], [bkk, "RG"], ["KTn%d" % j])

        def expV(h, t0):
            nt = min(8, NT - t0)
            bk, bkk = bank()
            for i in range(nt):
                MM(bk[:, i * 64:(i + 1) * 64], cnT[:, (t0 + i) * 128:(t0 + i + 1) * 128], WUKV3[:, h, 64:128], True, True,
                   ["cn%d" % ((t0 + i) * 128 // CH), "WUKV", "RG"], [bkk])
            CP("dve", Vb[:, t0:t0 + nt, 0:64], bk[:, 0:nt * 64].rearrange("p (t c) -> p t c", c=64), [bkk, "RG", "Vones"], ["V%d" % (t0 // 8)])

        def expQ(h, grp):
            QT = QTs[h % 2]
            qk_ = "QT%d" % (h % 2)
            cols = slice(grp * CH, (grp + 1) * CH)
            bk, bkk = bank()
            for k in range(2):
                MM(bk[:, 0:CH], WUQ4[:, k, h, :], cqn3[:, k, cols], k == 0, k == 1, ["cqn%d_%d" % (grp, k), "WUQ"], [bkk])
            CP("dve", QT[0:64, cols], bk[0:64, 0:CH], [bkk, "RG"], [qk_ + "n%d" % grp])
            ta, tak = ftmp()
            tb, tbk = ftmp()
            TT("dve", ta[64:96, 0:CH], bk[64:96, 0:CH], ropq[64:96, cols], ALU.mult, [bkk, "ropq"], [tak])
            TT("dve", tb[64:96, 0:CH], bk[96:128, 0:CH], ropq[96:128, cols], ALU.mult, [bkk, "ropq"], [tbk])
            TT("dve", QT[64:96, cols], ta[64:96, 0:CH], tb[64:96, 0:CH], ALU.add, [tak, tbk, "RG"], [qk_ + "r%d" % grp])

        for grp in range(NGQ):
            expQ(0, grp)
        for j in range(NJ):
            expK(0, j)
        for t0 in range(0, NT, 8):
            expV(0, t0)
        units = [(h, qc, t) for h in range(HB) for qc in range(NGQ) for t in range(NT)]
        LOOK = min(3, NT * NGQ)

        def qk(u):
            h, qc, t = units[u]
            QT = QTs[h % 2]
            qk_ = "QT%d" % (h % 2)
            s_, sk_ = pb[4 + (u % 4)], "pb%d" % (4 + u % 4)
            j = t * 128 // CH
            MM(s_[:, 0:CH], KT[0:96, t * 128:(t + 1) * 128], QT[0:96, qc * CH:(qc + 1) * CH], True, True,
               ["KTn%d" % j, "KTr%d" % j, qk_ + "n%d" % qc, qk_ + "r%d" % qc, "RG"], [sk_])

        for u in range(min(LOOK, len(units))):
            qk(u)
        O, Ok = None, None
        pendK = []
        for u, (h, qc, t) in enumerate(units):
            if t == 0:
                O, Ok = pb[2 + ocount % 2], "pb%d" % (2 + ocount % 2)
                ocount += 1
            if h == HB - 1 and qc == 0 and t == 0:
                for i in EARLY:
                    a, b = wpieces[i]
                    DMA("sp", RG[:, a:b], wd_img[:, a:b], reads=["wd_img%d" % i],
                        writes=["WD%d" % i] + ["cn%d" % j for j in range(a // CH, (b + CH - 1) // CH)], key="ld_wd%d" % i)
            s_, sk_ = pb[4 + (u % 4)], "pb%d" % (4 + u % 4)
            pt, ptk = btmp()
            ACT(pt[:, 0:CH], s_[:, 0:CH], AF.Exp, [sk_], [ptk], scale=SC_B)
            last_sweep = (qc == NGQ - 1) and (h + 1 < HB)
            if u + LOOK < len(units):
                h2, qc2, t2 = units[u + LOOK]
                if h2 == h or (qc == NGQ - 1 and (t2 // NBC + 1) * NBC - 1 < t):
                    qk(u + LOOK)
                else:
                    pendK.append(u + LOOK)
            MM(O[0:96, 0:CH], Vb[:, t, :], pt[:, 0:CH], t == 0, t == NT - 1, ["V%d" % (t // 8), "Vones", ptk, "RG"], [Ok])
            if qc == 0 and t == 0 and h + 1 < HB:
                for grp in range(NGQ):
                    expQ(h + 1, grp)
            if last_sweep:
                if (t + 1) % NBC == 0:
                    expK(h + 1, t // NBC)
                if (t + 1) % 8 == 0 or t == NT - 1:
                    expV(h + 1, (t // 8) * 8)
            if t == NT - 1:
                r, rk = ftmp()
                ACT(r[64:96, 0:CH], O[64:96, 0:CH], AF.Ln, [Ok], [rk + "l"])
                ACT(r[0:32, 0:CH], r[64:96, 0:CH], AF.Exp, [rk + "l"], [rk], scale=-1.0)
                ACT(r[32:64, 0:CH], r[64:96, 0:CH], AF.Exp, [rk + "l"], [rk + "b"], scale=-1.0)
                rows = slice((h % 2) * 64, (h % 2) * 64 + 64)
                TT("dve", yb3[rows, h // 2, qc * CH:(qc + 1) * CH], O[0:64, 0:CH], r[0:64, 0:CH], ALU.mult, [Ok, rk, rk + "b"],
                   ["yb%d_%d_%d" % (h // 2, qc, h % 2)])
                if qc == NGQ - 1:
                    for u2 in pendK:
                        qk(u2)
                    pendK = []

        if p == 0:
            dump("d_yb", ybT[:, :], [128, 4 * QP], BF16, ["yb%d_%d_%d" % (j_, c_, h_) for j_ in range(4) for c_ in range(NGQ) for h_ in range(2)])
        st["nb"] = 8
        fence()
        for i, (a, b) in enumerate(wpieces):
            if i not in EARLY:
                DMA("sp" if i % 2 == 0 else "pool", RG[:, a:b], wd_img[:, a:b], reads=["RG", "wd_img%d" % i], writes=["WD%d" % i], key="ld_wd%d" % i)
        K_QA, K_G, K_P, K_O = ["WD0", "RG"], ["WD1", "WD2", "RG"], ["WD3", "RG"], ["WD4", "RG"]
        for ch in range(NGQ):
            cols = slice(ch * CH, (ch + 1) * CH)
            hk = ["hTo%d" % b for b in range(ch * NBC, (ch + 1) * NBC)]
            for j in range(4):
                bk, bkk = bank()
                for k in range(8):
                    MM(bk[:, 0:CH], WQA[:, k, j * 128:(j + 1) * 128], hTo3[:, k, cols], k == 0, k == 7, hk + K_QA, [bkk])
                ACT(qaT3[:, j, 0:CH], bk[:, 0:CH], AF.Copy, [bkk, "RG"], ["qaT", "mT0", "mT1", "mT2", "mT3"])
            def win_task(b, g, ch=ch, p=p):
                gbk = ch * NBC + b
                rows = slice(g * 64, g * 64 + 64)
                sl = []
                for t in range(3):
                    kblk = gbk + t
                    s_, sk_ = bank()
                    sl.append((s_, sk_))
                    MM(s_[:, 0:512], kaT[rows, kblk * 128:(kblk + 1) * 128], qaT3[rows, :, b * 128:(b + 1) * 128], True, True,
                       ["kaT%d" % kblk, "qaT"], [sk_])
                yield
                pl = []
                for t in range(3):
                    s_, sk_ = sl[t]
                    pf, pfk = ftmp()
                    ACT(pf[:, :], s_[:, 0:512], AF.Exp, [sk_], [pfk], scale=SC_A)
                    if (gbk == 0 and t == 0) or (gbk == NBQ - 1 and t == 2):
                        fcol = 2 * p if t == 0 else 2 * p + 1
                        Ef, Ek = ftmp()
                        Efv = Ef[:, :].bitcast(BF16)[:, 0:512]
                        TS("dve", Efv, Etab4[:, g, t, :], flg[:, fcol:fcol + 1], None, ALU.mult, None, ["Etab", "flg"], [Ek])
                        Et = Efv
                    else:
                        Et, Ek = Etab4[:, g, t, :], "Etab"
                    pw, pwk = btmp()
                    pl.append((pw, pwk))
                    TT("dve", pw[:, :], pf[:, :], Et, ALU.mult, [pfk, Ek], [pwk])
                yield
                O, Ok = bank()
                for t in range(3):
                    kblk = gbk + t
                    pw, pwk = pl[t]
                    MM(O[:, 0:512], vaw4[:, kblk, g, :], pw[:, :], t == 0, t == 2, ["vaw%d" % kblk, "vaw_init", pwk], [Ok])
                dn, dnk = ftmp()
                TT("dve", dn[64:128, :], O[64:128, 0:512], esink3[64:128, g, :], ALU.add, [Ok, "esinkT"], [dnk])
                yield
                r, rk = ftmp()
                ACT(dn[64:128, :], dn[64:128, :], AF.Ln, [dnk], [dnk])
                ACT(r[0:64, :], dn[64:128, :], AF.Exp, [dnk], [rk], scale=-1.0)
                for half in range(2):
                    orow = slice(half * 64, half * 64 + 64)
                    TT("dve", yaT3[orow, 2 * g:2 * g + 2, b * 128:(b + 1) * 128],
                       O[0:64, 0:512].rearrange("p (i a c) -> p i a c", i=2, a=2)[:, :, half, :],
                       r[0:64, :].rearrange("p (i a c) -> p i a c", i=2, a=2)[:, :, half, :], ALU.mult, [Ok, rk, "RG"], ["yaT"])

            pipeline([win_task(b, g) for b in range(NBC) for g in range(2)])
            if p == 0 and ch == 0:
                dump("d_ya", QY[:, 2048:4096], [128, 2048], BF16, ["yaT"])
                dump("d_qa", QY[:, 0:2048], [128, 2048], BF16, ["qaT"])
            for br in range(2):
                for j in range(4):
                    bk, bkk = bank()
                    c0 = br * 512 + j * 128
                    for k in range(8):
                        MM(bk[:, 0:CH], WG[:, k, c0:c0 + 128], hTo3[:, k, cols], k == 0, k == 7, hk + K_G, [bkk])
                    sz, szk = ftmp()
                    ACT(sz[:, 0:CH], bk[:, 0:CH], AF.Silu, [bkk], [szk])
                    if br == 0:
                        yv, yk = yaT3[:, j, 0:CH], ["yaT"]
                    else:
                        yv, yk = yb3[:, j, cols], ["yb%d_%d_0" % (j, ch), "yb%d_%d_1" % (j, ch)]
                    TT("dve", yv, sz[:, 0:CH], yv, ALU.mult, [szk] + yk, yk + ["uT%d" % (br * 4 + j)])
            for m in range(8):
                ba, bak = bank()
                for j in range(4):
                    MM(ba[:, 0:CH], WPA[:, j, m * 128:(m + 1) * 128], yaT3[:, j, 0:CH], j == 0, j == 3, ["uT%d" % j, "yaT"] + K_P, [bak])
                bb, bbk = bank()
                for j in range(4):
                    MM(bb[:, 0:CH], WPB[:, j, m * 128:(m + 1) * 128], yb3[:, j, cols], j == 0, j == 3,
                       ["uT%d" % (4 + j), "yb%d_%d_0" % (j, ch), "yb%d_%d_1" % (j, ch)] + K_P, [bbk])
                sg = []
                for br in range(2):
                    bk, bkk = bank()
                    c0 = 1024 + br * 1024 + m * 128
                    for k in range(8):
                        MM(bk[:, 0:CH], WG[:, k, c0:c0 + 128], hTo3[:, k, cols], k == 0, k == 7, hk + K_G, [bkk])
                    s_, sk_ = ftmp()
                    ACT(s_[:, 0:CH], bk[:, 0:CH], AF.Sigmoid, [bkk, "bg"], [sk_], bias=bg[:, br * 8 + m:br * 8 + m + 1])
                    sg.append((s_, sk_))
                t1, t1k = ftmp()
                TT("dve", t1[:, 0:CH], ba[:, 0:CH], sg[0][0][:, 0:CH], ALU.mult, [bak, sg[0][1]], [t1k])
                t2, t2k = ftmp()
                TT("dve", t2[:, 0:CH], bb[:, 0:CH], sg[1][0][:, 0:CH], ALU.mult, [bbk, sg[1][1]], [t2k])
                TT("pool", mTl[m][:, 0:CH], t1[:, 0:CH], t2[:, 0:CH], ALU.add, [t1k, t2k, "RG"], ["mT%d" % m] + (["qaT"] if m < 4 else []))
            if p == 0 and ch == 0:
                dump("d_m0", QY[:, 0:2048], [128, 2048], BF16, ["mT%d" % m_ for m_ in range(4)])
                dump("d_m1", mTb[:, :], [128, 2048], BF16, ["mT%d" % m_ for m_ in range(4, 8)])
                dump("d_u", QY[:, 2048:4096], [128, 2048], BF16, ["yaT"])
            def outproj_task(tb_, ch=ch, p=p):
                nonlocal ocnt
                blk = ch * NBC + tb_
                row0 = (p * NBW + 1 + blk) * 128
                oi = ocnt % 2
                ocnt += 1
                i = st["xt"] % NXT
                st["xt"] += 1
                xtile, xk, sst, sk = xt[i], "xt%d" % i, ss[i], "ss%d" % i
                DMA("sp", xtile[:, :], xq[row0:row0 + 128, :], writes=[xk], key="ld_" + xk)
                rs, rsk = xtile, xk
                for hf2 in range(2):
                    bk, bkk = bank()
                    for k in range(8):
                        MM(bk[:, 0:512], mTl[k][:, tb_ * 128:(tb_ + 1) * 128], WO[:, k, hf2 * 512:(hf2 + 1) * 512], k == 0, k == 7,
                           ["mT%d" % k] + K_O, [bkk])
                    TT("dve", rs[:, hf2 * 512:(hf2 + 1) * 512], bk[:, 0:512], xtile[:, hf2 * 512:(hf2 + 1) * 512], ALU.add, [bkk, xk], [rsk])
                yield
                jx, jxk = xsb[oi], "xsb%d" % oi
                ACT(jx[:, :], rs[:, :], AF.Square, [rsk], [jxk, sk], accum_out=sst[:, 0:1])
                ACT(sst[:, 1:2], sst[:, 0:1], AF.Sqrt, [sk, "epsT"], [sk], scale=1.0 / D, bias=epsT[:, 0:1])
                RCP(sst[:, 2:3], sst[:, 1:2], [sk], [sk])
                yo, yok = xtile, xk
                STT("dve", yo[:, :], rs[:, :], sst[:, 2:3], fgB[:, :], [rsk, sk, "fgB"], [yok])
                orow = p * QP + blk * 128
                DMA("sp", yout[orow:orow + 128, :], yo[:, :], reads=[yok], key="st_xt%d" % i)

            optasks = [outproj_task(tb_) for tb_ in range(NBC)]
            if ch == NGQ - 1 and p + 1 < NPASS:
                st["nb"] = 7
                btasks = make_B_tasks(p + 1)
                merged = []
                for k_ in range(max(len(optasks), len(btasks))):
                    if k_ < len(optasks):
                        merged.append(optasks[k_])
                    if k_ < len(btasks):
                        merged.append(btasks[k_])
                pipeline(merged)
                B_done.add(p + 1)
                pend_cq = True
            else:
                pipeline(optasks)
                pend_cq = False
        if pend_cq:
            emit_B_cq()
    n = P.emit()
    print("ops:", n)
    return nc


def host_inputs(cfg, x_prompt, x_sample, ln_g, w_in, b_gate, sink_a, q_norm_g, kv_norm_g, w_uq, w_ukv,
                w_proj_a, w_proj_b, w_out, rel_bias, final_g):
    f = np.float32
    x_prompt = np.asarray(x_prompt, f)
    x_sample = np.asarray(x_sample, f)
    xp = np.ascontiguousarray(x_prompt[0])
    NBW = cfg.NBQ + 2
    QP = cfg.QP
    half = ROPE // 2
    inv = np.power(np.float32(10000.0), -np.arange(half, dtype=f) / half).astype(f)
    pos = np.arange(cfg.TMAX, dtype=f)
    ang = pos[None, :] * inv[:, None]
    cos2 = np.concatenate([np.cos(ang), np.cos(ang)], 0).astype(f)
    sin2 = np.concatenate([np.sin(ang), np.sin(ang)], 0).astype(f)
    ropek = np.ascontiguousarray(np.concatenate([cos2, sin2], 0))
    j = np.arange(512)
    rel = 255 - j
    bucket = t5_bucket_np(rel.astype(np.int32))
    onehot = np.zeros((NBUCK, 512), f)
    onehot[bucket[:511], np.arange(511)] = 1.0
    valid = (np.abs(rel) <= 128).astype(f)
    valid[511] = 0.0
    validv = np.ascontiguousarray(np.broadcast_to(valid[None, :], (8, 512))).astype(f)
    shared = dict(
        xp=xp, ropek=ropek, onehot=onehot, validv=validv,
        w_in=np.ascontiguousarray(np.asarray(w_in, f)[0]),
        ln_g=np.ascontiguousarray(np.asarray(ln_g, f)[0].reshape(8, 128).T),
        b_gate=np.ascontiguousarray(np.asarray(b_gate, f)[0].reshape(16, 128).T),
        sink_a=np.ascontiguousarray(np.asarray(sink_a, f)[0]),
        q_norm_g=np.ascontiguousarray(np.asarray(q_norm_g, f)[0].reshape(2, 128).T),
        kv_norm_g=np.ascontiguousarray(np.asarray(kv_norm_g, f)[0].reshape(1, 128).T),
        w_uq=np.ascontiguousarray(np.asarray(w_uq, f)[0]),
        w_ukv=np.ascontiguousarray(np.asarray(w_ukv, f)[0]),
        w_proj_a=np.ascontiguousarray(np.asarray(w_proj_a, f)[0]),
        w_proj_b=np.ascontiguousarray(np.asarray(w_proj_b, f)[0]),
        w_out=np.ascontiguousarray(np.asarray(w_out, f)[0]),
        rel_bias=np.ascontiguousarray(np.asarray(rel_bias, f)),
        final_g=np.ascontiguousarray(np.asarray(final_g, f)),
    )
    in_maps = []
    for c in range(cfg.NC):
        xs = np.ascontiguousarray(x_sample[c * cfg.NS:(c + 1) * cfg.NS].reshape(cfg.NS * cfg.DSEQ, D))
        xq = np.zeros((cfg.NPASS, NBW * 128, D), f)
        flags = np.zeros((cfg.NPASS, 2), f)
        ropeq = np.zeros((cfg.NPASS, 64, QP), f)
        for p, (seg, hf) in enumerate(cfg.passes):
            if seg == 0:
                seq, S = xp, cfg.SEQ
                start = c * cfg.OWNP + hf * QP
            else:
                seq, S = x_sample[c * cfg.NS + seg - 1], cfg.DSEQ
                start = hf * QP
            lo, hi = start - 128, start + QP + 128
            a, b = max(lo, 0), min(hi, S)
            xq[p, a - lo:b - lo] = seq[a:b]
            flags[p, 0] = 1.0 if lo >= 0 else 0.0
            flags[p, 1] = 1.0 if hi <= S else 0.0
            ropeq[p] = ropek[:, start:start + QP]
        m = dict(shared)
        m.update(xs=xs, xq=xq.reshape(cfg.NPASS * NBW * 128, D), flags=flags.reshape(-1), ropeq=ropeq)
        in_maps.append(m)
    return in_maps


def assemble(cfg, results):
    yp = np.zeros((1, cfg.SEQ, D), np.float32)
    ysm = np.zeros((cfg.DB, cfg.DSEQ, D), np.float32)
    for c in range(cfg.NC):
        y = np.asarray(results[c]["y"]).reshape(cfg.NPASS, cfg.QP, D)
        for p, (seg, hf) in enumerate(cfg.passes):
            if seg == 0:
                s0 = c * cfg.OWNP + hf * cfg.QP
                yp[0, s0:s0 + cfg.QP] = y[p]
            else:
                ysm[c * cfg.NS + seg - 1, hf * cfg.QP:(hf + 1) * cfg.QP] = y[p]
    return yp, ysm


def run(cfg, inputs, trace=False):
    nc = build_program(cfg)
    in_maps = host_inputs(cfg, **inputs)
    res = run_bass_kernel_spmd(nc, in_maps, core_ids=list(range(cfg.NC)), **({"trace": True} if trace else {}))
    return assemble(cfg, res.results), res


def kernel(**inputs):
    cfg = Cfg()
    (yp, ysm), _ = run(cfg, inputs)
    return yp, ysm
```

```python
import math
import contextlib
import numpy as np
import concourse.bass as bass
import concourse.mybir as mybir
from concourse.bass_utils import run_bass_kernel_spmd

F32 = mybir.dt.float32
BF16 = mybir.dt.bfloat16
AF = mybir.ActivationFunctionType
ALU = mybir.AluOpType

ENGS = ("pe", "act", "dve", "pool", "sp")
SYNC_SAME = ("act", "dve", "pool")

D = 1024
HA, HKV, DH = 8, 2, 64
HB, QL, KVL, NOPE, ROPE, DV = 8, 256, 128, 64, 32, 64
NBUCK, MAXD = 32, 128
EPS = 1e-6
C_QA, C_KA, C_VA, C_ZA, C_CQ, C_CKV, C_KR, C_ZB, C_GA, C_GB, C_END = 0, 512, 640, 768, 1280, 1536, 1664, 1696, 2208, 3232, 4256


class Cfg:
    def __init__(self, NC=8, SEQ=16384, DB=16, DSEQ=2048, QP=1024, CH=512):
        self.NC, self.SEQ, self.DB, self.DSEQ, self.QP, self.CH = NC, SEQ, DB, DSEQ, QP, CH
        self.OWNP = SEQ // NC
        self.NS = DB // NC
        self.segs = [(SEQ, self.OWNP)] + [(DSEQ, DSEQ)] * self.NS
        self.passes = []
        for s, (T, own) in enumerate(self.segs):
            assert own % QP == 0 and T % CH == 0 and QP % CH == 0
            for hf in range(own // QP):
                self.passes.append((s, hf))
        self.NPASS = len(self.passes)
        self.NBQ = QP // 128
        self.TK = SEQ + self.NS * DSEQ
        self.TMAX = max(SEQ, DSEQ)
        self.kbase = [0] + [SEQ + i * DSEQ for i in range(self.NS)]


class Prog:
    def __init__(self, nc):
        self.nc = nc
        self.ops = []

    def op(self, eng, fn, reads=(), writes=(), dma=None):
        self.ops.append((eng, fn, tuple(reads), tuple(writes), dma))

    def emit(self, final_wait_eng="sp"):
        nc, ops = self.nc, self.ops
        n = len(ops)
        last_w, readers = {}, {}
        deps = [None] * n
        raw = [None] * n
        for i, (eng, fn, rds, wrs, dma) in enumerate(ops):
            d = set()
            rw = set()
            for r in rds:
                j = last_w.get(r)
                if j is not None:
                    d.add(j)
                    rw.add(j)
            raw[i] = rw
            for w in wrs:
                j = last_w.get(w)
                if j is not None:
                    d.add(j)
                rl = readers.get(w)
                if rl:
                    d.update(rl)
            for r in rds:
                readers.setdefault(r, []).append(i)
            for w in wrs:
                last_w[w] = i
                readers[w] = []
            d.discard(i)
            deps[i] = d
        need_sig = [False] * n
        for i in range(n):
            e = ops[i][0]
            for j in deps[i]:
                if ops[j][4] is None and (ops[j][0] != e or e in SYNC_SAME):
                    need_sig[j] = True
        dma_keys = []
        seenk = set()
        for o in ops:
            if o[4] is not None and o[4] not in seenk:
                seenk.add(o[4])
                dma_keys.append(o[4])
        with contextlib.ExitStack() as st:
            eng_sems = {e: st.enter_context(nc.semaphore("s_" + e)) for e in ENGS}
            dma_sems = {k: st.enter_context(nc.semaphore("d_%d" % i)) for i, k in enumerate(dma_keys)}
            cnt = {e: 0 for e in ENGS}
            dcnt = {k: 0 for k in dma_keys}
            sig = [None] * n
            for i, (eng, fn, rds, wrs, dma) in enumerate(ops):
                if dma is not None:
                    dcnt[dma] += 16
                    sig[i] = (dma_sems[dma], dcnt[dma], ("d", dma))
                elif need_sig[i]:
                    cnt[eng] += 1
                    sig[i] = (eng_sems[eng], cnt[eng], ("e", eng))
            per_eng = {e: [] for e in ENGS}
            for i, o in enumerate(ops):
                per_eng[o[0]].append(i)
            block = st.enter_context(nc.Block())

            def make_body(e):
                def body(eng):
                    seen = {}
                    for i in per_eng[e]:
                        _, fn, _, _, dma = ops[i]
                        need = {}
                        for j in deps[i]:
                            s = sig[j]
                            if s is None:
                                continue
                            if ops[j][4] is None and ops[j][0] == e and e not in SYNC_SAME:
                                continue
                            sem, val, key = s
                            if seen.get(key, 0) >= val:
                                continue
                            if key not in need or need[key][1] < val:
                                need[key] = (sem, val)
                        for key, (sem, val) in need.items():
                            eng.wait_ge(sem, val)
                            seen[key] = val
                        ins = fn(eng)
                        if sig[i] is not None:
                            ins.then_inc(sig[i][0], 16 if dma is not None else 1)
                    if e == final_wait_eng:
                        for k in dma_keys:
                            if seen.get(("d", k), 0) < dcnt[k]:
                                eng.wait_ge(dma_sems[k], dcnt[k])
                        for e2 in ENGS:
                            if e2 != e and cnt[e2] > 0 and seen.get(("e", e2), 0) < cnt[e2]:
                                eng.wait_ge(eng_sems[e2], cnt[e2])
                return body

            block.tensor(make_body("pe"))
            block.scalar(make_body("act"))
            block.vector(make_body("dve"))
            block.gpsimd(make_body("pool"))
            block.sync(make_body("sp"))
        return n


def t5_bucket_np(rel):
    nb = NBUCK // 2
    max_exact = nb // 2
    ret = (rel > 0).astype(np.int32) * nb
    n = np.abs(rel)
    nf = np.maximum(n, 1).astype(np.float32)
    large = max_exact + (np.log(nf / max_exact) / math.log(MAXD / max_exact) * (nb - max_exact)).astype(np.int32)
    large = np.minimum(large, nb - 1)
    return ret + np.where(n < max_exact, n, large)


def build_program(cfg):
    nc = bass.Bass("TRN2", target_bir_lowering=False)
    P = Prog(nc)
    CH, QP, NBQ, NPASS, TK, TMAX = cfg.CH, cfg.QP, cfg.NBQ, cfg.NPASS, cfg.TK, cfg.TMAX
    NBW = NBQ + 2
    NGQ = QP // CH
    NBC = CH // 128
    TREG = max(cfg.SEQ, cfg.DSEQ)

    def din(name, shape, dt=F32):
        return nc.dram_tensor(name, list(shape), dt, kind="ExternalInput").ap()

    xp = din("xp", [cfg.SEQ, D])
    xs = din("xs", [cfg.NS * cfg.DSEQ, D])
    xq = din("xq", [NPASS * NBW * 128, D])
    flags = din("flags", [NPASS * 2])
    ropek = din("ropek", [64, TMAX])
    ropeq = din("ropeq", [NPASS, 64, QP])
    w_in = din("w_in", [D, C_END])
    ln_g = din("ln_g", [128, 8])
    b_gate = din("b_gate", [128, 16])
    sink_a = din("sink_a", [8])
    q_norm_g = din("q_norm_g", [128, 2])
    kv_norm_g = din("kv_norm_g", [128, 1])
    w_uq = din("w_uq", [QL, HB * 96])
    w_ukv = din("w_ukv", [KVL, HB * 128])
    w_proj_a = din("w_proj_a", [512, D])
    w_proj_b = din("w_proj_b", [512, D])
    w_out = din("w_out", [D, D])
    rel_bias = din("rel_bias", [NBUCK, HA])
    final_g = din("final_g", [D])
    onehot = din("onehot", [NBUCK, 512])
    validv = din("validv", [8, 512])
    yout = nc.dram_tensor("y", [NPASS * QP, D], F32, kind="ExternalOutput").ap()

    NWD = 24576 + 4096 * 3 + 8192
    wd_img = nc.dram_tensor("wd_img", [128, NWD], BF16, kind="Internal").ap()
    kvc = nc.dram_tensor("kvc", [160, TK], BF16, kind="Internal").ap()
    evec_t = nc.dram_tensor("evec", [8, 512], F32, kind="Internal")
    etoe_t = nc.dram_tensor("etoe", [8 * 128 * 512], F32, kind="Internal")

    sb = nc.alloc_sbuf_tensor
    O_CN, O_KT, O_V = 0, TREG, 2 * TREG
    NV = (TREG // 128) * 96
    O_QT = O_V + NV
    NREG = max(O_QT + 2 * QP, NWD)
    RG = sb("RG", [128, NREG], BF16)
    cnT = RG[:, O_CN:O_CN + TREG]
    KT = RG[:, O_KT:O_KT + TREG]
    Vb = RG[:, O_V:O_V + NV].rearrange("p (t c) -> p t c", c=96)
    QTs = [RG[:, O_QT + i * QP:O_QT + (i + 1) * QP] for i in range(2)]
    WQA = RG[:, 0:4096].rearrange("p (k c) -> p k c", k=8)
    WG = RG[:, 4096:28672].rearrange("p (k c) -> p k c", k=8)
    WPA = RG[:, 28672:32768].rearrange("p (k c) -> p k c", k=4)
    WPB = RG[:, 32768:36864].rearrange("p (k c) -> p k c", k=4)
    WO = RG[:, 36864:45056].rearrange("p (k c) -> p k c", k=8)

    hTo = sb("hTo", [128, 8 * QP], BF16)
    hTo3 = hTo[:, :].rearrange("p (k c) -> p k c", k=8)
    HR = sb("HR", [128, max(2048, 2 * QP)], BF16)
    hTh = [HR[:, i * 1024:(i + 1) * 1024] for i in range(2)]
    cqnT = sb("cqnT", [128, 2 * QP], BF16)
    cqn3 = cqnT[:, :].rearrange("p (k c) -> p k c", k=2)
    ybT = sb("ybT", [128, 4 * QP], BF16)
    yb3 = ybT[:, :].rearrange("p (k c) -> p k c", k=4)
    kaT = sb("kaT", [128, NBW * 128], BF16)
    vaw = sb("vaw", [128, NBW * 2 * 128], BF16)
    vaw4 = vaw[:, :].rearrange("p (b g c) -> p b g c", b=NBW, g=2)
    WA = sb("WA", [128, 8 * 192], BF16)
    WA3 = WA[:, :].rearrange("p (k c) -> p k c", k=8)
    WB = sb("WB", [128, 8 * 512], BF16)
    WB3 = WB[:, :].rearrange("p (k c) -> p k c", k=8)
    WUQ = sb("WUQ", [128, 2 * 8 * 128], BF16)
    WUQ4 = WUQ[:, :].rearrange("p (k h c) -> p k h c", k=2, h=8)
    WUKV = sb("WUKV", [128, 8 * 128], BF16)
    WUKV3 = WUKV[:, :].rearrange("p (h c) -> p h c", h=8)
    gcol = sb("gcol", [128, 8], F32)
    bg = sb("bg", [128, 16], F32)
    gq = sb("gq", [128, 2], F32)
    gkv = sb("gkv", [128, 1], F32)
    fgB = sb("fgB", [128, D], F32)
    epsT = sb("epsT", [128, 1], F32)
    ident = sb("ident", [128, 128], BF16)
    onesf = sb("onesf", [128, 128], F32)
    esk = sb("esk", [128, 8], F32)
    esinkT = sb("esinkT", [128, 2 * 512], BF16)
    esink3 = esinkT[:, :].rearrange("p (g c) -> p g c", g=2)
    Etab = sb("Etab", [128, 6 * 512], BF16)
    Etab4 = Etab[:, :].rearrange("p (g t c) -> p g t c", g=2, t=3)
    flg = sb("flg", [128, NPASS * 2], F32)
    ropq = HR[:, 0:2 * QP].bitcast(F32)
    NXT = 4
    xt = [sb("xt%d" % i, [128, D], F32) for i in range(NXT)]
    xsb = [sb("xsb%d" % i, [128, D], BF16) for i in range(2)]
    ss = [sb("ss%d" % i, [128, 4], F32) for i in range(NXT)]
    NF = 5
    f32t = [sb("f32t%d" % i, [128, 512], F32) for i in range(NF)]
    b16t = [sb("b16t%d" % i, [128, 512], BF16) for i in range(4)]
    QY = sb("QY", [128, 4096], BF16)
    qaT3 = QY[:, 0:2048].rearrange("p (j c) -> p j c", j=4)
    yaT3 = QY[:, 2048:4096].rearrange("p (j c) -> p j c", j=4)
    rpf = QY[:, 0:2048].bitcast(F32)
    rpk = [rpf[0:64, i * 512:(i + 1) * 512] for i in range(2)]
    cng = [QY[:, 2048 + i * 512:2048 + (i + 1) * 512] for i in range(2)]
    krg = [QY[0:32, 3072 + i * 512:3072 + (i + 1) * 512] for i in range(2)]
    mTb = sb("mTb", [128, 4 * 512], BF16)
    mTl = [QY[:, m * 512:(m + 1) * 512] for m in range(4)] + [mTb[:, m * 512:(m + 1) * 512] for m in range(4)]
    fence_t = sb("fence_t", [128, 2], F32)
    pb = [nc.alloc_psum_tensor("pb%d" % i, [128, 512], F32) for i in range(8)]
    print("sbuf bytes remaining:", nc.sbuf_bytes_remaining)

    st = {"bank": 0, "nb": 7, "f": 0, "b": 0, "xt": 0}
    NWARM = 2

    live_banks = set()

    def bank(hold=False):
        for _ in range(16):
            i = st["bank"] % st["nb"]
            st["bank"] += 1
            if "pb%d" % i not in live_banks:
                break
        else:
            raise RuntimeError("no free PSUM bank")
        if hold:
            live_banks.add("pb%d" % i)
        return pb[i], "pb%d" % i

    def release(k):
        live_banks.discard(k)

    def ftmp():
        i = st["f"] % NF
        st["f"] += 1
        return f32t[i], "f32t%d" % i

    def btmp():
        i = st["b"] % 4
        st["b"] += 1
        return b16t[i], "b16t%d" % i

    def I(eng, name, *args, reads=(), writes=(), dma=None, **kw):
        P.op(eng, lambda e: getattr(e, name)(*args, **kw), reads, writes, dma)

    def DMA(eng, out, in_, reads=(), writes=(), key=None):
        I(eng, "dma_start", out=out, in_=in_, reads=reads, writes=writes, dma=key)

    def MM(out, lhsT, rhs, start, stop, reads, writes):
        I("pe", "matmul", out, lhsT=lhsT, rhs=rhs, start=start, stop=stop, reads=reads, writes=writes)

    def ACT(out, in_, func, reads, writes, **kw):
        I("act", "activation", out=out, in_=in_, func=func, reads=reads, writes=writes, **kw)

    def TT(eng, out, in0, in1, op_, reads, writes):
        I(eng, "tensor_tensor", out=out, in0=in0, in1=in1, op=op_, reads=reads, writes=writes)

    def TS(eng, out, in0, s1, s2, op0, op1, reads, writes):
        if s2 is None:
            I(eng, "tensor_scalar", out=out, in0=in0, scalar1=s1, scalar2=None, op0=op0, reads=reads, writes=writes)
        else:
            I(eng, "tensor_scalar", out=out, in0=in0, scalar1=s1, scalar2=s2, op0=op0, op1=op1, reads=reads, writes=writes)

    def STT(eng, out, in0, scalar, in1, reads, writes):
        I(eng, "scalar_tensor_tensor", out=out, in0=in0, scalar=scalar, in1=in1, op0=ALU.mult, op1=ALU.mult, reads=reads, writes=writes)

    def CP(eng, out, in_, reads, writes):
        I(eng, "tensor_copy", out=out, in_=in_, reads=reads, writes=writes)

    def RCP(out, in_, reads, writes):
        I("dve", "reciprocal", out=out, in_=in_, reads=reads, writes=writes)

    I("pool", "memset", epsT[:], EPS, writes=["epsT"])
    I("pool", "memset", onesf[:], 1.0, writes=["onesf"])
    idf, idk = f32t[0], "f32t0"
    I("pool", "memset", idf[:, 0:128], 1.0, writes=[idk])
    I("pool", "affine_select", out=idf[:, 0:128], in_=idf[:, 0:128], pattern=[[-1, 128]], compare_op=ALU.is_equal,
      fill=0.0, base=0, channel_multiplier=1, reads=[idk], writes=[idk])
    CP("dve", ident[:], idf[:, 0:128], [idk], ["ident"])
    I("pool", "memset", vaw[:], 1.0, writes=["vaw_init"])
    for t_, d_, k_ in [(gcol, ln_g, "gcol"), (bg, b_gate, "bg"), (gq, q_norm_g, "gq"), (gkv, kv_norm_g, "gkv")]:
        DMA("sp", t_[:], d_[:, :], writes=[k_], key="ld_" + k_)
    DMA("sp", fgB[:], bass.AP(final_g.tensor, 0, [[0, 128], [1, D]]), writes=["fgB"], key="ld_fgB")
    DMA("sp", esk[:], bass.AP(sink_a.tensor, 0, [[0, 128], [1, 8]]), writes=["esk"], key="ld_esk")
    DMA("sp", flg[:], bass.AP(flags.tensor, 0, [[0, 128], [1, NPASS * 2]]), writes=["flg"], key="ld_flg")
    ACT(esk[:], esk[:], AF.Exp, ["esk"], ["esk"])
    for h in range(8):
        g, i = h // 4, h % 4
        CP("dve", esink3[:, g, i * 128:(i + 1) * 128], esk[:, h:h + 1].to_broadcast([128, 128]), ["esk"], ["esinkT"])

    rbt, oht, vvt, ev = f32t[1], f32t[2], f32t[3], f32t[4]
    DMA("sp", rbt[0:NBUCK, 0:8], rel_bias[:, :], writes=["f32t1"], key="ld_rb")
    DMA("sp", oht[0:NBUCK, :], onehot[:, :], writes=["f32t2"], key="ld_oh")
    DMA("sp", vvt[0:8, :], validv[:, :], writes=["f32t3"], key="ld_vv")
    MM(pb[0][0:8, :], rbt[0:NBUCK, 0:8], oht[0:NBUCK, :], True, True, ["f32t1", "f32t2"], ["pb0"])
    ACT(ev[0:8, :], pb[0][0:8, :], AF.Exp, ["pb0"], ["f32t4"])
    TT("dve", ev[0:8, :], ev[0:8, :], vvt[0:8, :], ALU.mult, ["f32t4", "f32t3"], ["f32t4"])
    DMA("sp", evec_t.ap()[:, :], ev[0:8, :], reads=["f32t4"], writes=["evec"], key="st_evec")
    DMA("sp", bass.AP(etoe_t, 0, [[128 * 512, 8], [512, 128], [1, 511]]), bass.AP(evec_t, 0, [[512, 8], [0, 128], [1, 511]]),
        reads=["evec"], writes=["etoe"], key="st_etoe")
    CT = [383, 255, 127]
    st["f"] = 0
    for g in range(2):
        for t in range(3):
            ft, fk = ftmp()
            DMA("sp", ft[:, :].rearrange("p (h c) -> p h c", h=4),
                bass.AP(etoe_t, 4 * g * 128 * 512 + CT[t], [[511, 128], [128 * 512, 4], [1, 128]]),
                reads=["etoe"], writes=[fk], key="ld_E%d%d" % (g, t))
            CP("dve", Etab4[:, g, t, :], ft[:, :], [fk], ["Etab"])

    def stage(src_ap, ncols):
        i = st["xt"] % NXT
        st["xt"] += 1
        DMA("sp", xt[i][:, 0:ncols], src_ap, writes=["xt%d" % i], key="ld_xt%d" % i)
        return xt[i], ["xt%d" % i, "gcol", "RG"]

    for k in range(8):
        rows = slice(k * 128, (k + 1) * 128)
        gk = gcol[:, k:k + 1]
        x_, rd = stage(w_in[rows, 0:768], 768)
        for hf in range(2):
            TS("dve", WQA[:, k, :].rearrange("p (j a c) -> p j a c", j=4, a=2)[:, :, hf, :],
               x_[:, hf * 256:(hf + 1) * 256].rearrange("p (j c) -> p j c", j=4), gk, None, ALU.mult, None, rd, ["WQA"])
        TS("dve", WB3[:, k, 256:512], x_[:, 512:768], gk, None, ALU.mult, None, rd, ["WB"])
        x_, rd = stage(w_in[rows, 768:1696], 928)
        TS("dve", WG[:, k, 0:512], x_[:, 0:512], gk, None, ALU.mult, None, rd, ["WG"])
        TS("dve", WB3[:, k, 0:256], x_[:, 512:768], gk, None, ALU.mult, None, rd, ["WB"])
        TS("dve", WA3[:, k, 0:160], x_[:, 768:928], gk, None, ALU.mult, None, rd, ["WA"])
        TS("dve", WA3[:, k, 176:192], x_[:, 896:912], gk, None, ALU.mult, None, rd, ["WA"])
        TS("dve", WA3[:, k, 160:176], x_[:, 912:928], gk, -1.0, ALU.mult, ALU.mult, rd, ["WA"])
        x_, rd = stage(w_in[rows, 1696:2720], 1024)
        TS("dve", WG[:, k, 512:1536], x_[:, 0:1024], gk, None, ALU.mult, None, rd, ["WG"])
        x_, rd = stage(w_in[rows, 2720:3744], 1024)
        TS("dve", WG[:, k, 1536:2560], x_[:, 0:1024], gk, None, ALU.mult, None, rd, ["WG"])
        x_, rd = stage(w_in[rows, 3744:4256], 512)
        TS("dve", WG[:, k, 2560:3072], x_[:, 0:512], gk, None, ALU.mult, None, rd, ["WG"])
    for k in range(2):
        x_, rd = stage(w_uq[k * 128:(k + 1) * 128, :], 768)
        x3 = x_[:, 0:768].rearrange("p (h c) -> p h c", h=8)
        CP("dve", WUQ4[:, k, :, 0:96], x3, rd, ["WUQ"])
        TS("dve", WUQ4[:, k, :, 96:112], x3[:, :, 80:96], -1.0, None, ALU.mult, None, rd, ["WUQ"])
        CP("dve", WUQ4[:, k, :, 112:128], x3[:, :, 64:80], rd, ["WUQ"])
    x_, rd = stage(w_ukv[:, :], 1024)
    CP("dve", WUKV[:, :], x_[:, :], rd, ["WUKV"])
    for k in range(4):
        x_, rd = stage(w_proj_a[k * 128:(k + 1) * 128, :], 1024)
        CP("dve", WPA[:, k, :], x_[:, :], rd, ["WPA"])
        x_, rd = stage(w_proj_b[k * 128:(k + 1) * 128, :], 1024)
        CP("dve", WPB[:, k, :], x_[:, :], rd, ["WPB"])
    for k in range(8):
        x_, rd = stage(w_out[k * 128:(k + 1) * 128, :], 1024)
        CP("dve", WO[:, k, :], x_[:, :], rd, ["WO"])
    wpieces = [(0, 4096), (4096, 16384), (16384, 28672), (28672, 36864), (36864, NWD)]
    NWP = len(wpieces)
    EARLY = [i for i in (0, 1) if wpieces[i][1] <= TREG]
    for i, (a, b) in enumerate(wpieces):
        DMA("sp", wd_img[:, a:b], RG[:, a:b], reads=["WG", "WQA", "WPA", "WPB", "WO", "RG"], writes=["wd_img%d" % i], key="st_wd%d" % i)

    dbg = getattr(cfg, "debug", False)

    def dump(name, ap, shape, dt, reads):
        if not dbg:
            return
        o = nc.dram_tensor(name, list(shape), dt, kind="ExternalOutput").ap()
        DMA("sp", o, ap, reads=reads, key="dbg_" + name)

    def fence():
        I("pool", "memset", fence_t[:], 0.0, writes=["RG", "fence_t"])

    def pipeline(tasks):
        active = []
        tasks = list(tasks)
        while tasks or active:
            if tasks:
                active.append(tasks.pop(0))
            nxt = []
            for g_ in active:
                try:
                    next(g_)
                    nxt.append(g_)
                except StopIteration:
                    pass
            active = nxt

    def load_norm_transpose(src_ap, dst_view, dst_keys, extra_reads=(), tail=None):
        i = st["xt"] % NXT
        st["xt"] += 1
        xtile, xk, sst, sk = xt[i], "xt%d" % i, ss[i], "ss%d" % i
        j = st["xt"] % 2
        xb, xbk = xsb[j], "xsb%d" % j
        DMA("sp", xtile[:, :], src_ap, writes=[xk], key="ld_" + xk)
        ACT(xb[:, :], xtile[:, :], AF.Square, [xk], [xbk, sk], accum_out=sst[:, 0:1])
        ACT(sst[:, 1:2], sst[:, 0:1], AF.Ln, [sk, "epsT"], [sk], scale=1.0 / D, bias=epsT[:, 0:1])
        ACT(sst[:, 2:3], sst[:, 1:2], AF.Exp, [sk], [sk], scale=-0.5)
        TS("dve", xb[:, :], xtile[:, :], sst[:, 2:3], None, ALU.mult, None, [xk, sk], [xbk])
        yield
        bk, bkk = bank(hold=True)
        bkb = bk[:, :].bitcast(BF16)
        for k in range(8):
            I("pe", "transpose", out=bkb[:, k * 128:(k + 1) * 128], in_=xb[:, k * 128:(k + 1) * 128], identity=ident[:],
              reads=[xbk, "ident"], writes=[bkk])
        for _ in range(NWARM):
            MM(pb[7][:, 0:512], ident[:], WB[:, 0:512], True, True, ["ident", "WB"], ["pb7"])
        yield
        CP("dve", dst_view, bkb[:, :].rearrange("p (k c) -> p k c", k=8), [bkk] + list(extra_reads), dst_keys)
        release(bkk)
        if tail is not None:
            yield
            r_ = tail()
            if r_ is not None:
                yield from r_

    if QP >= 2 * CH:
        hA = [hTo[:, i * 8 * CH:(i + 1) * 8 * CH].rearrange("p (k c) -> p k c", k=8) for i in range(2)]
    else:
        hAt = [sb("hA%d" % i, [128, 8 * CH], BF16) for i in range(2)]
        hA = [t_[:, :].rearrange("p (k c) -> p k c", k=8) for t_ in hAt]
    _ob = onesf[:, :].bitcast(BF16)
    onesb = bass.AP(_ob.tensor, _ob.offset + 1, [[_ob.ap[0][0], 128], [2, 128]])

    def phaseA_group_tail(grp, slot, tok0, pos0, hks):
        b1, b1k = bank(hold=True)
        for k in range(8):
            MM(b1[:, 0:CH], WA3[:, k, 0:128], hA[slot][:, k, :], k == 0, k == 7, hks + ["WA", "RG"], [b1k])
        b2, b2k = bank(hold=True)
        for k in range(8):
            MM(b2[0:64, 0:CH], WA3[:, k, 128:192], hA[slot][:, k, :], k == 0, k == 7, hks + ["WA", "RG"], [b2k])
        rp, rpkk = rpk[slot], "rpk%d" % slot
        DMA("sp", rp[:, 0:CH], ropek[:, pos0:pos0 + CH], reads=["RG"], writes=[rpkk], key="ld_" + rpkk)
        yield
        sq, sqk = btmp()
        ACT(sq[:, 0:CH], b1[:, 0:CH], AF.Square, [b1k], [sqk])
        b3, b3k = bank(hold=True)
        MM(b3[:, 0:CH], onesb, sq[:, 0:CH], True, True, [sqk, "onesf"], [b3k])
        ta, tak = ftmp()
        tb, tbk = ftmp()
        TT("dve", ta[0:32, 0:CH], b2[0:32, 0:CH], rp[0:32, 0:CH], ALU.mult, [b2k, rpkk, "RG"], [tak])
        TT("dve", tb[0:32, 0:CH], b2[32:64, 0:CH], rp[32:64, 0:CH], ALU.mult, [b2k, rpkk, "RG"], [tbk])
        release(b2k)
        yield
        rc, rck = ftmp()
        ACT(rc[:, 0:CH], b3[:, 0:CH], AF.Ln, [b3k, "epsT"], [rck], scale=1.0 / KVL, bias=epsT[:, 0:1])
        release(b3k)
        ACT(rc[:, 0:CH], rc[:, 0:CH], AF.Exp, [rck], [rck], scale=-0.5)
        kg, kgk = krg[slot], "krg%d" % slot
        TT("dve", kg[:, 0:CH], ta[0:32, 0:CH], tb[0:32, 0:CH], ALU.add, [tak, tbk, "RG"], [kgk])
        DMA("pool", kvc[128:160, tok0:tok0 + CH], kg[:, 0:CH], reads=[kgk, "RG"], writes=["kvck%d" % grp], key="st_" + kgk)
        yield
        cg, cgk = cng[slot], "cng%d" % slot
        STT("dve", cg[:, 0:CH], b1[:, 0:CH], gkv[:, 0:1], rc[:, 0:CH], [b1k, rck, "gkv", "RG"], [cgk])
        release(b1k)
        DMA("pool", kvc[0:128, tok0:tok0 + CH], cg[:, 0:CH], reads=[cgk, "RG"], writes=["kvcc%d" % grp], key="st_" + cgk)

    tasksA = []
    for grp in range(TK // CH):
        slot = grp % 2
        tok0 = grp * CH
        if tok0 < cfg.SEQ:
            src, r0, pos0 = xp, tok0, tok0
        else:
            src, r0 = xs, tok0 - cfg.SEQ
            pos0 = r0 % cfg.DSEQ
        hks = ["hA%d_%d" % (slot, tl) for tl in range(NBC)]
        for tl in range(NBC):
            tail = None
            if tl == NBC - 1:
                tail = (lambda grp=grp, slot=slot, tok0=tok0, pos0=pos0, hks=hks: phaseA_group_tail(grp, slot, tok0, pos0, hks))
            tasksA.append(load_norm_transpose(src[r0 + tl * 128: r0 + (tl + 1) * 128, :], hA[slot][:, :, tl * 128:(tl + 1) * 128],
                                              [hks[tl]], ["RG"], tail))
    pipeline(tasksA)

    dump("d_kvc", kvc[:, :], [160, TK], BF16, ["kvcc%d" % g_ for g_ in range(TK // CH)] + ["kvck%d" % g_ for g_ in range(TK // CH)])
    dump("d_E", Etab[:, :], [128, 6 * 512], BF16, ["Etab"])
    SC_A = DH ** -0.5
    SC_B = (NOPE + ROPE) ** -0.5
    ocnt = 0
    B_done = set()
    for p, (seg, hf) in enumerate(cfg.passes):
        T, own = cfg.segs[seg]
        kb = cfg.kbase[seg]
        NJ = T // CH
        NT = T // 128
        st["nb"] = 7
        fence()
        NLD = min(4, NJ)
        for i in range(NLD):
            j0, j1 = i * NJ // NLD, (i + 1) * NJ // NLD
            a, b = j0 * CH, j1 * CH
            kvr = ["kvcc%d" % ((kb + jj * CH) // CH) for jj in range(j0, j1)] + ["kvck%d" % ((kb + jj * CH) // CH) for jj in range(j0, j1)]
            DMA("pool", cnT[:, a:b], kvc[0:128, kb + a:kb + b], reads=["RG"] + kvr, writes=["cn%d" % j for j in range(j0, j1)], key="ld_cn%d" % i)
            DMA("pool", KT[64:96, a:b], kvc[128:160, kb + a:kb + b], reads=["RG"] + kvr, writes=["KTr%d" % j for j in range(j0, j1)], key="ld_kr%d" % i)
        I("pool", "memset", Vb[:, 0:NT, 64:96], 1.0, reads=["RG"], writes=["Vones"])
        def phaseB_tail(blk, hv, hkeys):
            bka, bkak = bank(hold=True)
            for k in range(8):
                MM(bka[:, 0:128], WB3[:, k, 256:384], hv[:, k, :], k == 0, k == 7, hkeys + ["WB"], [bkak])
            bkv, bkvk = bank(hold=True)
            for k in range(8):
                MM(bkv[:, 0:128], hv[:, k, :], WB3[:, k, 384:512], k == 0, k == 7, hkeys + ["WB"], [bkvk])
            yield
            CP("dve", kaT[:, blk * 128:(blk + 1) * 128], bka[:, 0:128], [bkak], ["kaT%d" % blk])
            CP("dve", vaw4[:, blk, :, 0:64], bkv[:, 0:128].rearrange("p (g c) -> p g c", g=2), [bkvk, "vaw_init"], ["vaw%d" % blk])
            release(bkak)
            release(bkvk)

        def make_B_tasks(pp):
          tasksB = []
          for blk in range(NBW):
            row0 = (pp * NBW + blk) * 128
            if blk == 0 or blk == NBW - 1:
                hi = 0 if blk == 0 else 1
                hv = hTh[hi].rearrange("p (k c) -> p k c", k=8)
                hkeys = ["hTh%d" % hi, "ropq"]
            else:
                hv = hTo3[:, :, (blk - 1) * 128: blk * 128]
                hkeys = ["hTo%d" % (blk - 1)]
            tasksB.append(load_norm_transpose(xq[row0:row0 + 128, :], hv, hkeys, ["RG"] if pp == 0 else [],
                                              (lambda blk=blk, hv=hv, hkeys=hkeys: phaseB_tail(blk, hv, hkeys))))
          return tasksB

        def emit_B_cq():
          for grp in range(NGQ):
              cols = slice(grp * CH, (grp + 1) * CH)
              hk = ["hTo%d" % b for b in range(grp * NBC, (grp + 1) * NBC)]
              bq = []
              for m in range(2):
                  bk, bkk = bank()
                  bq.append((bk, bkk))
                  for k in range(8):
                      MM(bk[:, 0:CH], WB3[:, k, m * 128:(m + 1) * 128], hTo3[:, k, cols], k == 0, k == 7, hk + ["WB"], [bkk])
              b3, b3k = bank()
              for m in range(2):
                  sq, sqk = btmp()
                  ACT(sq[:, 0:CH], bq[m][0][:, 0:CH], AF.Square, [bq[m][1]], [sqk])
                  MM(b3[:, 0:CH], onesb, sq[:, 0:CH], m == 0, m == 1, [sqk, "onesf"], [b3k])
              rc, rck = ftmp()
              ACT(rc[:, 0:CH], b3[:, 0:CH], AF.Ln, [b3k, "epsT"], [rck], scale=1.0 / QL, bias=epsT[:, 0:1])
              ACT(rc[:, 0:CH], rc[:, 0:CH], AF.Exp, [rck], [rck], scale=-0.5)
              for m in range(2):
                  STT("dve", cqn3[:, m, cols], bq[m][0][:, 0:CH], gq[:, m:m + 1], rc[:, 0:CH], [bq[m][1], rck, "gq"], ["cqn%d_%d" % (grp, m)])

        if p not in B_done:
            pipeline(make_B_tasks(p))
            emit_B_cq()
            B_done.add(p)

        if p == 0:
            dump("d_hTo", hTo[:, :], [128, 8 * QP], BF16, ["hTo%d" % b_ for b_ in range(NBQ)])
            dump("d_cqn", cqnT[:, :], [128, 2 * QP], BF16, ["cqn%d_%d" % (g_, m_) for g_ in range(NGQ) for m_ in range(2)])
            dump("d_ka", kaT[:, :], [128, NBW * 128], BF16, ["kaT%d" % b_ for b_ in range(NBW)])
            dump("d_va", vaw[:, :], [128, NBW * 2 * 128], BF16, ["vaw%d" % b_ for b_ in range(NBW)])
        DMA("sp", ropq[64:128, :], ropeq[p, :, :], writes=["ropq", "hTh0", "hTh1"], key="ld_ropq")
        st["nb"] = 2
        st["bank"] = 0
        ocount = 0
        def expK(h, j):
            bk, bkk = bank()
            MM(bk[0:64, 0:CH], WUKV3[:, h, 0:64], cnT[:, j * CH:(j + 1) * CH], True, True, ["cn%d" % j, "WUKV", "RG"], [bkk])
            CP("dve", KT[0:64, j * CH:(j + 1) * CH], bk[0:64, 0:CH], [bkk, "RG"], ["KTn%d" % j])

        def expV(h, t0):
            nt = min(8, NT - t0)
            bk, bkk = bank()
            for i in range(nt):
                MM(bk[:, i * 64:(i + 1) * 64], cnT[:, (t0 + i) * 128:(t0 + i + 1) * 128], WUKV3[:, h, 64:128], True, True,
                   ["cn%d" % ((t0 + i) * 128 // CH), "WUKV", "RG"], [bkk])
            CP("dve", Vb[:, t0:t0 + nt, 0:64], bk[:, 0:nt * 64].rearrange("p (t c) -> p t c", c=64), [bkk, "RG", "Vones"], ["V%d" % (t0 // 8)])

        def expQ(h, grp):
            QT = QTs[h % 2]
            qk_ = "QT%d" % (h % 2)
            cols = slice(grp * CH, (grp + 1) * CH)
            bk, bkk = bank()
            for k in range(2):
                MM(bk[:, 0:CH], WUQ4[:, k, h, :], cqn3[:, k, cols], k == 0, k == 1, ["cqn%d_%d" % (grp, k), "WUQ"], [bkk])
            CP("dve", QT[0:64, cols], bk[0:64, 0:CH], [bkk, "RG"], [qk_ + "n%d" % grp])
            ta, tak = ftmp()
            tb, tbk = ftmp()
            TT("dve", ta[64:96, 0:CH], bk[64:96, 0:CH], ropq[64:96, cols], ALU.mult, [bkk, "ropq"], [tak])
            TT("dve", tb[64:96, 0:CH], bk[96:128, 0:CH], ropq[96:128, cols], ALU.mult, [bkk, "ropq"], [tbk])
            TT("dve", QT[64:96, cols], ta[64:96, 0:CH], tb[64:96, 0:CH], ALU.add, [tak, tbk, "RG"], [qk_ + "r%d" % grp])

        for grp in range(NGQ):
            expQ(0, grp)
        for j in range(NJ):
            expK(0, j)
        for t0 in range(0, NT, 8):
            expV(0, t0)
        units = [(h, qc, t) for h in range(HB) for qc in range(NGQ) for t in range(NT)]
        LOOK = min(3, NT * NGQ)

        def qk(u):
            h, qc, t = units[u]
            QT = QTs[h % 2]
            qk_ = "QT%d" % (h % 2)
            s_, sk_ = pb[4 + (u % 4)], "pb%d" % (4 + u % 4)
            j = t * 128 // CH
            MM(s_[:, 0:CH], KT[0:96, t * 128:(t + 1) * 128], QT[0:96, qc * CH:(qc + 1) * CH], True, True,
               ["KTn%d" % j, "KTr%d" % j, qk_ + "n%d" % qc, qk_ + "r%d" % qc, "RG"], [sk_])

        for u in range(min(LOOK, len(units))):
            qk(u)
        O, Ok = None, None
        pendK = []
        for u, (h, qc, t) in enumerate(units):
            if t == 0:
                O, Ok = pb[2 + ocount % 2], "pb%d" % (2 + ocount % 2)
                ocount += 1
            if h == HB - 1 and qc == 0 and t == 0:
                for i in EARLY:
                    a, b = wpieces[i]
                    DMA("sp", RG[:, a:b], wd_img[:, a:b], reads=["wd_img%d" % i],
                        writes=["WD%d" % i] + ["cn%d" % j for j in range(a // CH, (b + CH - 1) // CH)], key="ld_wd%d" % i)
            s_, sk_ = pb[4 + (u % 4)], "pb%d" % (4 + u % 4)
            pt, ptk = btmp()
            ACT(pt[:, 0:CH], s_[:, 0:CH], AF.Exp, [sk_], [ptk], scale=SC_B)
            last_sweep = (qc == NGQ - 1) and (h + 1 < HB)
            if u + LOOK < len(units):
                h2, qc2, t2 = units[u + LOOK]
                if h2 == h or (qc == NGQ - 1 and (t2 // NBC + 1) * NBC - 1 < t):
                    qk(u + LOOK)
                else:
                    pendK.append(u + LOOK)
            MM(O[0:96, 0:CH], Vb[:, t, :], pt[:, 0:CH], t == 0, t == NT - 1, ["V%d" % (t // 8), "Vones", ptk, "RG"], [Ok])
            if qc == 0 and t == 0 and h + 1 < HB:
                for grp in range(NGQ):
                    expQ(h + 1, grp)
            if last_sweep:
                if (t + 1) % NBC == 0:
                    expK(h + 1, t // NBC)
                if (t + 1) % 8 == 0 or t == NT - 1:
                    expV(h + 1, (t // 8) * 8)
            if t == NT - 1:
                r, rk = ftmp()
                ACT(r[64:96, 0:CH], O[64:96, 0:CH], AF.Ln, [Ok], [rk + "l"])
                ACT(r[0:32, 0:CH], r[64:96, 0:CH], AF.Exp, [rk + "l"], [rk], scale=-1.0)
                ACT(r[32:64, 0:CH], r[64:96, 0:CH], AF.Exp, [rk + "l"], [rk + "b"], scale=-1.0)
                rows = slice((h % 2) * 64, (h % 2) * 64 + 64)
                TT("dve", yb3[rows, h // 2, qc * CH:(qc + 1) * CH], O[0:64, 0:CH], r[0:64, 0:CH], ALU.mult, [Ok, rk, rk + "b"],
                   ["yb%d_%d_%d" % (h // 2, qc, h % 2)])
                if qc == NGQ - 1:
                    for u2 in pendK:
                        qk(u2)
                    pendK = []

        if p == 0:
            dump("d_yb", ybT[:, :], [128, 4 * QP], BF16, ["yb%d_%d_%d" % (j_, c_, h_) for j_ in range(4) for c_ in range(NGQ) for h_ in range(2)])
        st["nb"] = 8
        fence()
        for i, (a, b) in enumerate(wpieces):
            if i not in EARLY:
                DMA("sp" if i % 2 == 0 else "pool", RG[:, a:b], wd_img[:, a:b], reads=["RG", "wd_img%d" % i], writes=["WD%d" % i], key="ld_wd%d" % i)
        K_QA, K_G, K_P, K_O = ["WD0", "RG"], ["WD1", "WD2", "RG"], ["WD3", "RG"], ["WD4", "RG"]
        for ch in range(NGQ):
            cols = slice(ch * CH, (ch + 1) * CH)
            hk = ["hTo%d" % b for b in range(ch * NBC, (ch + 1) * NBC)]
            for j in range(4):
                bk, bkk = bank()
                for k in range(8):
                    MM(bk[:, 0:CH], WQA[:, k, j * 128:(j + 1) * 128], hTo3[:, k, cols], k == 0, k == 7, hk + K_QA, [bkk])
                ACT(qaT3[:, j, 0:CH], bk[:, 0:CH], AF.Copy, [bkk, "RG"], ["qaT", "mT0", "mT1", "mT2", "mT3"])
            def win_task(b, g, ch=ch, p=p):
                gbk = ch * NBC + b
                rows = slice(g * 64, g * 64 + 64)
                sl = []
                for t in range(3):
                    kblk = gbk + t
                    s_, sk_ = bank()
                    sl.append((s_, sk_))
                    MM(s_[:, 0:512], kaT[rows, kblk * 128:(kblk + 1) * 128], qaT3[rows, :, b * 128:(b + 1) * 128], True, True,
                       ["kaT%d" % kblk, "qaT"], [sk_])
                yield
                pl = []
                for t in range(3):
                    s_, sk_ = sl[t]
                    pf, pfk = ftmp()
                    ACT(pf[:, :], s_[:, 0:512], AF.Exp, [sk_], [pfk], scale=SC_A)
                    if (gbk == 0 and t == 0) or (gbk == NBQ - 1 and t == 2):
                        fcol = 2 * p if t == 0 else 2 * p + 1
                        Ef, Ek = ftmp()
                        Efv = Ef[:, :].bitcast(BF16)[:, 0:512]
                        TS("dve", Efv, Etab4[:, g, t, :], flg[:, fcol:fcol + 1], None, ALU.mult, None, ["Etab", "flg"], [Ek])
                        Et = Efv
                    else:
                        Et, Ek = Etab4[:, g, t, :], "Etab"
                    pw, pwk = btmp()
                    pl.append((pw, pwk))
                    TT("dve", pw[:, :], pf[:, :], Et, ALU.mult, [pfk, Ek], [pwk])
                yield
                O, Ok = bank()
                for t in range(3):
                    kblk = gbk + t
                    pw, pwk = pl[t]
                    MM(O[:, 0:512], vaw4[:, kblk, g, :], pw[:, :], t == 0, t == 2, ["vaw%d" % kblk, "vaw_init", pwk], [Ok])
                dn, dnk = ftmp()
                TT("dve", dn[64:128, :], O[64:128, 0:512], esink3[64:128, g, :], ALU.add, [Ok, "esinkT"], [dnk])
                yield
                r, rk = ftmp()
                ACT(dn[64:128, :], dn[64:128, :], AF.Ln, [dnk], [dnk])
                ACT(r[0:64, :], dn[64:128, :], AF.Exp, [dnk], [rk], scale=-1.0)
                for half in range(2):
                    orow = slice(half * 64, half * 64 + 64)
                    TT("dve", yaT3[orow, 2 * g:2 * g + 2, b * 128:(b + 1) * 128],
                       O[0:64, 0:512].rearrange("p (i a c) -> p i a c", i=2, a=2)[:, :, half, :],
                       r[0:64, :].rearrange("p (i a c) -> p i a c", i=2, a=2)[:, :, half, :], ALU.mult, [Ok, rk, "RG"], ["yaT"])

            pipeline([win_task(b, g) for b in range(NBC) for g in range(2)])
            if p == 0 and ch == 0:
                dump("d_ya", QY[:, 2048:4096], [128, 2048], BF16, ["yaT"])
                dump("d_qa", QY[:, 0:2048], [128, 2048], BF16, ["qaT"])
            for br in range(2):
                for j in range(4):
                    bk, bkk = bank()
                    c0 = br * 512 + j * 128
                    for k in range(8):
                        MM(bk[:, 0:CH], WG[:, k, c0:c0 + 128], hTo3[:, k, cols], k == 0, k == 7, hk + K_G, [bkk])
                    sz, szk = ftmp()
                    ACT(sz[:, 0:CH], bk[:, 0:CH], AF.Silu, [bkk], [szk])
                    if br == 0:
                        yv, yk = yaT3[:, j, 0:CH], ["yaT"]
                    else:
                        yv, yk = yb3[:, j, cols], ["yb%d_%d_0" % (j, ch), "yb%d_%d_1" % (j, ch)]
                    TT("dve", yv, sz[:, 0:CH], yv, ALU.mult, [szk] + yk, yk + ["uT%d" % (br * 4 + j)])
            for m in range(8):
                ba, bak = bank()
                for j in range(4):
                    MM(ba[:, 0:CH], WPA[:, j, m * 128:(m + 1) * 128], yaT3[:, j, 0:CH], j == 0, j == 3, ["uT%d" % j, "yaT"] + K_P, [bak])
                bb, bbk = bank()
                for j in range(4):
                    MM(bb[:, 0:CH], WPB[:, j, m * 128:(m + 1) * 128], yb3[:, j, cols], j == 0, j == 3,
                       ["uT%d" % (4 + j), "yb%d_%d_0" % (j, ch), "yb%d_%d_1" % (j, ch)] + K_P, [bbk])
                sg = []
                for br in range(2):
                    bk, bkk = bank()
                    c0 = 1024 + br * 1024 + m * 128
                    for k in range(8):
                        MM(bk[:, 0:CH], WG[:, k, c0:c0 + 128], hTo3[:, k, cols], k == 0, k == 7, hk + K_G, [bkk])
                    s_, sk_ = ftmp()
                    ACT(s_[:, 0:CH], bk[:, 0:CH], AF.Sigmoid, [bkk, "bg"], [sk_], bias=bg[:, br * 8 + m:br * 8 + m + 1])
                    sg.append((s_, sk_))
                t1, t1k = ftmp()
                TT("dve", t1[:, 0:CH], ba[:, 0:CH], sg[0][0][:, 0:CH], ALU.mult, [bak, sg[0][1]], [t1k])
                t2, t2k = ftmp()
                TT("dve", t2[:, 0:CH], bb[:, 0:CH], sg[1][0][:, 0:CH], ALU.mult, [bbk, sg[1][1]], [t2k])
                TT("pool", mTl[m][:, 0:CH], t1[:, 0:CH], t2[:, 0:CH], ALU.add, [t1k, t2k, "RG"], ["mT%d" % m] + (["qaT"] if m < 4 else []))
            if p == 0 and ch == 0:
                dump("d_m0", QY[:, 0:2048], [128, 2048], BF16, ["mT%d" % m_ for m_ in range(4)])
                dump("d_m1", mTb[:, :], [128, 2048], BF16, ["mT%d" % m_ for m_ in range(4, 8)])
                dump("d_u", QY[:, 2048:4096], [128, 2048], BF16, ["yaT"])
            def outproj_task(tb_, ch=ch, p=p):
                nonlocal ocnt
                blk = ch * NBC + tb_
                row0 = (p * NBW + 1 + blk) * 128
                oi = ocnt % 2
                ocnt += 1
                i = st["xt"] % NXT
                st["xt"] += 1
                xtile, xk, sst, sk = xt[i], "xt%d" % i, ss[i], "ss%d" % i
                DMA("sp", xtile[:, :], xq[row0:row0 + 128, :], writes=[xk], key="ld_" + xk)
                rs, rsk = xtile, xk
                for hf2 in range(2):
                    bk, bkk = bank()
                    for k in range(8):
                        MM(bk[:, 0:512], mTl[k][:, tb_ * 128:(tb_ + 1) * 128], WO[:, k, hf2 * 512:(hf2 + 1) * 512], k == 0, k == 7,
                           ["mT%d" % k] + K_O, [bkk])
                    TT("dve", rs[:, hf2 * 512:(hf2 + 1) * 512], bk[:, 0:512], xtile[:, hf2 * 512:(hf2 + 1) * 512], ALU.add, [bkk, xk], [rsk])
                yield
                jx, jxk = xsb[oi], "xsb%d" % oi
                ACT(jx[:, :], rs[:, :], AF.Square, [rsk], [jxk, sk], accum_out=sst[:, 0:1])
                ACT(sst[:, 1:2], sst[:, 0:1], AF.Ln, [sk, "epsT"], [sk], scale=1.0 / D, bias=epsT[:, 0:1])
                ACT(sst[:, 2:3], sst[:, 1:2], AF.Exp, [sk], [sk], scale=-0.5)
                yo, yok = xtile, xk
                STT("dve", yo[:, :], rs[:, :], sst[:, 2:3], fgB[:, :], [rsk, sk, "fgB"], [yok])
                orow = p * QP + blk * 128
                DMA("pool", yout[orow:orow + 128, :], yo[:, :], reads=[yok], key="st_xt%d" % i)

            optasks = [outproj_task(tb_) for tb_ in range(NBC)]
            if ch == NGQ - 1 and p + 1 < NPASS:
                st["nb"] = 7
                btasks = make_B_tasks(p + 1)
                merged = []
                for k_ in range(max(len(optasks), len(btasks))):
                    if k_ < len(optasks):
                        merged.append(optasks[k_])
                    if k_ < len(btasks):
                        merged.append(btasks[k_])
                pipeline(merged)
                B_done.add(p + 1)
                pend_cq = True
            else:
                pipeline(optasks)
                pend_cq = False
        if pend_cq:
            emit_B_cq()
    n = P.emit()
    print("ops:", n)
    return nc


def host_inputs(cfg, x_prompt, x_sample, ln_g, w_in, b_gate, sink_a, q_norm_g, kv_norm_g, w_uq, w_ukv,
                w_proj_a, w_proj_b, w_out, rel_bias, final_g):
    f = np.float32
    x_prompt = np.asarray(x_prompt, f)
    x_sample = np.asarray(x_sample, f)
    xp = np.ascontiguousarray(x_prompt[0])
    NBW = cfg.NBQ + 2
    QP = cfg.QP
    half = ROPE // 2
    inv = np.power(np.float32(10000.0), -np.arange(half, dtype=f) / half).astype(f)
    pos = np.arange(cfg.TMAX, dtype=f)
    ang = pos[None, :] * inv[:, None]
    cos2 = np.concatenate([np.cos(ang), np.cos(ang)], 0).astype(f)
    sin2 = np.concatenate([np.sin(ang), np.sin(ang)], 0).astype(f)
    ropek = np.ascontiguousarray(np.concatenate([cos2, sin2], 0))
    j = np.arange(512)
    rel = 255 - j
    bucket = t5_bucket_np(rel.astype(np.int32))
    onehot = np.zeros((NBUCK, 512), f)
    onehot[bucket[:511], np.arange(511)] = 1.0
    valid = (np.abs(rel) <= 128).astype(f)
    valid[511] = 0.0
    validv = np.ascontiguousarray(np.broadcast_to(valid[None, :], (8, 512))).astype(f)
    shared = dict(
        xp=xp, ropek=ropek, onehot=onehot, validv=validv,
        w_in=np.ascontiguousarray(np.asarray(w_in, f)[0]),
        ln_g=np.ascontiguousarray(np.asarray(ln_g, f)[0].reshape(8, 128).T),
        b_gate=np.ascontiguousarray(np.asarray(b_gate, f)[0].reshape(16, 128).T),
        sink_a=np.ascontiguousarray(np.asarray(sink_a, f)[0]),
        q_norm_g=np.ascontiguousarray(np.asarray(q_norm_g, f)[0].reshape(2, 128).T),
        kv_norm_g=np.ascontiguousarray(np.asarray(kv_norm_g, f)[0].reshape(1, 128).T),
        w_uq=np.ascontiguousarray(np.asarray(w_uq, f)[0]),
        w_ukv=np.ascontiguousarray(np.asarray(w_ukv, f)[0]),
        w_proj_a=np.ascontiguousarray(np.asarray(w_proj_a, f)[0]),
        w_proj_b=np.ascontiguousarray(np.asarray(w_proj_b, f)[0]),
        w_out=np.ascontiguousarray(np.asarray(w_out, f)[0]),
        rel_bias=np.ascontiguousarray(np.asarray(rel_bias, f)),
        final_g=np.ascontiguousarray(np.asarray(final_g, f)),
    )
    in_maps = []
    for c in range(cfg.NC):
        xs = np.ascontiguousarray(x_sample[c * cfg.NS:(c + 1) * cfg.NS].reshape(cfg.NS * cfg.DSEQ, D))
        xq = np.zeros((cfg.NPASS, NBW * 128, D), f)
        flags = np.zeros((cfg.NPASS, 2), f)
        ropeq = np.zeros((cfg.NPASS, 64, QP), f)
        for p, (seg, hf) in enumerate(cfg.passes):
            if seg == 0:
                seq, S = xp, cfg.SEQ
                start = c * cfg.OWNP + hf * QP
            else:
                seq, S = x_sample[c * cfg.NS + seg - 1], cfg.DSEQ
                start = hf * QP
            lo, hi = start - 128, start + QP + 128
            a, b = max(lo, 0), min(hi, S)
            xq[p, a - lo:b - lo] = seq[a:b]
            flags[p, 0] = 1.0 if lo >= 0 else 0.0
            flags[p, 1] = 1.0 if hi <= S else 0.0
            ropeq[p] = ropek[:, start:start + QP]
        m = dict(shared)
        m.update(xs=xs, xq=xq.reshape(cfg.NPASS * NBW * 128, D), flags=flags.reshape(-1), ropeq=ropeq)
        in_maps.append(m)
    return in_maps


def assemble(cfg, results):
    yp = np.zeros((1, cfg.SEQ, D), np.float32)
    ysm = np.zeros((cfg.DB, cfg.DSEQ, D), np.float32)
    for c in range(cfg.NC):
        y = np.asarray(results[c]["y"]).reshape(cfg.NPASS, cfg.QP, D)
        for p, (seg, hf) in enumerate(cfg.passes):
            if seg == 0:
                s0 = c * cfg.OWNP + hf * cfg.QP
                yp[0, s0:s0 + cfg.QP] = y[p]
            else:
                ysm[c * cfg.NS + seg - 1, hf * cfg.QP:(hf + 1) * cfg.QP] = y[p]
    return yp, ysm


def run(cfg, inputs, trace=False):
    nc = build_program(cfg)
    in_maps = host_inputs(cfg, **inputs)
    res = run_bass_kernel_spmd(nc, in_maps, core_ids=list(range(cfg.NC)), **({"trace": True} if trace else {}))
    return assemble(cfg, res.results), res


def kernel(**inputs):
    cfg = Cfg()
    (yp, ysm), _ = run(cfg, inputs)
    return yp, ysm
```

```python
import math
import contextlib
import numpy as np
import concourse.bass as bass
import concourse.mybir as mybir
from concourse.bass_utils import run_bass_kernel_spmd

F32 = mybir.dt.float32
BF16 = mybir.dt.bfloat16
AF = mybir.ActivationFunctionType
ALU = mybir.AluOpType

ENGS = ("pe", "act", "dve", "pool", "sp")
SYNC_SAME = ("act", "dve", "pool")

D = 1024
HA, HKV, DH = 8, 2, 64
HB, QL, KVL, NOPE, ROPE, DV = 8, 256, 128, 64, 32, 64
NBUCK, MAXD = 32, 128
EPS = 1e-6
C_QA, C_KA, C_VA, C_ZA, C_CQ, C_CKV, C_KR, C_ZB, C_GA, C_GB, C_END = 0, 512, 640, 768, 1280, 1536, 1664, 1696, 2208, 3232, 4256


class Cfg:
    def __init__(self, NC=8, SEQ=16384, DB=16, DSEQ=2048, QP=1024, CH=512):
        self.NC, self.SEQ, self.DB, self.DSEQ, self.QP, self.CH = NC, SEQ, DB, DSEQ, QP, CH
        self.OWNP = SEQ // NC
        self.NS = DB // NC
        self.segs = [(SEQ, self.OWNP)] + [(DSEQ, DSEQ)] * self.NS
        self.passes = []
        for s, (T, own) in enumerate(self.segs):
            assert own % QP == 0 and T % CH == 0 and QP % CH == 0
            for hf in range(own // QP):
                self.passes.append((s, hf))
        self.NPASS = len(self.passes)
        self.NBQ = QP // 128
        self.TK = SEQ + self.NS * DSEQ
        self.TMAX = max(SEQ, DSEQ)
        self.kbase = [0] + [SEQ + i * DSEQ for i in range(self.NS)]


class Prog:
    def __init__(self, nc):
        self.nc = nc
        self.ops = []

    def op(self, eng, fn, reads=(), writes=(), dma=None):
        self.ops.append((eng, fn, tuple(reads), tuple(writes), dma))

    def emit(self, final_wait_eng="sp"):
        nc, ops = self.nc, self.ops
        n = len(ops)
        last_w, readers = {}, {}
        deps = [None] * n
        raw = [None] * n
        for i, (eng, fn, rds, wrs, dma) in enumerate(ops):
            d = set()
            rw = set()
            for r in rds:
                j = last_w.get(r)
                if j is not None:
                    d.add(j)
                    rw.add(j)
            raw[i] = rw
            for w in wrs:
                j = last_w.get(w)
                if j is not None:
                    d.add(j)
                rl = readers.get(w)
                if rl:
                    d.update(rl)
            for r in rds:
                readers.setdefault(r, []).append(i)
            for w in wrs:
                last_w[w] = i
                readers[w] = []
            d.discard(i)
            deps[i] = d
        need_sig = [False] * n
        for i in range(n):
            e = ops[i][0]
            for j in deps[i]:
                if ops[j][4] is None and (ops[j][0] != e or e in SYNC_SAME):
                    need_sig[j] = True
        dma_keys = []
        seenk = set()
        for o in ops:
            if o[4] is not None and o[4] not in seenk:
                seenk.add(o[4])
                dma_keys.append(o[4])
        with contextlib.ExitStack() as st:
            eng_sems = {e: st.enter_context(nc.semaphore("s_" + e)) for e in ENGS}
            dma_sems = {k: st.enter_context(nc.semaphore("d_%d" % i)) for i, k in enumerate(dma_keys)}
            cnt = {e: 0 for e in ENGS}
            dcnt = {k: 0 for k in dma_keys}
            sig = [None] * n
            for i, (eng, fn, rds, wrs, dma) in enumerate(ops):
                if dma is not None:
                    dcnt[dma] += 16
                    sig[i] = (dma_sems[dma], dcnt[dma], ("d", dma))
                elif need_sig[i]:
                    cnt[eng] += 1
                    sig[i] = (eng_sems[eng], cnt[eng], ("e", eng))
            per_eng = {e: [] for e in ENGS}
            for i, o in enumerate(ops):
                per_eng[o[0]].append(i)
            block = st.enter_context(nc.Block())

            def make_body(e):
                def body(eng):
                    seen = {}
                    for i in per_eng[e]:
                        _, fn, _, _, dma = ops[i]
                        need = {}
                        for j in deps[i]:
                            s = sig[j]
                            if s is None:
                                continue
                            if ops[j][4] is None and ops[j][0] == e and e not in SYNC_SAME:
                                continue
                            sem, val, key = s
                            if seen.get(key, 0) >= val:
                                continue
                            if key not in need or need[key][1] < val:
                                need[key] = (sem, val)
                        for key, (sem, val) in need.items():
                            eng.wait_ge(sem, val)
                            seen[key] = val
                        ins = fn(eng)
                        if sig[i] is not None:
                            ins.then_inc(sig[i][0], 16 if dma is not None else 1)
                    if e == final_wait_eng:
                        for k in dma_keys:
                            if seen.get(("d", k), 0) < dcnt[k]:
                                eng.wait_ge(dma_sems[k], dcnt[k])
                        for e2 in ENGS:
                            if e2 != e and cnt[e2] > 0 and seen.get(("e", e2), 0) < cnt[e2]:
                                eng.wait_ge(eng_sems[e2], cnt[e2])
                return body

            block.tensor(make_body("pe"))
            block.scalar(make_body("act"))
            block.vector(make_body("dve"))
            block.gpsimd(make_body("pool"))
            block.sync(make_body("sp"))
        return n


def t5_bucket_np(rel):
    nb = NBUCK // 2
    max_exact = nb // 2
    ret = (rel > 0).astype(np.int32) * nb
    n = np.abs(rel)
    nf = np.maximum(n, 1).astype(np.float32)
    large = max_exact + (np.log(nf / max_exact) / math.log(MAXD / max_exact) * (nb - max_exact)).astype(np.int32)
    large = np.minimum(large, nb - 1)
    return ret + np.where(n < max_exact, n, large)


def build_program(cfg):
    nc = bass.Bass("TRN2", target_bir_lowering=False)
    P = Prog(nc)
    CH, QP, NBQ, NPASS, TK, TMAX = cfg.CH, cfg.QP, cfg.NBQ, cfg.NPASS, cfg.TK, cfg.TMAX
    NBW = NBQ + 2
    NGQ = QP // CH
    NBC = CH // 128
    TREG = max(cfg.SEQ, cfg.DSEQ)

    def din(name, shape, dt=F32):
        return nc.dram_tensor(name, list(shape), dt, kind="ExternalInput").ap()

    xp = din("xp", [cfg.SEQ, D])
    xs = din("xs", [cfg.NS * cfg.DSEQ, D])
    xq = din("xq", [NPASS * NBW * 128, D])
    flags = din("flags", [NPASS * 2])
    ropek = din("ropek", [64, TMAX])
    ropeq = din("ropeq", [NPASS, 64, QP])
    w_in = din("w_in", [D, C_END])
    ln_g = din("ln_g", [128, 8])
    b_gate = din("b_gate", [128, 16])
    sink_a = din("sink_a", [8])
    q_norm_g = din("q_norm_g", [128, 2])
    kv_norm_g = din("kv_norm_g", [128, 1])
    w_uq = din("w_uq", [QL, HB * 96])
    w_ukv = din("w_ukv", [KVL, HB * 128])
    w_proj_a = din("w_proj_a", [512, D])
    w_proj_b = din("w_proj_b", [512, D])
    w_out = din("w_out", [D, D])
    rel_bias = din("rel_bias", [NBUCK, HA])
    final_g = din("final_g", [D])
    onehot = din("onehot", [NBUCK, 512])
    validv = din("validv", [8, 512])
    yout = nc.dram_tensor("y", [NPASS * QP, D], F32, kind="ExternalOutput").ap()

    NWD = 24576 + 4096 * 3 + 8192
    wd_img = nc.dram_tensor("wd_img", [128, NWD], BF16, kind="Internal").ap()
    kvc = nc.dram_tensor("kvc", [160, TK], BF16, kind="Internal").ap()
    evec_t = nc.dram_tensor("evec", [8, 512], F32, kind="Internal")
    etoe_t = nc.dram_tensor("etoe", [8 * 128 * 512], F32, kind="Internal")

    sb = nc.alloc_sbuf_tensor
    O_CN, O_KT, O_V = 0, TREG, 2 * TREG
    NV = (TREG // 128) * 96
    O_QT = O_V + NV
    NREG = max(O_QT + 2 * QP, NWD)
    RG = sb("RG", [128, NREG], BF16)
    cnT = RG[:, O_CN:O_CN + TREG]
    KT = RG[:, O_KT:O_KT + TREG]
    Vb = RG[:, O_V:O_V + NV].rearrange("p (t c) -> p t c", c=96)
    QTs = [RG[:, O_QT + i * QP:O_QT + (i + 1) * QP] for i in range(2)]
    WQA = RG[:, 0:4096].rearrange("p (k c) -> p k c", k=8)
    WG = RG[:, 4096:28672].rearrange("p (k c) -> p k c", k=8)
    WPA = RG[:, 28672:32768].rearrange("p (k c) -> p k c", k=4)
    WPB = RG[:, 32768:36864].rearrange("p (k c) -> p k c", k=4)
    WO = RG[:, 36864:45056].rearrange("p (k c) -> p k c", k=8)

    hTo = sb("hTo", [128, 8 * QP], BF16)
    hTo3 = hTo[:, :].rearrange("p (k c) -> p k c", k=8)
    HR = sb("HR", [128, max(2048, 2 * QP)], BF16)
    hTh = [HR[:, i * 1024:(i + 1) * 1024] for i in range(2)]
    cqnT = sb("cqnT", [128, 2 * QP], BF16)
    cqn3 = cqnT[:, :].rearrange("p (k c) -> p k c", k=2)
    ybT = sb("ybT", [128, 4 * QP], BF16)
    yb3 = ybT[:, :].rearrange("p (k c) -> p k c", k=4)
    kaT = sb("kaT", [128, NBW * 128], BF16)
    vaw = sb("vaw", [128, NBW * 2 * 128], BF16)
    vaw4 = vaw[:, :].rearrange("p (b g c) -> p b g c", b=NBW, g=2)
    WA = sb("WA", [128, 8 * 192], BF16)
    WA3 = WA[:, :].rearrange("p (k c) -> p k c", k=8)
    WB = sb("WB", [128, 8 * 512], BF16)
    WB3 = WB[:, :].rearrange("p (k c) -> p k c", k=8)
    WUQ = sb("WUQ", [128, 2 * 8 * 128], BF16)
    WUQ4 = WUQ[:, :].rearrange("p (k h c) -> p k h c", k=2, h=8)
    WUKV = sb("WUKV", [128, 8 * 128], BF16)
    WUKV3 = WUKV[:, :].rearrange("p (h c) -> p h c", h=8)
    gcol = sb("gcol", [128, 8], F32)
    bg = sb("bg", [128, 16], F32)
    gq = sb("gq", [128, 2], F32)
    gkv = sb("gkv", [128, 1], F32)
    fgB = sb("fgB", [128, D], F32)
    epsT = sb("epsT", [128, 1], F32)
    ident = sb("ident", [128, 128], BF16)
    onesf = sb("onesf", [128, 128], F32)
    esk = sb("esk", [128, 8], F32)
    esinkT = sb("esinkT", [128, 2 * 512], BF16)
    esink3 = esinkT[:, :].rearrange("p (g c) -> p g c", g=2)
    Etab = sb("Etab", [128, 6 * 512], BF16)
    Etab4 = Etab[:, :].rearrange("p (g t c) -> p g t c", g=2, t=3)
    flg = sb("flg", [128, NPASS * 2], F32)
    ropq = HR[:, 0:2 * QP].bitcast(F32)
    NXT = 4
    xt = [sb("xt%d" % i, [128, D], F32) for i in range(NXT)]
    xsb = [sb("xsb%d" % i, [128, D], BF16) for i in range(2)]
    ss = [sb("ss%d" % i, [128, 4], F32) for i in range(NXT)]
    NF = 5
    f32t = [sb("f32t%d" % i, [128, 512], F32) for i in range(NF)]
    b16t = [sb("b16t%d" % i, [128, 512], BF16) for i in range(4)]
    QY = sb("QY", [128, 4096], BF16)
    qaT3 = QY[:, 0:2048].rearrange("p (j c) -> p j c", j=4)
    yaT3 = QY[:, 2048:4096].rearrange("p (j c) -> p j c", j=4)
    rpf = QY[:, 0:2048].bitcast(F32)
    rpk = [rpf[0:64, i * 512:(i + 1) * 512] for i in range(2)]
    cng = [QY[:, 2048 + i * 512:2048 + (i + 1) * 512] for i in range(2)]
    krg = [QY[0:32, 3072 + i * 512:3072 + (i + 1) * 512] for i in range(2)]
    mTb = sb("mTb", [128, 4 * 512], BF16)
    mTl = [QY[:, m * 512:(m + 1) * 512] for m in range(4)] + [mTb[:, m * 512:(m + 1) * 512] for m in range(4)]
    fence_t = sb("fence_t", [128, 2], F32)
    pb = [nc.alloc_psum_tensor("pb%d" % i, [128, 512], F32) for i in range(8)]
    print("sbuf bytes remaining:", nc.sbuf_bytes_remaining)

    st = {"bank": 0, "nb": 7, "f": 0, "b": 0, "xt": 0}
    NWARM = 2

    live_banks = set()

    def bank(hold=False):
        for _ in range(16):
            i = st["bank"] % st["nb"]
            st["bank"] += 1
            if "pb%d" % i not in live_banks:
                break
        else:
            raise RuntimeError("no free PSUM bank")
        if hold:
            live_banks.add("pb%d" % i)
        return pb[i], "pb%d" % i

    def release(k):
        live_banks.discard(k)

    def ftmp():
        i = st["f"] % NF
        st["f"] += 1
        return f32t[i], "f32t%d" % i

    def btmp():
        i = st["b"] % 4
        st["b"] += 1
        return b16t[i], "b16t%d" % i

    def I(eng, name, *args, reads=(), writes=(), dma=None, **kw):
        P.op(eng, lambda e: getattr(e, name)(*args, **kw), reads, writes, dma)

    def DMA(eng, out, in_, reads=(), writes=(), key=None):
        I(eng, "dma_start", out=out, in_=in_, reads=reads, writes=writes, dma=key)

    def MM(out, lhsT, rhs, start, stop, reads, writes):
        I("pe", "matmul", out, lhsT=lhsT, rhs=rhs, start=start, stop=stop, reads=reads, writes=writes)

    def ACT(out, in_, func, reads, writes, **kw):
        I("act", "activation", out=out, in_=in_, func=func, reads=reads, writes=writes, **kw)

    def TT(eng, out, in0, in1, op_, reads, writes):
        I(eng, "tensor_tensor", out=out, in0=in0, in1=in1, op=op_, reads=reads, writes=writes)

    def TS(eng, out, in0, s1, s2, op0, op1, reads, writes):
        if s2 is None:
            I(eng, "tensor_scalar", out=out, in0=in0, scalar1=s1, scalar2=None, op0=op0, reads=reads, writes=writes)
        else:
            I(eng, "tensor_scalar", out=out, in0=in0, scalar1=s1, scalar2=s2, op0=op0, op1=op1, reads=reads, writes=writes)

    def STT(eng, out, in0, scalar, in1, reads, writes):
        I(eng, "scalar_tensor_tensor", out=out, in0=in0, scalar=scalar, in1=in1, op0=ALU.mult, op1=ALU.mult, reads=reads, writes=writes)

    def CP(eng, out, in_, reads, writes):
        I(eng, "tensor_copy", out=out, in_=in_, reads=reads, writes=writes)

    def RCP(out, in_, reads, writes):
        I("dve", "reciprocal", out=out, in_=in_, reads=reads, writes=writes)

    I("pool", "memset", epsT[:], EPS, writes=["epsT"])
    I("pool", "memset", onesf[:], 1.0, writes=["onesf"])
    idf, idk = f32t[0], "f32t0"
    I("pool", "memset", idf[:, 0:128], 1.0, writes=[idk])
    I("pool", "affine_select", out=idf[:, 0:128], in_=idf[:, 0:128], pattern=[[-1, 128]], compare_op=ALU.is_equal,
      fill=0.0, base=0, channel_multiplier=1, reads=[idk], writes=[idk])
    CP("dve", ident[:], idf[:, 0:128], [idk], ["ident"])
    I("pool", "memset", vaw[:], 1.0, writes=["vaw_init"])
    for t_, d_, k_ in [(gcol, ln_g, "gcol"), (bg, b_gate, "bg"), (gq, q_norm_g, "gq"), (gkv, kv_norm_g, "gkv")]:
        DMA("sp", t_[:], d_[:, :], writes=[k_], key="ld_" + k_)
    DMA("sp", fgB[:], bass.AP(final_g.tensor, 0, [[0, 128], [1, D]]), writes=["fgB"], key="ld_fgB")
    DMA("sp", esk[:], bass.AP(sink_a.tensor, 0, [[0, 128], [1, 8]]), writes=["esk"], key="ld_esk")
    DMA("sp", flg[:], bass.AP(flags.tensor, 0, [[0, 128], [1, NPASS * 2]]), writes=["flg"], key="ld_flg")
    ACT(esk[:], esk[:], AF.Exp, ["esk"], ["esk"])
    for h in range(8):
        g, i = h // 4, h % 4
        CP("dve", esink3[:, g, i * 128:(i + 1) * 128], esk[:, h:h + 1].to_broadcast([128, 128]), ["esk"], ["esinkT"])

    rbt, oht, vvt, ev = f32t[1], f32t[2], f32t[3], f32t[4]
    DMA("sp", rbt[0:NBUCK, 0:8], rel_bias[:, :], writes=["f32t1"], key="ld_rb")
    DMA("sp", oht[0:NBUCK, :], onehot[:, :], writes=["f32t2"], key="ld_oh")
    DMA("sp", vvt[0:8, :], validv[:, :], writes=["f32t3"], key="ld_vv")
    MM(pb[0][0:8, :], rbt[0:NBUCK, 0:8], oht[0:NBUCK, :], True, True, ["f32t1", "f32t2"], ["pb0"])
    ACT(ev[0:8, :], pb[0][0:8, :], AF.Exp, ["pb0"], ["f32t4"])
    TT("dve", ev[0:8, :], ev[0:8, :], vvt[0:8, :], ALU.mult, ["f32t4", "f32t3"], ["f32t4"])
    DMA("sp", evec_t.ap()[:, :], ev[0:8, :], reads=["f32t4"], writes=["evec"], key="st_evec")
    DMA("sp", bass.AP(etoe_t, 0, [[128 * 512, 8], [512, 128], [1, 511]]), bass.AP(evec_t, 0, [[512, 8], [0, 128], [1, 511]]),
        reads=["evec"], writes=["etoe"], key="st_etoe")
    CT = [383, 255, 127]
    st["f"] = 0
    for g in range(2):
        for t in range(3):
            ft, fk = ftmp()
            DMA("sp", ft[:, :].rearrange("p (h c) -> p h c", h=4),
                bass.AP(etoe_t, 4 * g * 128 * 512 + CT[t], [[511, 128], [128 * 512, 4], [1, 128]]),
                reads=["etoe"], writes=[fk], key="ld_E%d%d" % (g, t))
            CP("dve", Etab4[:, g, t, :], ft[:, :], [fk], ["Etab"])

    def stage(src_ap, ncols):
        i = st["xt"] % NXT
        st["xt"] += 1
        DMA("sp", xt[i][:, 0:ncols], src_ap, writes=["xt%d" % i], key="ld_xt%d" % i)
        return xt[i], ["xt%d" % i, "gcol", "RG"]

    for k in range(8):
        rows = slice(k * 128, (k + 1) * 128)
        gk = gcol[:, k:k + 1]
        x_, rd = stage(w_in[rows, 0:768], 768)
        for hf in range(2):
            TS("dve", WQA[:, k, :].rearrange("p (j a c) -> p j a c", j=4, a=2)[:, :, hf, :],
               x_[:, hf * 256:(hf + 1) * 256].rearrange("p (j c) -> p j c", j=4), gk, None, ALU.mult, None, rd, ["WQA"])
        TS("dve", WB3[:, k, 256:512], x_[:, 512:768], gk, None, ALU.mult, None, rd, ["WB"])
        x_, rd = stage(w_in[rows, 768:1696], 928)
        TS("dve", WG[:, k, 0:512], x_[:, 0:512], gk, None, ALU.mult, None, rd, ["WG"])
        TS("dve", WB3[:, k, 0:256], x_[:, 512:768], gk, None, ALU.mult, None, rd, ["WB"])
        TS("dve", WA3[:, k, 0:160], x_[:, 768:928], gk, None, ALU.mult, None, rd, ["WA"])
        TS("dve", WA3[:, k, 176:192], x_[:, 896:912], gk, None, ALU.mult, None, rd, ["WA"])
        TS("dve", WA3[:, k, 160:176], x_[:, 912:928], gk, -1.0, ALU.mult, ALU.mult, rd, ["WA"])
        x_, rd = stage(w_in[rows, 1696:2720], 1024)
        TS("dve", WG[:, k, 512:1536], x_[:, 0:1024], gk, None, ALU.mult, None, rd, ["WG"])
        x_, rd = stage(w_in[rows, 2720:3744], 1024)
        TS("dve", WG[:, k, 1536:2560], x_[:, 0:1024], gk, None, ALU.mult, None, rd, ["WG"])
        x_, rd = stage(w_in[rows, 3744:4256], 512)
        TS("dve", WG[:, k, 2560:3072], x_[:, 0:512], gk, None, ALU.mult, None, rd, ["WG"])
    for k in range(2):
        x_, rd = stage(w_uq[k * 128:(k + 1) * 128, :], 768)
        x3 = x_[:, 0:768].rearrange("p (h c) -> p h c", h=8)
        CP("dve", WUQ4[:, k, :, 0:96], x3, rd, ["WUQ"])
        TS("dve", WUQ4[:, k, :, 96:112], x3[:, :, 80:96], -1.0, None, ALU.mult, None, rd, ["WUQ"])
        CP("dve", WUQ4[:, k, :, 112:128], x3[:, :, 64:80], rd, ["WUQ"])
    x_, rd = stage(w_ukv[:, :], 1024)
    CP("dve", WUKV[:, :], x_[:, :], rd, ["WUKV"])
    for k in range(4):
        x_, rd = stage(w_proj_a[k * 128:(k + 1) * 128, :], 1024)
        CP("dve", WPA[:, k, :], x_[:, :], rd, ["WPA"])
        x_, rd = stage(w_proj_b[k * 128:(k + 1) * 128, :], 1024)
        CP("dve", WPB[:, k, :], x_[:, :], rd, ["WPB"])
    for k in range(8):
        x_, rd = stage(w_out[k * 128:(k + 1) * 128, :], 1024)
        CP("dve", WO[:, k, :], x_[:, :], rd, ["WO"])
    wpieces = [(0, 4096), (4096, 16384), (16384, 28672), (28672, 36864), (36864, NWD)]
    NWP = len(wpieces)
    EARLY = [i for i in (0, 1) if wpieces[i][1] <= TREG]
    for i, (a, b) in enumerate(wpieces):
        DMA("sp", wd_img[:, a:b], RG[:, a:b], reads=["WG", "WQA", "WPA", "WPB", "WO", "RG"], writes=["wd_img%d" % i], key="st_wd%d" % i)

    dbg = getattr(cfg, "debug", False)

    def dump(name, ap, shape, dt, reads):
        if not dbg:
            return
        o = nc.dram_tensor(name, list(shape), dt, kind="ExternalOutput").ap()
        DMA("sp", o, ap, reads=reads, key="dbg_" + name)

    def fence():
        I("pool", "memset", fence_t[:], 0.0, writes=["RG", "fence_t"])

    def pipeline(tasks):
        active = []
        tasks = list(tasks)
        while tasks or active:
            if tasks:
                active.append(tasks.pop(0))
            nxt = []
            for g_ in active:
                try:
                    next(g_)
                    nxt.append(g_)
                except StopIteration:
                    pass
            active = nxt

    def load_norm_transpose(src_ap, dst_view, dst_keys, extra_reads=(), tail=None):
        i = st["xt"] % NXT
        st["xt"] += 1
        xtile, xk, sst, sk = xt[i], "xt%d" % i, ss[i], "ss%d" % i
        j = st["xt"] % 2
        xb, xbk = xsb[j], "xsb%d" % j
        DMA("sp", xtile[:, :], src_ap, writes=[xk], key="ld_" + xk)
        ACT(xb[:, :], xtile[:, :], AF.Square, [xk], [xbk, sk], accum_out=sst[:, 0:1])
        ACT(sst[:, 1:2], sst[:, 0:1], AF.Ln, [sk, "epsT"], [sk], scale=1.0 / D, bias=epsT[:, 0:1])
        ACT(sst[:, 2:3], sst[:, 1:2], AF.Exp, [sk], [sk], scale=-0.5)
        TS("dve", xb[:, :], xtile[:, :], sst[:, 2:3], None, ALU.mult, None, [xk, sk], [xbk])
        yield
        bk, bkk = bank(hold=True)
        bkb = bk[:, :].bitcast(BF16)
        for k in range(8):
            I("pe", "transpose", out=bkb[:, k * 128:(k + 1) * 128], in_=xb[:, k * 128:(k + 1) * 128], identity=ident[:],
              reads=[xbk, "ident"], writes=[bkk])
        for _ in range(NWARM):
            MM(pb[7][:, 0:512], ident[:], WB[:, 0:512], True, True, ["ident", "WB"], ["pb7"])
        yield
        CP("dve", dst_view, bkb[:, :].rearrange("p (k c) -> p k c", k=8), [bkk] + list(extra_reads), dst_keys)
        release(bkk)
        if tail is not None:
            yield
            r_ = tail()
            if r_ is not None:
                yield from r_

    if QP >= 2 * CH:
        hA = [hTo[:, i * 8 * CH:(i + 1) * 8 * CH].rearrange("p (k c) -> p k c", k=8) for i in range(2)]
    else:
        hAt = [sb("hA%d" % i, [128, 8 * CH], BF16) for i in range(2)]
        hA = [t_[:, :].rearrange("p (k c) -> p k c", k=8) for t_ in hAt]
    _ob = onesf[:, :].bitcast(BF16)
    onesb = bass.AP(_ob.tensor, _ob.offset + 1, [[_ob.ap[0][0], 128], [2, 128]])

    def phaseA_group_tail(grp, slot, tok0, pos0, hks):
        b1, b1k = bank(hold=True)
        for k in range(8):
            MM(b1[:, 0:CH], WA3[:, k, 0:128], hA[slot][:, k, :], k == 0, k == 7, hks + ["WA", "RG"], [b1k])
        b2, b2k = bank(hold=True)
        for k in range(8):
            MM(b2[0:64, 0:CH], WA3[:, k, 128:192], hA[slot][:, k, :], k == 0, k == 7, hks + ["WA", "RG"], [b2k])
        rp, rpkk = rpk[slot], "rpk%d" % slot
        DMA("sp", rp[:, 0:CH], ropek[:, pos0:pos0 + CH], reads=["RG"], writes=[rpkk], key="ld_" + rpkk)
        yield
        sq, sqk = btmp()
        ACT(sq[:, 0:CH], b1[:, 0:CH], AF.Square, [b1k], [sqk])
        b3, b3k = bank(hold=True)
        MM(b3[:, 0:CH], onesb, sq[:, 0:CH], True, True, [sqk, "onesf"], [b3k])
        ta, tak = ftmp()
        tb, tbk = ftmp()
        TT("dve", ta[0:32, 0:CH], b2[0:32, 0:CH], rp[0:32, 0:CH], ALU.mult, [b2k, rpkk, "RG"], [tak])
        TT("dve", tb[0:32, 0:CH], b2[32:64, 0:CH], rp[32:64, 0:CH], ALU.mult, [b2k, rpkk, "RG"], [tbk])
        release(b2k)
        yield
        rc, rck = ftmp()
        ACT(rc[:, 0:CH], b3[:, 0:CH], AF.Ln, [b3k, "epsT"], [rck], scale=1.0 / KVL, bias=epsT[:, 0:1])
        release(b3k)
        ACT(rc[:, 0:CH], rc[:, 0:CH], AF.Exp, [rck], [rck], scale=-0.5)
        kg, kgk = krg[slot], "krg%d" % slot
        TT("dve", kg[:, 0:CH], ta[0:32, 0:CH], tb[0:32, 0:CH], ALU.add, [tak, tbk, "RG"], [kgk])
        DMA("pool", kvc[128:160, tok0:tok0 + CH], kg[:, 0:CH], reads=[kgk, "RG"], writes=["kvck%d" % grp], key="st_" + kgk)
        yield
        cg, cgk = cng[slot], "cng%d" % slot
        STT("dve", cg[:, 0:CH], b1[:, 0:CH], gkv[:, 0:1], rc[:, 0:CH], [b1k, rck, "gkv", "RG"], [cgk])
        release(b1k)
        DMA("pool", kvc[0:128, tok0:tok0 + CH], cg[:, 0:CH], reads=[cgk, "RG"], writes=["kvcc%d" % grp], key="st_" + cgk)

    tasksA = []
    for grp in range(TK // CH):
        slot = grp % 2
        tok0 = grp * CH
        if tok0 < cfg.SEQ:
            src, r0, pos0 = xp, tok0, tok0
        else:
            src, r0 = xs, tok0 - cfg.SEQ
            pos0 = r0 % cfg.DSEQ
        hks = ["hA%d_%d" % (slot, tl) for tl in range(NBC)]
        for tl in range(NBC):
            tail = None
            if tl == NBC - 1:
                tail = (lambda grp=grp, slot=slot, tok0=tok0, pos0=pos0, hks=hks: phaseA_group_tail(grp, slot, tok0, pos0, hks))
            tasksA.append(load_norm_transpose(src[r0 + tl * 128: r0 + (tl + 1) * 128, :], hA[slot][:, :, tl * 128:(tl + 1) * 128],
                                              [hks[tl]], ["RG"], tail))
    pipeline(tasksA)

    dump("d_kvc", kvc[:, :], [160, TK], BF16, ["kvcc%d" % g_ for g_ in range(TK // CH)] + ["kvck%d" % g_ for g_ in range(TK // CH)])
    dump("d_E", Etab[:, :], [128, 6 * 512], BF16, ["Etab"])
    SC_A = DH ** -0.5
    SC_B = (NOPE + ROPE) ** -0.5
    ocnt = 0
    B_done = set()
    for p, (seg, hf) in enumerate(cfg.passes):
        T, own = cfg.segs[seg]
        kb = cfg.kbase[seg]
        NJ = T // CH
        NT = T // 128
        st["nb"] = 7
        fence()
        NLD = min(4, NJ)
        for i in range(NLD):
            j0, j1 = i * NJ // NLD, (i + 1) * NJ // NLD
            a, b = j0 * CH, j1 * CH
            kvr = ["kvcc%d" % ((kb + jj * CH) // CH) for jj in range(j0, j1)] + ["kvck%d" % ((kb + jj * CH) // CH) for jj in range(j0, j1)]
            DMA("pool", cnT[:, a:b], kvc[0:128, kb + a:kb + b], reads=["RG"] + kvr, writes=["cn%d" % j for j in range(j0, j1)], key="ld_cn%d" % i)
            DMA("pool", KT[64:96, a:b], kvc[128:160, kb + a:kb + b], reads=["RG"] + kvr, writes=["KTr%d" % j for j in range(j0, j1)], key="ld_kr%d" % i)
        I("pool", "memset", Vb[:, 0:NT, 64:96], 1.0, reads=["RG"], writes=["Vones"])
        def phaseB_tail(blk, hv, hkeys):
            bka, bkak = bank(hold=True)
            for k in range(8):
                MM(bka[:, 0:128], WB3[:, k, 256:384], hv[:, k, :], k == 0, k == 7, hkeys + ["WB"], [bkak])
            bkv, bkvk = bank(hold=True)
            for k in range(8):
                MM(bkv[:, 0:128], hv[:, k, :], WB3[:, k, 384:512], k == 0, k == 7, hkeys + ["WB"], [bkvk])
            yield
            CP("dve", kaT[:, blk * 128:(blk + 1) * 128], bka[:, 0:128], [bkak], ["kaT%d" % blk])
            CP("dve", vaw4[:, blk, :, 0:64], bkv[:, 0:128].rearrange("p (g c) -> p g c", g=2), [bkvk, "vaw_init"], ["vaw%d" % blk])
            release(bkak)
            release(bkvk)

        def make_B_tasks(pp):
          tasksB = []
          for blk in range(NBW):
            row0 = (pp * NBW + blk) * 128
            if blk == 0 or blk == NBW - 1:
                hi = 0 if blk == 0 else 1
                hv = hTh[hi].rearrange("p (k c) -> p k c", k=8)
                hkeys = ["hTh%d" % hi, "ropq"]
            else:
                hv = hTo3[:, :, (blk - 1) * 128: blk * 128]
                hkeys = ["hTo%d" % (blk - 1)]
            tasksB.append(load_norm_transpose(xq[row0:row0 + 128, :], hv, hkeys, ["RG"] if pp == 0 else [],
                                              (lambda blk=blk, hv=hv, hkeys=hkeys: phaseB_tail(blk, hv, hkeys))))
          return tasksB

        def emit_B_cq():
          for grp in range(NGQ):
              cols = slice(grp * CH, (grp + 1) * CH)
              hk = ["hTo%d" % b for b in range(grp * NBC, (grp + 1) * NBC)]
              bq = []
              for m in range(2):
                  bk, bkk = bank()
                  bq.append((bk, bkk))
                  for k in range(8):
                      MM(bk[:, 0:CH], WB3[:, k, m * 128:(m + 1) * 128], hTo3[:, k, cols], k == 0, k == 7, hk + ["WB"], [bkk])
              b3, b3k = bank()
              for m in range(2):
                  sq, sqk = btmp()
                  ACT(sq[:, 0:CH], bq[m][0][:, 0:CH], AF.Square, [bq[m][1]], [sqk])
                  MM(b3[:, 0:CH], onesb, sq[:, 0:CH], m == 0, m == 1, [sqk, "onesf"], [b3k])
              rc, rck = ftmp()
              ACT(rc[:, 0:CH], b3[:, 0:CH], AF.Ln, [b3k, "epsT"], [rck], scale=1.0 / QL, bias=epsT[:, 0:1])
              ACT(rc[:, 0:CH], rc[:, 0:CH], AF.Exp, [rck], [rck], scale=-0.5)
              for m in range(2):
                  STT("dve", cqn3[:, m, cols], bq[m][0][:, 0:CH], gq[:, m:m + 1], rc[:, 0:CH], [bq[m][1], rck, "gq"], ["cqn%d_%d" % (grp, m)])

        if p not in B_done:
            pipeline(make_B_tasks(p))
            emit_B_cq()
            B_done.add(p)

        if p == 0:
            dump("d_hTo", hTo[:, :], [128, 8 * QP], BF16, ["hTo%d" % b_ for b_ in range(NBQ)])
            dump("d_cqn", cqnT[:, :], [128, 2 * QP], BF16, ["cqn%d_%d" % (g_, m_) for g_ in range(NGQ) for m_ in range(2)])
            dump("d_ka", kaT[:, :], [128, NBW * 128], BF16, ["kaT%d" % b_ for b_ in range(NBW)])
            dump("d_va", vaw[:, :], [128, NBW * 2 * 128], BF16, ["vaw%d" % b_ for b_ in range(NBW)])
        DMA("sp", ropq[64:128, :], ropeq[p, :, :], writes=["ropq", "hTh0", "hTh1"], key="ld_ropq")
        st["nb"] = 2
        st["bank"] = 0
        ocount = 0
        def expK(h, j):
            bk, bkk = bank()
            MM(bk[0:64, 0:CH], WUKV3[:, h, 0:64], cnT[:, j * CH:(j + 1) * CH], True, True, ["cn%d" % j, "WUKV", "RG"], [bkk])
            CP("dve", KT[0:64, j * CH:(j + 1) * CH], bk[0:64, 0:CH], [bkk, "RG"], ["KTn%d" % j])

        def expV(h, t0):
            nt = min(8, NT - t0)
            bk, bkk = bank()
            for i in range(nt):
                MM(bk[:, i * 64:(i + 1) * 64], cnT[:, (t0 + i) * 128:(t0 + i + 1) * 128], WUKV3[:, h, 64:128], True, True,
                   ["cn%d" % ((t0 + i) * 128 // CH), "WUKV", "RG"], [bkk])
            CP("dve", Vb[:, t0:t0 + nt, 0:64], bk[:, 0:nt * 64].rearrange("p (t c) -> p t c", c=64), [bkk, "RG", "Vones"], ["V%d" % (t0 // 8)])

        def expQ(h, grp):
            QT = QTs[h % 2]
            qk_ = "QT%d" % (h % 2)
            cols = slice(grp * CH, (grp + 1) * CH)
            bk, bkk = bank()
            for k in range(2):
                MM(bk[:, 0:CH], WUQ4[:, k, h, :], cqn3[:, k, cols], k == 0, k == 1, ["cqn%d_%d" % (grp, k), "WUQ"], [bkk])
            CP("dve", QT[0:64, cols], bk[0:64, 0:CH], [bkk, "RG"], [qk_ + "n%d" % grp])
            ta, tak = ftmp()
            tb, tbk = ftmp()
            TT("dve", ta[64:96, 0:CH], bk[64:96, 0:CH], ropq[64:96, cols], ALU.mult, [bkk, "ropq"], [tak])
            TT("dve", tb[64:96, 0:CH], bk[96:128, 0:CH], ropq[96:128, cols], ALU.mult, [bkk, "ropq"], [tbk])
            TT("dve", QT[64:96, cols], ta[64:96, 0:CH], tb[64:96, 0:CH], ALU.add, [tak, tbk, "RG"], [qk_ + "r%d" % grp])

        for grp in range(NGQ):
            expQ(0, grp)
        for j in range(NJ):
            expK(0, j)
        for t0 in range(0, NT, 8):
            expV(0, t0)
        units = [(h, qc, t) for h in range(HB) for qc in range(NGQ) for t in range(NT)]
        LOOK = min(3, NT * NGQ)

        def qk(u):
            h, qc, t = units[u]
            QT = QTs[h % 2]
            qk_ = "QT%d" % (h % 2)
            s_, sk_ = pb[4 + (u % 4)], "pb%d" % (4 + u % 4)
            j = t * 128 // CH
            MM(s_[:, 0:CH], KT[0:96, t * 128:(t + 1) * 128], QT[0:96, qc * CH:(qc + 1) * CH], True, True,
               ["KTn%d" % j, "KTr%d" % j, qk_ + "n%d" % qc, qk_ + "r%d" % qc, "RG"], [sk_])

        for u in range(min(LOOK, len(units))):
            qk(u)
        O, Ok = None, None
        pendK = []
        for u, (h, qc, t) in enumerate(units):
            if t == 0:
                O, Ok = pb[2 + ocount % 2], "pb%d" % (2 + ocount % 2)
                ocount += 1
            if h == HB - 1 and qc == 0 and t == 0:
                for i in EARLY:
                    a, b = wpieces[i]
                    DMA("sp", RG[:, a:b], wd_img[:, a:b], reads=["wd_img%d" % i],
                        writes=["WD%d" % i] + ["cn%d" % j for j in range(a // CH, (b + CH - 1) // CH)], key="ld_wd%d" % i)
            s_, sk_ = pb[4 + (u % 4)], "pb%d" % (4 + u % 4)
            pt, ptk = btmp()
            ACT(pt[:, 0:CH], s_[:, 0:CH], AF.Exp, [sk_], [ptk], scale=SC_B)
            last_sweep = (qc == NGQ - 1) and (h + 1 < HB)
            if u + LOOK < len(units):
                h2, qc2, t2 = units[u + LOOK]
                if h2 == h or (qc == NGQ - 1 and (t2 // NBC + 1) * NBC - 1 < t):
                    qk(u + LOOK)
                else:
                    pendK.append(u + LOOK)
            MM(O[0:96, 0:CH], Vb[:, t, :], pt[:, 0:CH], t == 0, t == NT - 1, ["V%d" % (t // 8), "Vones", ptk, "RG"], [Ok])
            if qc == 0 and t == 0 and h + 1 < HB:
                for grp in range(NGQ):
                    expQ(h + 1, grp)
            if last_sweep:
                if (t + 1) % NBC == 0:
                    expK(h + 1, t // NBC)
                if (t + 1) % 8 == 0 or t == NT - 1:
                    expV(h + 1, (t // 8) * 8)
            if t == NT - 1:
                r, rk = ftmp()
                RCP(r[0:32, 0:CH], O[64:96, 0:CH], [Ok], [rk])
                r0_ = (h % 2) * 64
                for hh_ in range(2):
                    TT("dve", yb3[r0_ + 32 * hh_:r0_ + 32 * hh_ + 32, h // 2, qc * CH:(qc + 1) * CH], O[32 * hh_:32 * hh_ + 32, 0:CH],
                       r[0:32, 0:CH], ALU.mult, [Ok, rk], ["yb%d_%d_%d" % (h // 2, qc, h % 2)])
                if qc == NGQ - 1:
                    for u2 in pendK:
                        qk(u2)
                    pendK = []

        if p == 0:
            dump("d_yb", ybT[:, :], [128, 4 * QP], BF16, ["yb%d_%d_%d" % (j_, c_, h_) for j_ in range(4) for c_ in range(NGQ) for h_ in range(2)])
        st["nb"] = 8
        fence()
        for i, (a, b) in enumerate(wpieces):
            if i not in EARLY:
                DMA("sp" if i % 2 == 0 else "pool", RG[:, a:b], wd_img[:, a:b], reads=["RG", "wd_img%d" % i], writes=["WD%d" % i], key="ld_wd%d" % i)
        K_QA, K_G, K_P, K_O = ["WD0", "RG"], ["WD1", "WD2", "RG"], ["WD3", "RG"], ["WD4", "RG"]
        for ch in range(NGQ):
            cols = slice(ch * CH, (ch + 1) * CH)
            hk = ["hTo%d" % b for b in range(ch * NBC, (ch + 1) * NBC)]
            for j in range(4):
                bk, bkk = bank()
                for k in range(8):
                    MM(bk[:, 0:CH], WQA[:, k, j * 128:(j + 1) * 128], hTo3[:, k, cols], k == 0, k == 7, hk + K_QA, [bkk])
                ACT(qaT3[:, j, 0:CH], bk[:, 0:CH], AF.Copy, [bkk, "RG"], ["qaT", "mT0", "mT1", "mT2", "mT3"])
            def win_task(b, g, ch=ch, p=p):
                gbk = ch * NBC + b
                rows = slice(g * 64, g * 64 + 64)
                sl = []
                for t in range(3):
                    kblk = gbk + t
                    s_, sk_ = bank()
                    sl.append((s_, sk_))
                    MM(s_[:, 0:512], kaT[rows, kblk * 128:(kblk + 1) * 128], qaT3[rows, :, b * 128:(b + 1) * 128], True, True,
                       ["kaT%d" % kblk, "qaT"], [sk_])
                yield
                pl = []
                for t in range(3):
                    s_, sk_ = sl[t]
                    pf, pfk = ftmp()
                    ACT(pf[:, :], s_[:, 0:512], AF.Exp, [sk_], [pfk], scale=SC_A)
                    if (gbk == 0 and t == 0) or (gbk == NBQ - 1 and t == 2):
                        fcol = 2 * p if t == 0 else 2 * p + 1
                        Ef, Ek = ftmp()
                        Efv = Ef[:, :].bitcast(BF16)[:, 0:512]
                        TS("dve", Efv, Etab4[:, g, t, :], flg[:, fcol:fcol + 1], None, ALU.mult, None, ["Etab", "flg"], [Ek])
                        Et = Efv
                    else:
                        Et, Ek = Etab4[:, g, t, :], "Etab"
                    pw, pwk = btmp()
                    pl.append((pw, pwk))
                    TT("dve", pw[:, :], pf[:, :], Et, ALU.mult, [pfk, Ek], [pwk])
                yield
                O, Ok = bank()
                for t in range(3):
                    kblk = gbk + t
                    pw, pwk = pl[t]
                    MM(O[:, 0:512], vaw4[:, kblk, g, :], pw[:, :], t == 0, t == 2, ["vaw%d" % kblk, "vaw_init", pwk], [Ok])
                dn, dnk = ftmp()
                TT("dve", dn[64:128, :], O[64:128, 0:512], esink3[64:128, g, :], ALU.add, [Ok, "esinkT"], [dnk])
                yield
                r, rk = ftmp()
                ACT(dn[64:128, :], dn[64:128, :], AF.Ln, [dnk], [dnk])
                ACT(r[0:64, :], dn[64:128, :], AF.Exp, [dnk], [rk], scale=-1.0)
                for half in range(2):
                    orow = slice(half * 64, half * 64 + 64)
                    TT("dve", yaT3[orow, 2 * g:2 * g + 2, b * 128:(b + 1) * 128],
                       O[0:64, 0:512].rearrange("p (i a c) -> p i a c", i=2, a=2)[:, :, half, :],
                       r[0:64, :].rearrange("p (i a c) -> p i a c", i=2, a=2)[:, :, half, :], ALU.mult, [Ok, rk, "RG"], ["yaT"])

            pipeline([win_task(b, g) for b in range(NBC) for g in range(2)])
            if p == 0 and ch == 0:
                dump("d_ya", QY[:, 2048:4096], [128, 2048], BF16, ["yaT"])
                dump("d_qa", QY[:, 0:2048], [128, 2048], BF16, ["qaT"])
            for br in range(2):
                for j in range(4):
                    bk, bkk = bank()
                    c0 = br * 512 + j * 128
                    for k in range(8):
                        MM(bk[:, 0:CH], WG[:, k, c0:c0 + 128], hTo3[:, k, cols], k == 0, k == 7, hk + K_G, [bkk])
                    sz, szk = ftmp()
                    ACT(sz[:, 0:CH], bk[:, 0:CH], AF.Silu, [bkk], [szk])
                    if br == 0:
                        yv, yk = yaT3[:, j, 0:CH], ["yaT"]
                    else:
                        yv, yk = yb3[:, j, cols], ["yb%d_%d_0" % (j, ch), "yb%d_%d_1" % (j, ch)]
                    TT("dve", yv, sz[:, 0:CH], yv, ALU.mult, [szk] + yk, yk + ["uT%d" % (br * 4 + j)])
            for m in range(8):
                ba, bak = bank()
                for j in range(4):
                    MM(ba[:, 0:CH], WPA[:, j, m * 128:(m + 1) * 128], yaT3[:, j, 0:CH], j == 0, j == 3, ["uT%d" % j, "yaT"] + K_P, [bak])
                bb, bbk = bank()
                for j in range(4):
                    MM(bb[:, 0:CH], WPB[:, j, m * 128:(m + 1) * 128], yb3[:, j, cols], j == 0, j == 3,
                       ["uT%d" % (4 + j), "yb%d_%d_0" % (j, ch), "yb%d_%d_1" % (j, ch)] + K_P, [bbk])
                sg = []
                for br in range(2):
                    bk, bkk = bank()
                    c0 = 1024 + br * 1024 + m * 128
                    for k in range(8):
                        MM(bk[:, 0:CH], WG[:, k, c0:c0 + 128], hTo3[:, k, cols], k == 0, k == 7, hk + K_G, [bkk])
                    s_, sk_ = ftmp()
                    ACT(s_[:, 0:CH], bk[:, 0:CH], AF.Sigmoid, [bkk, "bg"], [sk_], bias=bg[:, br * 8 + m:br * 8 + m + 1])
                    sg.append((s_, sk_))
                t1, t1k = ftmp()
                TT("dve", t1[:, 0:CH], ba[:, 0:CH], sg[0][0][:, 0:CH], ALU.mult, [bak, sg[0][1]], [t1k])
                t2, t2k = ftmp()
                TT("dve", t2[:, 0:CH], bb[:, 0:CH], sg[1][0][:, 0:CH], ALU.mult, [bbk, sg[1][1]], [t2k])
                TT("pool", mTl[m][:, 0:CH], t1[:, 0:CH], t2[:, 0:CH], ALU.add, [t1k, t2k, "RG"], ["mT%d" % m] + (["qaT"] if m < 4 else []))
            if p == 0 and ch == 0:
                dump("d_m0", QY[:, 0:2048], [128, 2048], BF16, ["mT%d" % m_ for m_ in range(4)])
                dump("d_m1", mTb[:, :], [128, 2048], BF16, ["mT%d" % m_ for m_ in range(4, 8)])
                dump("d_u", QY[:, 2048:4096], [128, 2048], BF16, ["yaT"])
            def outproj_task(tb_, ch=ch, p=p):
                nonlocal ocnt
                blk = ch * NBC + tb_
                row0 = (p * NBW + 1 + blk) * 128
                oi = ocnt % 2
                ocnt += 1
                i = st["xt"] % NXT
                st["xt"] += 1
                xtile, xk, sst, sk = xt[i], "xt%d" % i, ss[i], "ss%d" % i
                DMA("sp", xtile[:, :], xq[row0:row0 + 128, :], writes=[xk], key="ld_" + xk)
                rs, rsk = xtile, xk
                for hf2 in range(2):
                    bk, bkk = bank()
                    for k in range(8):
                        MM(bk[:, 0:512], mTl[k][:, tb_ * 128:(tb_ + 1) * 128], WO[:, k, hf2 * 512:(hf2 + 1) * 512], k == 0, k == 7,
                           ["mT%d" % k] + K_O, [bkk])
                    TT("dve", rs[:, hf2 * 512:(hf2 + 1) * 512], bk[:, 0:512], xtile[:, hf2 * 512:(hf2 + 1) * 512], ALU.add, [bkk, xk], [rsk])
                yield
                jx, jxk = xsb[oi], "xsb%d" % oi
                ACT(jx[:, :], rs[:, :], AF.Square, [rsk], [jxk, sk], accum_out=sst[:, 0:1])
                ACT(sst[:, 1:2], sst[:, 0:1], AF.Ln, [sk, "epsT"], [sk], scale=1.0 / D, bias=epsT[:, 0:1])
                ACT(sst[:, 2:3], sst[:, 1:2], AF.Exp, [sk], [sk], scale=-0.5)
                yo, yok = xtile, xk
                STT("dve", yo[:, :], rs[:, :], sst[:, 2:3], fgB[:, :], [rsk, sk, "fgB"], [yok])
                orow = p * QP + blk * 128
                DMA("pool", yout[orow:orow + 128, :], yo[:, :], reads=[yok], key="st_xt%d" % i)

            optasks = [outproj_task(tb_) for tb_ in range(NBC)]
            if ch == NGQ - 1 and p + 1 < NPASS:
                st["nb"] = 7
                btasks = make_B_tasks(p + 1)
                merged = []
                for k_ in range(max(len(optasks), len(btasks))):
                    if k_ < len(optasks):
                        merged.append(optasks[k_])
                    if k_ < len(btasks):
                        merged.append(btasks[k_])
                pipeline(merged)
                B_done.add(p + 1)
                pend_cq = True
            else:
                pipeline(optasks)
                pend_cq = False
        if pend_cq:
            emit_B_cq()
    n = P.emit()
    print("ops:", n)
    return nc


def host_inputs(cfg, x_prompt, x_sample, ln_g, w_in, b_gate, sink_a, q_norm_g, kv_norm_g, w_uq, w_ukv,
                w_proj_a, w_proj_b, w_out, rel_bias, final_g):
    f = np.float32
    x_prompt = np.asarray(x_prompt, f)
    x_sample = np.asarray(x_sample, f)
    xp = np.ascontiguousarray(x_prompt[0])
    NBW = cfg.NBQ + 2
    QP = cfg.QP
    half = ROPE // 2
    inv = np.power(np.float32(10000.0), -np.arange(half, dtype=f) / half).astype(f)
    pos = np.arange(cfg.TMAX, dtype=f)
    ang = pos[None, :] * inv[:, None]
    cos2 = np.concatenate([np.cos(ang), np.cos(ang)], 0).astype(f)
    sin2 = np.concatenate([np.sin(ang), np.sin(ang)], 0).astype(f)
    ropek = np.ascontiguousarray(np.concatenate([cos2, sin2], 0))
    j = np.arange(512)
    rel = 255 - j
    bucket = t5_bucket_np(rel.astype(np.int32))
    onehot = np.zeros((NBUCK, 512), f)
    onehot[bucket[:511], np.arange(511)] = 1.0
    valid = (np.abs(rel) <= 128).astype(f)
    valid[511] = 0.0
    validv = np.ascontiguousarray(np.broadcast_to(valid[None, :], (8, 512))).astype(f)
    shared = dict(
        xp=xp, ropek=ropek, onehot=onehot, validv=validv,
        w_in=np.ascontiguousarray(np.asarray(w_in, f)[0]),
        ln_g=np.ascontiguousarray(np.asarray(ln_g, f)[0].reshape(8, 128).T),
        b_gate=np.ascontiguousarray(np.asarray(b_gate, f)[0].reshape(16, 128).T),
        sink_a=np.ascontiguousarray(np.asarray(sink_a, f)[0]),
        q_norm_g=np.ascontiguousarray(np.asarray(q_norm_g, f)[0].reshape(2, 128).T),
        kv_norm_g=np.ascontiguousarray(np.asarray(kv_norm_g, f)[0].reshape(1, 128).T),
        w_uq=np.ascontiguousarray(np.asarray(w_uq, f)[0]),
        w_ukv=np.ascontiguousarray(np.asarray(w_ukv, f)[0]),
        w_proj_a=np.ascontiguousarray(np.asarray(w_proj_a, f)[0]),
        w_proj_b=np.ascontiguousarray(np.asarray(w_proj_b, f)[0]),
        w_out=np.ascontiguousarray(np.asarray(w_out, f)[0]),
        rel_bias=np.ascontiguousarray(np.asarray(rel_bias, f)),
        final_g=np.ascontiguousarray(np.asarray(final_g, f)),
    )
    in_maps = []
    for c in range(cfg.NC):
        xs = np.ascontiguousarray(x_sample[c * cfg.NS:(c + 1) * cfg.NS].reshape(cfg.NS * cfg.DSEQ, D))
        xq = np.zeros((cfg.NPASS, NBW * 128, D), f)
        flags = np.zeros((cfg.NPASS, 2), f)
        ropeq = np.zeros((cfg.NPASS, 64, QP), f)
        for p, (seg, hf) in enumerate(cfg.passes):
            if seg == 0:
                seq, S = xp, cfg.SEQ
                start = c * cfg.OWNP + hf * QP
            else:
                seq, S = x_sample[c * cfg.NS + seg - 1], cfg.DSEQ
                start = hf * QP
            lo, hi = start - 128, start + QP + 128
            a, b = max(lo, 0), min(hi, S)
            xq[p, a - lo:b - lo] = seq[a:b]
            flags[p, 0] = 1.0 if lo >= 0 else 0.0
            flags[p, 1] = 1.0 if hi <= S else 0.0
            ropeq[p] = ropek[:, start:start + QP]
        m = dict(shared)
        m.update(xs=xs, xq=xq.reshape(cfg.NPASS * NBW * 128, D), flags=flags.reshape(-1), ropeq=ropeq)
        in_maps.append(m)
    return in_maps


def assemble(cfg, results):
    yp = np.zeros((1, cfg.SEQ, D), np.float32)
    ysm = np.zeros((cfg.DB, cfg.DSEQ, D), np.float32)
    for c in range(cfg.NC):
        y = np.asarray(results[c]["y"]).reshape(cfg.NPASS, cfg.QP, D)
        for p, (seg, hf) in enumerate(cfg.passes):
            if seg == 0:
                s0 = c * cfg.OWNP + hf * cfg.QP
                yp[0, s0:s0 + cfg.QP] = y[p]
            else:
                ysm[c * cfg.NS + seg - 1, hf * cfg.QP:(hf + 1) * cfg.QP] = y[p]
    return yp, ysm


def run(cfg, inputs, trace=False):
    nc = build_program(cfg)
    in_maps = host_inputs(cfg, **inputs)
    res = run_bass_kernel_spmd(nc, in_maps, core_ids=list(range(cfg.NC)), **({"trace": True} if trace else {}))
    return assemble(cfg, res.results), res


def kernel(**inputs):
    cfg = Cfg()
    (yp, ysm), _ = run(cfg, inputs)
    return yp, ysm
```

```python
import math
import contextlib
import numpy as np
import concourse.bass as bass
import concourse.mybir as mybir
from concourse.bass_utils import run_bass_kernel_spmd

F32 = mybir.dt.float32
BF16 = mybir.dt.bfloat16
AF = mybir.ActivationFunctionType
ALU = mybir.AluOpType

ENGS = ("pe", "act", "dve", "pool", "sp")
SYNC_SAME = ("act", "dve", "pool")

D = 1024
HA, HKV, DH = 8, 2, 64
HB, QL, KVL, NOPE, ROPE, DV = 8, 256, 128, 64, 32, 64
NBUCK, MAXD = 32, 128
EPS = 1e-6
C_QA, C_KA, C_VA, C_ZA, C_CQ, C_CKV, C_KR, C_ZB, C_GA, C_GB, C_END = 0, 512, 640, 768, 1280, 1536, 1664, 1696, 2208, 3232, 4256


class Cfg:
    def __init__(self, NC=8, SEQ=16384, DB=16, DSEQ=2048, QP=1024, CH=512):
        self.NC, self.SEQ, self.DB, self.DSEQ, self.QP, self.CH = NC, SEQ, DB, DSEQ, QP, CH
        self.OWNP = SEQ // NC
        self.NS = DB // NC
        self.segs = [(SEQ, self.OWNP)] + [(DSEQ, DSEQ)] * self.NS
        self.passes = []
        for s, (T, own) in enumerate(self.segs):
            assert own % QP == 0 and T % CH == 0 and QP % CH == 0
            for hf in range(own // QP):
                self.passes.append((s, hf))
        self.NPASS = len(self.passes)
        self.NBQ = QP // 128
        self.TK = SEQ + self.NS * DSEQ
        self.TMAX = max(SEQ, DSEQ)
        self.kbase = [0] + [SEQ + i * DSEQ for i in range(self.NS)]


class Prog:
    def __init__(self, nc):
        self.nc = nc
        self.ops = []

    def op(self, eng, fn, reads=(), writes=(), dma=None):
        self.ops.append((eng, fn, tuple(reads), tuple(writes), dma))

    def emit(self, final_wait_eng="sp"):
        nc, ops = self.nc, self.ops
        n = len(ops)
        last_w, readers = {}, {}
        deps = [None] * n
        raw = [None] * n
        for i, (eng, fn, rds, wrs, dma) in enumerate(ops):
            d = set()
            rw = set()
            for r in rds:
                j = last_w.get(r)
                if j is not None:
                    d.add(j)
                    rw.add(j)
            raw[i] = rw
            for w in wrs:
                j = last_w.get(w)
                if j is not None:
                    d.add(j)
                rl = readers.get(w)
                if rl:
                    d.update(rl)
            for r in rds:
                readers.setdefault(r, []).append(i)
            for w in wrs:
                last_w[w] = i
                readers[w] = []
            d.discard(i)
            deps[i] = d
        need_sig = [False] * n
        for i in range(n):
            e = ops[i][0]
            for j in deps[i]:
                if ops[j][4] is None and (ops[j][0] != e or e in SYNC_SAME):
                    need_sig[j] = True
        dma_keys = []
        seenk = set()
        for o in ops:
            if o[4] is not None and o[4] not in seenk:
                seenk.add(o[4])
                dma_keys.append(o[4])
        with contextlib.ExitStack() as st:
            eng_sems = {e: st.enter_context(nc.semaphore("s_" + e)) for e in ENGS}
            dma_sems = {k: st.enter_context(nc.semaphore("d_%d" % i)) for i, k in enumerate(dma_keys)}
            cnt = {e: 0 for e in ENGS}
            dcnt = {k: 0 for k in dma_keys}
            sig = [None] * n
            for i, (eng, fn, rds, wrs, dma) in enumerate(ops):
                if dma is not None:
                    dcnt[dma] += 16
                    sig[i] = (dma_sems[dma], dcnt[dma], ("d", dma))
                elif need_sig[i]:
                    cnt[eng] += 1
                    sig[i] = (eng_sems[eng], cnt[eng], ("e", eng))
            per_eng = {e: [] for e in ENGS}
            for i, o in enumerate(ops):
                per_eng[o[0]].append(i)
            block = st.enter_context(nc.Block())

            def make_body(e):
                def body(eng):
                    seen = {}
                    for i in per_eng[e]:
                        _, fn, _, _, dma = ops[i]
                        need = {}
                        for j in deps[i]:
                            s = sig[j]
                            if s is None:
                                continue
                            if ops[j][4] is None and ops[j][0] == e and e not in SYNC_SAME:
                                continue
                            sem, val, key = s
                            if seen.get(key, 0) >= val:
                                continue
                            if key not in need or need[key][1] < val:
                                need[key] = (sem, val)
                        for key, (sem, val) in need.items():
                            eng.wait_ge(sem, val)
                            seen[key] = val
                        ins = fn(eng)
                        if sig[i] is not None:
                            ins.then_inc(sig[i][0], 16 if dma is not None else 1)
                    if e == final_wait_eng:
                        for k in dma_keys:
                            if seen.get(("d", k), 0) < dcnt[k]:
                                eng.wait_ge(dma_sems[k], dcnt[k])
                        for e2 in ENGS:
                            if e2 != e and cnt[e2] > 0 and seen.get(("e", e2), 0) < cnt[e2]:
                                eng.wait_ge(eng_sems[e2], cnt[e2])
                return body

            block.tensor(make_body("pe"))
            block.scalar(make_body("act"))
            block.vector(make_body("dve"))
            block.gpsimd(make_body("pool"))
            block.sync(make_body("sp"))
        return n


def t5_bucket_np(rel):
    nb = NBUCK // 2
    max_exact = nb // 2
    ret = (rel > 0).astype(np.int32) * nb
    n = np.abs(rel)
    nf = np.maximum(n, 1).astype(np.float32)
    large = max_exact + (np.log(nf / max_exact) / math.log(MAXD / max_exact) * (nb - max_exact)).astype(np.int32)
    large = np.minimum(large, nb - 1)
    return ret + np.where(n < max_exact, n, large)


def build_program(cfg):
    nc = bass.Bass("TRN2", target_bir_lowering=False)
    P = Prog(nc)
    CH, QP, NBQ, NPASS, TK, TMAX = cfg.CH, cfg.QP, cfg.NBQ, cfg.NPASS, cfg.TK, cfg.TMAX
    NBW = NBQ + 2
    NGQ = QP // CH
    NBC = CH // 128
    TREG = max(cfg.SEQ, cfg.DSEQ)

    def din(name, shape, dt=F32):
        return nc.dram_tensor(name, list(shape), dt, kind="ExternalInput").ap()

    xp = din("xp", [cfg.SEQ, D])
    xs = din("xs", [cfg.NS * cfg.DSEQ, D])
    xq = din("xq", [NPASS * NBW * 128, D])
    flags = din("flags", [NPASS * 2])
    ropek = din("ropek", [64, TMAX])
    ropeq = din("ropeq", [NPASS, 64, QP])
    w_in = din("w_in", [D, C_END])
    ln_g = din("ln_g", [128, 8])
    b_gate = din("b_gate", [128, 16])
    sink_a = din("sink_a", [8])
    q_norm_g = din("q_norm_g", [128, 2])
    kv_norm_g = din("kv_norm_g", [128, 1])
    w_uq = din("w_uq", [QL, HB * 96])
    w_ukv = din("w_ukv", [KVL, HB * 128])
    w_proj_a = din("w_proj_a", [512, D])
    w_proj_b = din("w_proj_b", [512, D])
    w_out = din("w_out", [D, D])
    rel_bias = din("rel_bias", [NBUCK, HA])
    final_g = din("final_g", [D])
    onehot = din("onehot", [NBUCK, 512])
    validv = din("validv", [8, 512])
    yout = nc.dram_tensor("y", [NPASS * QP, D], F32, kind="ExternalOutput").ap()

    NWD = 24576 + 4096 * 3 + 8192
    wd_img = nc.dram_tensor("wd_img", [128, NWD], BF16, kind="Internal").ap()
    kvc = nc.dram_tensor("kvc", [160, TK], BF16, kind="Internal").ap()
    evec_t = nc.dram_tensor("evec", [8, 512], F32, kind="Internal")
    etoe_t = nc.dram_tensor("etoe", [8 * 128 * 512], F32, kind="Internal")

    sb = nc.alloc_sbuf_tensor
    O_CN, O_KT, O_V = 0, TREG, 2 * TREG
    NV = (TREG // 128) * 96
    O_QT = O_V + NV
    NREG = max(O_QT + 2 * QP, NWD)
    RG = sb("RG", [128, NREG], BF16)
    cnT = RG[:, O_CN:O_CN + TREG]
    KT = RG[:, O_KT:O_KT + TREG]
    Vb = RG[:, O_V:O_V + NV].rearrange("p (t c) -> p t c", c=96)
    QTs = [RG[:, O_QT + i * QP:O_QT + (i + 1) * QP] for i in range(2)]
    WQA = RG[:, 0:4096].rearrange("p (k c) -> p k c", k=8)
    WG = RG[:, 4096:28672].rearrange("p (k c) -> p k c", k=8)
    WPA = RG[:, 28672:32768].rearrange("p (k c) -> p k c", k=4)
    WPB = RG[:, 32768:36864].rearrange("p (k c) -> p k c", k=4)
    WO = RG[:, 36864:45056].rearrange("p (k c) -> p k c", k=8)

    hTo = sb("hTo", [128, 8 * QP], BF16)
    hTo3 = hTo[:, :].rearrange("p (k c) -> p k c", k=8)
    HR = sb("HR", [128, max(2048, 2 * QP)], BF16)
    hTh = [HR[:, i * 1024:(i + 1) * 1024] for i in range(2)]
    cqnT = sb("cqnT", [128, 2 * QP], BF16)
    cqn3 = cqnT[:, :].rearrange("p (k c) -> p k c", k=2)
    ybT = sb("ybT", [128, 4 * QP], BF16)
    yb3 = ybT[:, :].rearrange("p (k c) -> p k c", k=4)
    kaT = sb("kaT", [128, NBW * 128], BF16)
    vaw = sb("vaw", [128, NBW * 2 * 128], BF16)
    vaw4 = vaw[:, :].rearrange("p (b g c) -> p b g c", b=NBW, g=2)
    WA = sb("WA", [128, 8 * 192], BF16)
    WA3 = WA[:, :].rearrange("p (k c) -> p k c", k=8)
    WB = sb("WB", [128, 8 * 512], BF16)
    WB3 = WB[:, :].rearrange("p (k c) -> p k c", k=8)
    WUQ = sb("WUQ", [128, 2 * 8 * 128], BF16)
    WUQ4 = WUQ[:, :].rearrange("p (k h c) -> p k h c", k=2, h=8)
    WUKV = sb("WUKV", [128, 8 * 128], BF16)
    WUKV3 = WUKV[:, :].rearrange("p (h c) -> p h c", h=8)
    gcol = sb("gcol", [128, 8], F32)
    bg = sb("bg", [128, 16], F32)
    gq = sb("gq", [128, 2], F32)
    gkv = sb("gkv", [128, 1], F32)
    fgB = sb("fgB", [128, D], F32)
    epsT = sb("epsT", [128, 1], F32)
    ident = sb("ident", [128, 128], BF16)
    onesf = sb("onesf", [128, 128], F32)
    esk = sb("esk", [128, 8], F32)
    esinkT = sb("esinkT", [128, 2 * 512], BF16)
    esink3 = esinkT[:, :].rearrange("p (g c) -> p g c", g=2)
    Etab = sb("Etab", [128, 6 * 512], BF16)
    Etab4 = Etab[:, :].rearrange("p (g t c) -> p g t c", g=2, t=3)
    flg = sb("flg", [128, NPASS * 2], F32)
    ropq = HR[:, 0:2 * QP].bitcast(F32)
    NXT = 4
    xt = [sb("xt%d" % i, [128, D], F32) for i in range(NXT)]
    xsb = [sb("xsb%d" % i, [128, D], BF16) for i in range(2)]
    ss = [sb("ss%d" % i, [128, 4], F32) for i in range(NXT)]
    NF = 5
    f32t = [sb("f32t%d" % i, [128, 512], F32) for i in range(NF)]
    b16t = [sb("b16t%d" % i, [128, 512], BF16) for i in range(4)]
    QY = sb("QY", [128, 4096], BF16)
    qaT3 = QY[:, 0:2048].rearrange("p (j c) -> p j c", j=4)
    yaT3 = QY[:, 2048:4096].rearrange("p (j c) -> p j c", j=4)
    rpf = QY[:, 0:2048].bitcast(F32)
    rpk = [rpf[0:64, i * 512:(i + 1) * 512] for i in range(2)]
    cng = [QY[:, 2048 + i * 512:2048 + (i + 1) * 512] for i in range(2)]
    krg = [QY[0:32, 3072 + i * 512:3072 + (i + 1) * 512] for i in range(2)]
    mTb = sb("mTb", [128, 4 * 512], BF16)
    mTl = [QY[:, m * 512:(m + 1) * 512] for m in range(4)] + [mTb[:, m * 512:(m + 1) * 512] for m in range(4)]
    fence_t = sb("fence_t", [128, 2], F32)
    pb = [nc.alloc_psum_tensor("pb%d" % i, [128, 512], F32) for i in range(8)]
    print("sbuf bytes remaining:", nc.sbuf_bytes_remaining)

    st = {"bank": 0, "nb": 7, "f": 0, "b": 0, "xt": 0}
    NWARM = 2

    live_banks = set()

    def bank(hold=False):
        for _ in range(16):
            i = st["bank"] % st["nb"]
            st["bank"] += 1
            if "pb%d" % i not in live_banks:
                break
        else:
            raise RuntimeError("no free PSUM bank")
        if hold:
            live_banks.add("pb%d" % i)
        return pb[i], "pb%d" % i

    def release(k):
        live_banks.discard(k)

    def ftmp():
        i = st["f"] % NF
        st["f"] += 1
        return f32t[i], "f32t%d" % i

    def btmp():
        i = st["b"] % 4
        st["b"] += 1
        return b16t[i], "b16t%d" % i

    def I(eng, name, *args, reads=(), writes=(), dma=None, **kw):
        P.op(eng, lambda e: getattr(e, name)(*args, **kw), reads, writes, dma)

    def DMA(eng, out, in_, reads=(), writes=(), key=None):
        I(eng, "dma_start", out=out, in_=in_, reads=reads, writes=writes, dma=key)

    def MM(out, lhsT, rhs, start, stop, reads, writes):
        I("pe", "matmul", out, lhsT=lhsT, rhs=rhs, start=start, stop=stop, reads=reads, writes=writes)

    def ACT(out, in_, func, reads, writes, **kw):
        I("act", "activation", out=out, in_=in_, func=func, reads=reads, writes=writes, **kw)

    def TT(eng, out, in0, in1, op_, reads, writes):
        I(eng, "tensor_tensor", out=out, in0=in0, in1=in1, op=op_, reads=reads, writes=writes)

    def TS(eng, out, in0, s1, s2, op0, op1, reads, writes):
        if s2 is None:
            I(eng, "tensor_scalar", out=out, in0=in0, scalar1=s1, scalar2=None, op0=op0, reads=reads, writes=writes)
        else:
            I(eng, "tensor_scalar", out=out, in0=in0, scalar1=s1, scalar2=s2, op0=op0, op1=op1, reads=reads, writes=writes)

    def STT(eng, out, in0, scalar, in1, reads, writes):
        I(eng, "scalar_tensor_tensor", out=out, in0=in0, scalar=scalar, in1=in1, op0=ALU.mult, op1=ALU.mult, reads=reads, writes=writes)

    def CP(eng, out, in_, reads, writes):
        I(eng, "tensor_copy", out=out, in_=in_, reads=reads, writes=writes)

    def RCP(out, in_, reads, writes):
        I("dve", "reciprocal", out=out, in_=in_, reads=reads, writes=writes)

    I("pool", "memset", epsT[:], EPS, writes=["epsT"])
    I("pool", "memset", onesf[:], 1.0, writes=["onesf"])
    idf, idk = f32t[0], "f32t0"
    I("pool", "memset", idf[:, 0:128], 1.0, writes=[idk])
    I("pool", "affine_select", out=idf[:, 0:128], in_=idf[:, 0:128], pattern=[[-1, 128]], compare_op=ALU.is_equal,
      fill=0.0, base=0, channel_multiplier=1, reads=[idk], writes=[idk])
    CP("dve", ident[:], idf[:, 0:128], [idk], ["ident"])
    I("pool", "memset", vaw[:], 1.0, writes=["vaw_init"])
    for t_, d_, k_ in [(gcol, ln_g, "gcol"), (bg, b_gate, "bg"), (gq, q_norm_g, "gq"), (gkv, kv_norm_g, "gkv")]:
        DMA("sp", t_[:], d_[:, :], writes=[k_], key="ld_" + k_)
    DMA("sp", fgB[:], bass.AP(final_g.tensor, 0, [[0, 128], [1, D]]), writes=["fgB"], key="ld_fgB")
    DMA("sp", esk[:], bass.AP(sink_a.tensor, 0, [[0, 128], [1, 8]]), writes=["esk"], key="ld_esk")
    DMA("sp", flg[:], bass.AP(flags.tensor, 0, [[0, 128], [1, NPASS * 2]]), writes=["flg"], key="ld_flg")
    ACT(esk[:], esk[:], AF.Exp, ["esk"], ["esk"])
    for h in range(8):
        g, i = h // 4, h % 4
        CP("dve", esink3[:, g, i * 128:(i + 1) * 128], esk[:, h:h + 1].to_broadcast([128, 128]), ["esk"], ["esinkT"])

    rbt, oht, vvt, ev = f32t[1], f32t[2], f32t[3], f32t[4]
    DMA("sp", rbt[0:NBUCK, 0:8], rel_bias[:, :], writes=["f32t1"], key="ld_rb")
    DMA("sp", oht[0:NBUCK, :], onehot[:, :], writes=["f32t2"], key="ld_oh")
    DMA("sp", vvt[0:8, :], validv[:, :], writes=["f32t3"], key="ld_vv")
    MM(pb[0][0:8, :], rbt[0:NBUCK, 0:8], oht[0:NBUCK, :], True, True, ["f32t1", "f32t2"], ["pb0"])
    ACT(ev[0:8, :], pb[0][0:8, :], AF.Exp, ["pb0"], ["f32t4"])
    TT("dve", ev[0:8, :], ev[0:8, :], vvt[0:8, :], ALU.mult, ["f32t4", "f32t3"], ["f32t4"])
    DMA("sp", evec_t.ap()[:, :], ev[0:8, :], reads=["f32t4"], writes=["evec"], key="st_evec")
    DMA("sp", bass.AP(etoe_t, 0, [[128 * 512, 8], [512, 128], [1, 511]]), bass.AP(evec_t, 0, [[512, 8], [0, 128], [1, 511]]),
        reads=["evec"], writes=["etoe"], key="st_etoe")
    CT = [383, 255, 127]
    st["f"] = 0
    for g in range(2):
        for t in range(3):
            ft, fk = ftmp()
            DMA("sp", ft[:, :].rearrange("p (h c) -> p h c", h=4),
                bass.AP(etoe_t, 4 * g * 128 * 512 + CT[t], [[511, 128], [128 * 512, 4], [1, 128]]),
                reads=["etoe"], writes=[fk], key="ld_E%d%d" % (g, t))
            CP("dve", Etab4[:, g, t, :], ft[:, :], [fk], ["Etab"])

    def stage(src_ap, ncols):
        i = st["xt"] % NXT
        st["xt"] += 1
        DMA("sp", xt[i][:, 0:ncols], src_ap, writes=["xt%d" % i], key="ld_xt%d" % i)
        return xt[i], ["xt%d" % i, "gcol", "RG"]

    for k in range(8):
        rows = slice(k * 128, (k + 1) * 128)
        gk = gcol[:, k:k + 1]
        x_, rd = stage(w_in[rows, 0:768], 768)
        for hf in range(2):
            TS("dve", WQA[:, k, :].rearrange("p (j a c) -> p j a c", j=4, a=2)[:, :, hf, :],
               x_[:, hf * 256:(hf + 1) * 256].rearrange("p (j c) -> p j c", j=4), gk, None, ALU.mult, None, rd, ["WQA"])
        TS("dve", WB3[:, k, 256:512], x_[:, 512:768], gk, None, ALU.mult, None, rd, ["WB"])
        x_, rd = stage(w_in[rows, 768:1696], 928)
        TS("dve", WG[:, k, 0:512], x_[:, 0:512], gk, None, ALU.mult, None, rd, ["WG"])
        TS("dve", WB3[:, k, 0:256], x_[:, 512:768], gk, None, ALU.mult, None, rd, ["WB"])
        TS("dve", WA3[:, k, 0:160], x_[:, 768:928], gk, None, ALU.mult, None, rd, ["WA"])
        TS("dve", WA3[:, k, 176:192], x_[:, 896:912], gk, None, ALU.mult, None, rd, ["WA"])
        TS("dve", WA3[:, k, 160:176], x_[:, 912:928], gk, -1.0, ALU.mult, ALU.mult, rd, ["WA"])
        x_, rd = stage(w_in[rows, 1696:2720], 1024)
        TS("dve", WG[:, k, 512:1536], x_[:, 0:1024], gk, None, ALU.mult, None, rd, ["WG"])
        x_, rd = stage(w_in[rows, 2720:3744], 1024)
        TS("dve", WG[:, k, 1536:2560], x_[:, 0:1024], gk, None, ALU.mult, None, rd, ["WG"])
        x_, rd = stage(w_in[rows, 3744:4256], 512)
        TS("dve", WG[:, k, 2560:3072], x_[:, 0:512], gk, None, ALU.mult, None, rd, ["WG"])
    for k in range(2):
        x_, rd = stage(w_uq[k * 128:(k + 1) * 128, :], 768)
        x3 = x_[:, 0:768].rearrange("p (h c) -> p h c", h=8)
        CP("dve", WUQ4[:, k, :, 0:96], x3, rd, ["WUQ"])
        TS("dve", WUQ4[:, k, :, 96:112], x3[:, :, 80:96], -1.0, None, ALU.mult, None, rd, ["WUQ"])
        CP("dve", WUQ4[:, k, :, 112:128], x3[:, :, 64:80], rd, ["WUQ"])
    x_, rd = stage(w_ukv[:, :], 1024)
    CP("dve", WUKV[:, :], x_[:, :], rd, ["WUKV"])
    for k in range(4):
        x_, rd = stage(w_proj_a[k * 128:(k + 1) * 128, :], 1024)
        CP("dve", WPA[:, k, :], x_[:, :], rd, ["WPA"])
        x_, rd = stage(w_proj_b[k * 128:(k + 1) * 128, :], 1024)
        CP("dve", WPB[:, k, :], x_[:, :], rd, ["WPB"])
    for k in range(8):
        x_, rd = stage(w_out[k * 128:(k + 1) * 128, :], 1024)
        CP("dve", WO[:, k, :], x_[:, :], rd, ["WO"])
    wpieces = [(0, 4096), (4096, 16384), (16384, 28672), (28672, 36864), (36864, NWD)]
    NWP = len(wpieces)
    EARLY = [i for i in (0, 1) if wpieces[i][1] <= TREG]
    for i, (a, b) in enumerate(wpieces):
        DMA("sp", wd_img[:, a:b], RG[:, a:b], reads=["WG", "WQA", "WPA", "WPB", "WO", "RG"], writes=["wd_img%d" % i], key="st_wd%d" % i)

    dbg = getattr(cfg, "debug", False)

    def dump(name, ap, shape, dt, reads):
        if not dbg:
            return
        o = nc.dram_tensor(name, list(shape), dt, kind="ExternalOutput").ap()
        DMA("sp", o, ap, reads=reads, key="dbg_" + name)

    def fence():
        I("pool", "memset", fence_t[:], 0.0, writes=["RG", "fence_t"])

    def pipeline(tasks):
        active = []
        tasks = list(tasks)
        while tasks or active:
            if tasks:
                active.append(tasks.pop(0))
            nxt = []
            for g_ in active:
                try:
                    next(g_)
                    nxt.append(g_)
                except StopIteration:
                    pass
            active = nxt

    def load_norm_transpose(src_ap, dst_view, dst_keys, extra_reads=(), tail=None):
        i = st["xt"] % NXT
        st["xt"] += 1
        xtile, xk, sst, sk = xt[i], "xt%d" % i, ss[i], "ss%d" % i
        j = st["xt"] % 2
        xb, xbk = xsb[j], "xsb%d" % j
        DMA("sp", xtile[:, :], src_ap, writes=[xk], key="ld_" + xk)
        ACT(xb[:, :], xtile[:, :], AF.Square, [xk], [xbk, sk], accum_out=sst[:, 0:1])
        ACT(sst[:, 1:2], sst[:, 0:1], AF.Ln, [sk, "epsT"], [sk], scale=1.0 / D, bias=epsT[:, 0:1])
        ACT(sst[:, 2:3], sst[:, 1:2], AF.Exp, [sk], [sk], scale=-0.5)
        TS("dve", xb[:, :], xtile[:, :], sst[:, 2:3], None, ALU.mult, None, [xk, sk], [xbk])
        yield
        bk, bkk = bank(hold=True)
        bkb = bk[:, :].bitcast(BF16)
        for k in range(8):
            I("pe", "transpose", out=bkb[:, k * 128:(k + 1) * 128], in_=xb[:, k * 128:(k + 1) * 128], identity=ident[:],
              reads=[xbk, "ident"], writes=[bkk])
        for _ in range(NWARM):
            MM(pb[7][:, 0:512], ident[:], WB[:, 0:512], True, True, ["ident", "WB"], ["pb7"])
        yield
        CP("dve", dst_view, bkb[:, :].rearrange("p (k c) -> p k c", k=8), [bkk] + list(extra_reads), dst_keys)
        release(bkk)
        if tail is not None:
            yield
            r_ = tail()
            if r_ is not None:
                yield from r_

    if QP >= 2 * CH:
        hA = [hTo[:, i * 8 * CH:(i + 1) * 8 * CH].rearrange("p (k c) -> p k c", k=8) for i in range(2)]
    else:
        hAt = [sb("hA%d" % i, [128, 8 * CH], BF16) for i in range(2)]
        hA = [t_[:, :].rearrange("p (k c) -> p k c", k=8) for t_ in hAt]
    _ob = onesf[:, :].bitcast(BF16)
    onesb = bass.AP(_ob.tensor, _ob.offset + 1, [[_ob.ap[0][0], 128], [2, 128]])

    def phaseA_group_tail(grp, slot, tok0, pos0, hks):
        b1, b1k = bank(hold=True)
        for k in range(8):
            MM(b1[:, 0:CH], WA3[:, k, 0:128], hA[slot][:, k, :], k == 0, k == 7, hks + ["WA", "RG"], [b1k])
        b2, b2k = bank(hold=True)
        for k in range(8):
            MM(b2[0:64, 0:CH], WA3[:, k, 128:192], hA[slot][:, k, :], k == 0, k == 7, hks + ["WA", "RG"], [b2k])
        rp, rpkk = rpk[slot], "rpk%d" % slot
        DMA("sp", rp[:, 0:CH], ropek[:, pos0:pos0 + CH], reads=["RG"], writes=[rpkk], key="ld_" + rpkk)
        yield
        sq, sqk = btmp()
        ACT(sq[:, 0:CH], b1[:, 0:CH], AF.Square, [b1k], [sqk])
        b3, b3k = bank(hold=True)
        MM(b3[:, 0:CH], onesb, sq[:, 0:CH], True, True, [sqk, "onesf"], [b3k])
        ta, tak = ftmp()
        tb, tbk = ftmp()
        TT("dve", ta[0:32, 0:CH], b2[0:32, 0:CH], rp[0:32, 0:CH], ALU.mult, [b2k, rpkk, "RG"], [tak])
        TT("dve", tb[0:32, 0:CH], b2[32:64, 0:CH], rp[32:64, 0:CH], ALU.mult, [b2k, rpkk, "RG"], [tbk])
        release(b2k)
        yield
        rc, rck = ftmp()
        ACT(rc[:, 0:CH], b3[:, 0:CH], AF.Ln, [b3k, "epsT"], [rck], scale=1.0 / KVL, bias=epsT[:, 0:1])
        release(b3k)
        ACT(rc[:, 0:CH], rc[:, 0:CH], AF.Exp, [rck], [rck], scale=-0.5)
        kg, kgk = krg[slot], "krg%d" % slot
        TT("dve", kg[:, 0:CH], ta[0:32, 0:CH], tb[0:32, 0:CH], ALU.add, [tak, tbk, "RG"], [kgk])
        DMA("pool", kvc[128:160, tok0:tok0 + CH], kg[:, 0:CH], reads=[kgk, "RG"], writes=["kvck%d" % grp], key="st_" + kgk)
        yield
        cg, cgk = cng[slot], "cng%d" % slot
        STT("dve", cg[:, 0:CH], b1[:, 0:CH], gkv[:, 0:1], rc[:, 0:CH], [b1k, rck, "gkv", "RG"], [cgk])
        release(b1k)
        DMA("pool", kvc[0:128, tok0:tok0 + CH], cg[:, 0:CH], reads=[cgk, "RG"], writes=["kvcc%d" % grp], key="st_" + cgk)

    tasksA = []
    for grp in range(TK // CH):
        slot = grp % 2
        tok0 = grp * CH
        if tok0 < cfg.SEQ:
            src, r0, pos0 = xp, tok0, tok0
        else:
            src, r0 = xs, tok0 - cfg.SEQ
            pos0 = r0 % cfg.DSEQ
        hks = ["hA%d_%d" % (slot, tl) for tl in range(NBC)]
        for tl in range(NBC):
            tail = None
            if tl == NBC - 1:
                tail = (lambda grp=grp, slot=slot, tok0=tok0, pos0=pos0, hks=hks: phaseA_group_tail(grp, slot, tok0, pos0, hks))
            tasksA.append(load_norm_transpose(src[r0 + tl * 128: r0 + (tl + 1) * 128, :], hA[slot][:, :, tl * 128:(tl + 1) * 128],
                                              [hks[tl]], ["RG"], tail))
    pipeline(tasksA)

    dump("d_kvc", kvc[:, :], [160, TK], BF16, ["kvcc%d" % g_ for g_ in range(TK // CH)] + ["kvck%d" % g_ for g_ in range(TK // CH)])
    dump("d_E", Etab[:, :], [128, 6 * 512], BF16, ["Etab"])
    SC_A = DH ** -0.5
    SC_B = (NOPE + ROPE) ** -0.5
    ocnt = 0
    B_done = set()
    for p, (seg, hf) in enumerate(cfg.passes):
        T, own = cfg.segs[seg]
        kb = cfg.kbase[seg]
        NJ = T // CH
        NT = T // 128
        st["nb"] = 7
        fence()
        NLD = min(4, NJ)
        for i in range(NLD):
            j0, j1 = i * NJ // NLD, (i + 1) * NJ // NLD
            a, b = j0 * CH, j1 * CH
            kvr = ["kvcc%d" % ((kb + jj * CH) // CH) for jj in range(j0, j1)] + ["kvck%d" % ((kb + jj * CH) // CH) for jj in range(j0, j1)]
            DMA("pool", cnT[:, a:b], kvc[0:128, kb + a:kb + b], reads=["RG"] + kvr, writes=["cn%d" % j for j in range(j0, j1)], key="ld_cn%d" % i)
            DMA("pool", KT[64:96, a:b], kvc[128:160, kb + a:kb + b], reads=["RG"] + kvr, writes=["KTr%d" % j for j in range(j0, j1)], key="ld_kr%d" % i)
        I("pool", "memset", Vb[:, 0:NT, 64:96], 1.0, reads=["RG"], writes=["Vones"])
        def phaseB_tail(blk, hv, hkeys):
            bka, bkak = bank(hold=True)
            for k in range(8):
                MM(bka[:, 0:128], WB3[:, k, 256:384], hv[:, k, :], k == 0, k == 7, hkeys + ["WB"], [bkak])
            bkv, bkvk = bank(hold=True)
            for k in range(8):
                MM(bkv[:, 0:128], hv[:, k, :], WB3[:, k, 384:512], k == 0, k == 7, hkeys + ["WB"], [bkvk])
            yield
            CP("dve", kaT[:, blk * 128:(blk + 1) * 128], bka[:, 0:128], [bkak], ["kaT%d" % blk])
            CP("dve", vaw4[:, blk, :, 0:64], bkv[:, 0:128].rearrange("p (g c) -> p g c", g=2), [bkvk, "vaw_init"], ["vaw%d" % blk])
            release(bkak)
            release(bkvk)

        def make_B_tasks(pp):
          tasksB = []
          for blk in range(NBW):
            row0 = (pp * NBW + blk) * 128
            if blk == 0 or blk == NBW - 1:
                hi = 0 if blk == 0 else 1
                hv = hTh[hi].rearrange("p (k c) -> p k c", k=8)
                hkeys = ["hTh%d" % hi, "ropq"]
            else:
                hv = hTo3[:, :, (blk - 1) * 128: blk * 128]
                hkeys = ["hTo%d" % (blk - 1)]
            tasksB.append(load_norm_transpose(xq[row0:row0 + 128, :], hv, hkeys, ["RG"] if pp == 0 else [],
                                              (lambda blk=blk, hv=hv, hkeys=hkeys: phaseB_tail(blk, hv, hkeys))))
          return tasksB

        def emit_B_cq():
          for grp in range(NGQ):
              cols = slice(grp * CH, (grp + 1) * CH)
              hk = ["hTo%d" % b for b in range(grp * NBC, (grp + 1) * NBC)]
              bq = []
              for m in range(2):
                  bk, bkk = bank()
                  bq.append((bk, bkk))
                  for k in range(8):
                      MM(bk[:, 0:CH], WB3[:, k, m * 128:(m + 1) * 128], hTo3[:, k, cols], k == 0, k == 7, hk + ["WB"], [bkk])
              b3, b3k = bank()
              for m in range(2):
                  sq, sqk = btmp()
                  ACT(sq[:, 0:CH], bq[m][0][:, 0:CH], AF.Square, [bq[m][1]], [sqk])
                  MM(b3[:, 0:CH], onesb, sq[:, 0:CH], m == 0, m == 1, [sqk, "onesf"], [b3k])
              rc, rck = ftmp()
              ACT(rc[:, 0:CH], b3[:, 0:CH], AF.Ln, [b3k, "epsT"], [rck], scale=1.0 / QL, bias=epsT[:, 0:1])
              ACT(rc[:, 0:CH], rc[:, 0:CH], AF.Exp, [rck], [rck], scale=-0.5)
              for m in range(2):
                  STT("dve", cqn3[:, m, cols], bq[m][0][:, 0:CH], gq[:, m:m + 1], rc[:, 0:CH], [bq[m][1], rck, "gq"], ["cqn%d_%d" % (grp, m)])

        if p not in B_done:
            pipeline(make_B_tasks(p))
            emit_B_cq()
            B_done.add(p)

        if p == 0:
            dump("d_hTo", hTo[:, :], [128, 8 * QP], BF16, ["hTo%d" % b_ for b_ in range(NBQ)])
            dump("d_cqn", cqnT[:, :], [128, 2 * QP], BF16, ["cqn%d_%d" % (g_, m_) for g_ in range(NGQ) for m_ in range(2)])
            dump("d_ka", kaT[:, :], [128, NBW * 128], BF16, ["kaT%d" % b_ for b_ in range(NBW)])
            dump("d_va", vaw[:, :], [128, NBW * 2 * 128], BF16, ["vaw%d" % b_ for b_ in range(NBW)])
        DMA("sp", ropq[64:128, :], ropeq[p, :, :], writes=["ropq", "hTh0", "hTh1"], key="ld_ropq")
        st["nb"] = 2
        st["bank"] = 0
        ocount = 0
        def expK(h, j):
            bk, bkk = bank()
            MM(bk[0:64, 0:CH], WUKV3[:, h, 0:64], cnT[:, j * CH:(j + 1) * CH], True, True, ["cn%d" % j, "WUKV", "RG"], [bkk])
            CP("dve", KT[0:64, j * CH:(j + 1) * CH], bk[0:64, 0:CH], [bkk, "RG"], ["KTn%d" % j])

        def expV(h, t0):
            nt = min(8, NT - t0)
            bk, bkk = bank()
            for i in range(nt):
                MM(bk[:, i * 64:(i + 1) * 64], cnT[:, (t0 + i) * 128:(t0 + i + 1) * 128], WUKV3[:, h, 64:128], True, True,
                   ["cn%d" % ((t0 + i) * 128 // CH), "WUKV", "RG"], [bkk])
            CP("dve", Vb[:, t0:t0 + nt, 0:64], bk[:, 0:nt * 64].rearrange("p (t c) -> p t c", c=64), [bkk, "RG", "Vones"], ["V%d" % (t0 // 8)])

        def expQ(h, grp):
            QT = QTs[h % 2]
            qk_ = "QT%d" % (h % 2)
            cols = slice(grp * CH, (grp + 1) * CH)
            bk, bkk = bank()
            for k in range(2):
                MM(bk[:, 0:CH], WUQ4[:, k, h, :], cqn3[:, k, cols], k == 0, k == 1, ["cqn%d_%d" % (grp, k), "WUQ"], [bkk])
            CP("dve", QT[0:64, cols], bk[0:64, 0:CH], [bkk, "RG"], [qk_ + "n%d" % grp])
            ta, tak = ftmp()
            tb, tbk = ftmp()
            TT("dve", ta[64:96, 0:CH], bk[64:96, 0:CH], ropq[64:96, cols], ALU.mult, [bkk, "ropq"], [tak])
            TT("dve", tb[64:96, 0:CH], bk[96:128, 0:CH], ropq[96:128, cols], ALU.mult, [bkk, "ropq"], [tbk])
            TT("dve", QT[64:96, cols], ta[64:96, 0:CH], tb[64:96, 0:CH], ALU.add, [tak, tbk, "RG"], [qk_ + "r%d" % grp])

        for grp in range(NGQ):
            expQ(0, grp)
        for j in range(NJ):
            expK(0, j)
        for t0 in range(0, NT, 8):
            expV(0, t0)
        units = [(h, qc, t) for h in range(HB) for qc in range(NGQ) for t in range(NT)]
        LOOK = min(3, NT * NGQ)

        def qk(u):
            h, qc, t = units[u]
            QT = QTs[h % 2]
            qk_ = "QT%d" % (h % 2)
            s_, sk_ = pb[4 + (u % 4)], "pb%d" % (4 + u % 4)
            j = t * 128 // CH
            MM(s_[:, 0:CH], KT[0:96, t * 128:(t + 1) * 128], QT[0:96, qc * CH:(qc + 1) * CH], True, True,
               ["KTn%d" % j, "KTr%d" % j, qk_ + "n%d" % qc, qk_ + "r%d" % qc, "RG"], [sk_])

        for u in range(min(LOOK, len(units))):
            qk(u)
        O, Ok = None, None
        pendK = []
        deferred = []
        for u, (h, qc, t) in enumerate(units):
            if t == 0:
                O, Ok = pb[2 + ocount % 2], "pb%d" % (2 + ocount % 2)
                ocount += 1
            if h == HB - 1 and qc == 0 and t == 0:
                for i in EARLY:
                    a, b = wpieces[i]
                    DMA("sp", RG[:, a:b], wd_img[:, a:b], reads=["wd_img%d" % i],
                        writes=["WD%d" % i] + ["cn%d" % j for j in range(a // CH, (b + CH - 1) // CH)], key="ld_wd%d" % i)
            s_, sk_ = pb[4 + (u % 4)], "pb%d" % (4 + u % 4)
            pt, ptk = btmp()
            ACT(pt[:, 0:CH], s_[:, 0:CH], AF.Exp, [sk_], [ptk], scale=SC_B)
            while deferred and deferred[0][0] <= u:
                deferred.pop(0)[1]()
            last_sweep = (qc == NGQ - 1) and (h + 1 < HB)
            if u + LOOK < len(units):
                h2, qc2, t2 = units[u + LOOK]
                if h2 == h or (qc == NGQ - 1 and (t2 // NBC + 1) * NBC - 1 < t):
                    qk(u + LOOK)
                else:
                    pendK.append(u + LOOK)
            MM(O[0:96, 0:CH], Vb[:, t, :], pt[:, 0:CH], t == 0, t == NT - 1, ["V%d" % (t // 8), "Vones", ptk, "RG"], [Ok])
            if qc == 0 and t == 0 and h + 1 < HB:
                for grp in range(NGQ):
                    expQ(h + 1, grp)
            if last_sweep:
                if (t + 1) % NBC == 0:
                    expK(h + 1, t // NBC)
                if (t + 1) % 8 == 0 or t == NT - 1:
                    expV(h + 1, (t // 8) * 8)
            if t == NT - 1:
                def fin(O=O, Ok=Ok, h=h, qc=qc):
                    r, rk = ftmp()
                    ACT(r[64:96, 0:CH], O[64:96, 0:CH], AF.Ln, [Ok], [rk + "l"])
                    ACT(r[0:32, 0:CH], r[64:96, 0:CH], AF.Exp, [rk + "l"], [rk], scale=-1.0)
                    r0_ = (h % 2) * 64
                    for hh_ in range(2):
                        TT("dve", yb3[r0_ + 32 * hh_:r0_ + 32 * hh_ + 32, h // 2, qc * CH:(qc + 1) * CH],
                           O[32 * hh_:32 * hh_ + 32, 0:CH], r[0:32, 0:CH], ALU.mult, [Ok, rk], ["yb%d_%d_%d" % (h // 2, qc, h % 2)])
                deferred.append((u + 2, fin))
                if qc == NGQ - 1:
                    for u2 in pendK:
                        qk(u2)
                    pendK = []

        while deferred:
            deferred.pop(0)[1]()
        if p == 0:
            dump("d_yb", ybT[:, :], [128, 4 * QP], BF16, ["yb%d_%d_%d" % (j_, c_, h_) for j_ in range(4) for c_ in range(NGQ) for h_ in range(2)])
        st["nb"] = 8
        fence()
        for i, (a, b) in enumerate(wpieces):
            if i not in EARLY:
                DMA("sp" if i % 2 == 0 else "pool", RG[:, a:b], wd_img[:, a:b], reads=["RG", "wd_img%d" % i], writes=["WD%d" % i], key="ld_wd%d" % i)
        K_QA, K_G, K_P, K_O = ["WD0", "RG"], ["WD1", "WD2", "RG"], ["WD3", "RG"], ["WD4", "RG"]
        for ch in range(NGQ):
            cols = slice(ch * CH, (ch + 1) * CH)
            hk = ["hTo%d" % b for b in range(ch * NBC, (ch + 1) * NBC)]
            for j in range(4):
                bk, bkk = bank()
                for k in range(8):
                    MM(bk[:, 0:CH], WQA[:, k, j * 128:(j + 1) * 128], hTo3[:, k, cols], k == 0, k == 7, hk + K_QA, [bkk])
                ACT(qaT3[:, j, 0:CH], bk[:, 0:CH], AF.Copy, [bkk, "RG"], ["qaT", "mT0", "mT1", "mT2", "mT3"])
            def win_task(b, g, ch=ch, p=p):
                gbk = ch * NBC + b
                rows = slice(g * 64, g * 64 + 64)
                sl = []
                for t in range(3):
                    kblk = gbk + t
                    s_, sk_ = bank()
                    sl.append((s_, sk_))
                    MM(s_[:, 0:512], kaT[rows, kblk * 128:(kblk + 1) * 128], qaT3[rows, :, b * 128:(b + 1) * 128], True, True,
                       ["kaT%d" % kblk, "qaT"], [sk_])
                yield
                pl = []
                for t in range(3):
                    s_, sk_ = sl[t]
                    pf, pfk = ftmp()
                    ACT(pf[:, :], s_[:, 0:512], AF.Exp, [sk_], [pfk], scale=SC_A)
                    if (gbk == 0 and t == 0) or (gbk == NBQ - 1 and t == 2):
                        fcol = 2 * p if t == 0 else 2 * p + 1
                        Ef, Ek = ftmp()
                        Efv = Ef[:, :].bitcast(BF16)[:, 0:512]
                        TS("dve", Efv, Etab4[:, g, t, :], flg[:, fcol:fcol + 1], None, ALU.mult, None, ["Etab", "flg"], [Ek])
                        Et = Efv
                    else:
                        Et, Ek = Etab4[:, g, t, :], "Etab"
                    pw, pwk = btmp()
                    pl.append((pw, pwk))
                    TT("dve", pw[:, :], pf[:, :], Et, ALU.mult, [pfk, Ek], [pwk])
                yield
                O, Ok = bank()
                for t in range(3):
                    kblk = gbk + t
                    pw, pwk = pl[t]
                    MM(O[:, 0:512], vaw4[:, kblk, g, :], pw[:, :], t == 0, t == 2, ["vaw%d" % kblk, "vaw_init", pwk], [Ok])
                dn, dnk = ftmp()
                TT("dve", dn[64:128, :], O[64:128, 0:512], esink3[64:128, g, :], ALU.add, [Ok, "esinkT"], [dnk])
                yield
                r, rk = ftmp()
                ACT(dn[64:128, :], dn[64:128, :], AF.Ln, [dnk], [dnk])
                ACT(r[0:64, :], dn[64:128, :], AF.Exp, [dnk], [rk], scale=-1.0)
                for half in range(2):
                    orow = slice(half * 64, half * 64 + 64)
                    TT("dve", yaT3[orow, 2 * g:2 * g + 2, b * 128:(b + 1) * 128],
                       O[0:64, 0:512].rearrange("p (i a c) -> p i a c", i=2, a=2)[:, :, half, :],
                       r[0:64, :].rearrange("p (i a c) -> p i a c", i=2, a=2)[:, :, half, :], ALU.mult, [Ok, rk, "RG"], ["yaT"])

            pipeline([win_task(b, g) for b in range(NBC) for g in range(2)])
            if p == 0 and ch == 0:
                dump("d_ya", QY[:, 2048:4096], [128, 2048], BF16, ["yaT"])
                dump("d_qa", QY[:, 0:2048], [128, 2048], BF16, ["qaT"])
            for br in range(2):
                for j in range(4):
                    bk, bkk = bank()
                    c0 = br * 512 + j * 128
                    for k in range(8):
                        MM(bk[:, 0:CH], WG[:, k, c0:c0 + 128], hTo3[:, k, cols], k == 0, k == 7, hk + K_G, [bkk])
                    sz, szk = ftmp()
                    ACT(sz[:, 0:CH], bk[:, 0:CH], AF.Silu, [bkk], [szk])
                    if br == 0:
                        yv, yk = yaT3[:, j, 0:CH], ["yaT"]
                    else:
                        yv, yk = yb3[:, j, cols], ["yb%d_%d_0" % (j, ch), "yb%d_%d_1" % (j, ch)]
                    TT("dve", yv, sz[:, 0:CH], yv, ALU.mult, [szk] + yk, yk + ["uT%d" % (br * 4 + j)])
            for m in range(8):
                ba, bak = bank()
                for j in range(4):
                    MM(ba[:, 0:CH], WPA[:, j, m * 128:(m + 1) * 128], yaT3[:, j, 0:CH], j == 0, j == 3, ["uT%d" % j, "yaT"] + K_P, [bak])
                bb, bbk = bank()
                for j in range(4):
                    MM(bb[:, 0:CH], WPB[:, j, m * 128:(m + 1) * 128], yb3[:, j, cols], j == 0, j == 3,
                       ["uT%d" % (4 + j), "yb%d_%d_0" % (j, ch), "yb%d_%d_1" % (j, ch)] + K_P, [bbk])
                sg = []
                for br in range(2):
                    bk, bkk = bank()
                    c0 = 1024 + br * 1024 + m * 128
                    for k in range(8):
                        MM(bk[:, 0:CH], WG[:, k, c0:c0 + 128], hTo3[:, k, cols], k == 0, k == 7, hk + K_G, [bkk])
                    s_, sk_ = ftmp()
                    ACT(s_[:, 0:CH], bk[:, 0:CH], AF.Sigmoid, [bkk, "bg"], [sk_], bias=bg[:, br * 8 + m:br * 8 + m + 1])
                    sg.append((s_, sk_))
                t1, t1k = ftmp()
                TT("dve", t1[:, 0:CH], ba[:, 0:CH], sg[0][0][:, 0:CH], ALU.mult, [bak, sg[0][1]], [t1k])
                t2, t2k = ftmp()
                TT("dve", t2[:, 0:CH], bb[:, 0:CH], sg[1][0][:, 0:CH], ALU.mult, [bbk, sg[1][1]], [t2k])
                TT("pool", mTl[m][:, 0:CH], t1[:, 0:CH], t2[:, 0:CH], ALU.add, [t1k, t2k, "RG"], ["mT%d" % m] + (["qaT"] if m < 4 else []))
            if p == 0 and ch == 0:
                dump("d_m0", QY[:, 0:2048], [128, 2048], BF16, ["mT%d" % m_ for m_ in range(4)])
                dump("d_m1", mTb[:, :], [128, 2048], BF16, ["mT%d" % m_ for m_ in range(4, 8)])
                dump("d_u", QY[:, 2048:4096], [128, 2048], BF16, ["yaT"])
            def outproj_task(tb_, ch=ch, p=p):
                nonlocal ocnt
                blk = ch * NBC + tb_
                row0 = (p * NBW + 1 + blk) * 128
                oi = ocnt % 2
                ocnt += 1
                i = st["xt"] % NXT
                st["xt"] += 1
                xtile, xk, sst, sk = xt[i], "xt%d" % i, ss[i], "ss%d" % i
                DMA("sp", xtile[:, :], xq[row0:row0 + 128, :], writes=[xk], key="ld_" + xk)
                rs, rsk = xtile, xk
                for hf2 in range(2):
                    bk, bkk = bank()
                    for k in range(8):
                        MM(bk[:, 0:512], mTl[k][:, tb_ * 128:(tb_ + 1) * 128], WO[:, k, hf2 * 512:(hf2 + 1) * 512], k == 0, k == 7,
                           ["mT%d" % k] + K_O, [bkk])
                    TT("dve", rs[:, hf2 * 512:(hf2 + 1) * 512], bk[:, 0:512], xtile[:, hf2 * 512:(hf2 + 1) * 512], ALU.add, [bkk, xk], [rsk])
                yield
                jx, jxk = xsb[oi], "xsb%d" % oi
                ACT(jx[:, :], rs[:, :], AF.Square, [rsk], [jxk, sk], accum_out=sst[:, 0:1])
                ACT(sst[:, 1:2], sst[:, 0:1], AF.Ln, [sk, "epsT"], [sk], scale=1.0 / D, bias=epsT[:, 0:1])
                ACT(sst[:, 2:3], sst[:, 1:2], AF.Exp, [sk], [sk], scale=-0.5)
                yo, yok = xtile, xk
                STT("dve", yo[:, :], rs[:, :], sst[:, 2:3], fgB[:, :], [rsk, sk, "fgB"], [yok])
                orow = p * QP + blk * 128
                DMA("pool", yout[orow:orow + 128, :], yo[:, :], reads=[yok], key="st_xt%d" % i)

            optasks = [outproj_task(tb_) for tb_ in range(NBC)]
            if ch == NGQ - 1 and p + 1 < NPASS:
                st["nb"] = 7
                btasks = make_B_tasks(p + 1)
                merged = []
                for k_ in range(max(len(optasks), len(btasks))):
                    if k_ < len(optasks):
                        merged.append(optasks[k_])
                    if k_ < len(btasks):
                        merged.append(btasks[k_])
                pipeline(merged)
                B_done.add(p + 1)
                pend_cq = True
            else:
                pipeline(optasks)
                pend_cq = False
        if pend_cq:
            emit_B_cq()
    n = P.emit()
    print("ops:", n)
    return nc


def host_inputs(cfg, x_prompt, x_sample, ln_g, w_in, b_gate, sink_a, q_norm_g, kv_norm_g, w_uq, w_ukv,
                w_proj_a, w_proj_b, w_out, rel_bias, final_g):
    f = np.float32
    x_prompt = np.asarray(x_prompt, f)
    x_sample = np.asarray(x_sample, f)
    xp = np.ascontiguousarray(x_prompt[0])
    NBW = cfg.NBQ + 2
    QP = cfg.QP
    half = ROPE // 2
    inv = np.power(np.float32(10000.0), -np.arange(half, dtype=f) / half).astype(f)
    pos = np.arange(cfg.TMAX, dtype=f)
    ang = pos[None, :] * inv[:, None]
    cos2 = np.concatenate([np.cos(ang), np.cos(ang)], 0).astype(f)
    sin2 = np.concatenate([np.sin(ang), np.sin(ang)], 0).astype(f)
    ropek = np.ascontiguousarray(np.concatenate([cos2, sin2], 0))
    j = np.arange(512)
    rel = 255 - j
    bucket = t5_bucket_np(rel.astype(np.int32))
    onehot = np.zeros((NBUCK, 512), f)
    onehot[bucket[:511], np.arange(511)] = 1.0
    valid = (np.abs(rel) <= 128).astype(f)
    valid[511] = 0.0
    validv = np.ascontiguousarray(np.broadcast_to(valid[None, :], (8, 512))).astype(f)
    shared = dict(
        xp=xp, ropek=ropek, onehot=onehot, validv=validv,
        w_in=np.ascontiguousarray(np.asarray(w_in, f)[0]),
        ln_g=np.ascontiguousarray(np.asarray(ln_g, f)[0].reshape(8, 128).T),
        b_gate=np.ascontiguousarray(np.asarray(b_gate, f)[0].reshape(16, 128).T),
        sink_a=np.ascontiguousarray(np.asarray(sink_a, f)[0]),
        q_norm_g=np.ascontiguousarray(np.asarray(q_norm_g, f)[0].reshape(2, 128).T),
        kv_norm_g=np.ascontiguousarray(np.asarray(kv_norm_g, f)[0].reshape(1, 128).T),
        w_uq=np.ascontiguousarray(np.asarray(w_uq, f)[0]),
        w_ukv=np.ascontiguousarray(np.asarray(w_ukv, f)[0]),
        w_proj_a=np.ascontiguousarray(np.asarray(w_proj_a, f)[0]),
        w_proj_b=np.ascontiguousarray(np.asarray(w_proj_b, f)[0]),
        w_out=np.ascontiguousarray(np.asarray(w_out, f)[0]),
        rel_bias=np.ascontiguousarray(np.asarray(rel_bias, f)),
        final_g=np.ascontiguousarray(np.asarray(final_g, f)),
    )
    in_maps = []
    for c in range(cfg.NC):
        xs = np.ascontiguousarray(x_sample[c * cfg.NS:(c + 1) * cfg.NS].reshape(cfg.NS * cfg.DSEQ, D))
        xq = np.zeros((cfg.NPASS, NBW * 128, D), f)
        flags = np.zeros((cfg.NPASS, 2), f)
        ropeq = np.zeros((cfg.NPASS, 64, QP), f)
        for p, (seg, hf) in enumerate(cfg.passes):
            if seg == 0:
                seq, S = xp, cfg.SEQ
                start = c * cfg.OWNP + hf * QP
            else:
                seq, S = x_sample[c * cfg.NS + seg - 1], cfg.DSEQ
                start = hf * QP
            lo, hi = start - 128, start + QP + 128
            a, b = max(lo, 0), min(hi, S)
            xq[p, a - lo:b - lo] = seq[a:b]
            flags[p, 0] = 1.0 if lo >= 0 else 0.0
            flags[p, 1] = 1.0 if hi <= S else 0.0
            ropeq[p] = ropek[:, start:start + QP]
        m = dict(shared)
        m.update(xs=xs, xq=xq.reshape(cfg.NPASS * NBW * 128, D), flags=flags.reshape(-1), ropeq=ropeq)
        in_maps.append(m)
    return in_maps


def assemble(cfg, results):
    yp = np.zeros((1, cfg.SEQ, D), np.float32)
    ysm = np.zeros((cfg.DB, cfg.DSEQ, D), np.float32)
    for c in range(cfg.NC):
        y = np.asarray(results[c]["y"]).reshape(cfg.NPASS, cfg.QP, D)
        for p, (seg, hf) in enumerate(cfg.passes):
            if seg == 0:
                s0 = c * cfg.OWNP + hf * cfg.QP
                yp[0, s0:s0 + cfg.QP] = y[p]
            else:
                ysm[c * cfg.NS + seg - 1, hf * cfg.QP:(hf + 1) * cfg.QP] = y[p]
    return yp, ysm


def run(cfg, inputs, trace=False):
    nc = build_program(cfg)
    in_maps = host_inputs(cfg, **inputs)
    res = run_bass_kernel_spmd(nc, in_maps, core_ids=list(range(cfg.NC)), **({"trace": True} if trace else {}))
    return assemble(cfg, res.results), res


def kernel(**inputs):
    cfg = Cfg()
    (yp, ysm), _ = run(cfg, inputs)
    return yp, ysm
```

```python
import math
import contextlib
import numpy as np
import concourse.bass as bass
import concourse.mybir as mybir
from concourse.bass_utils import run_bass_kernel_spmd

F32 = mybir.dt.float32
BF16 = mybir.dt.bfloat16
AF = mybir.ActivationFunctionType
ALU = mybir.AluOpType

ENGS = ("pe", "act", "dve", "pool", "sp")
SYNC_SAME = ("act", "dve", "pool")

D = 1024
HA, HKV, DH = 8, 2, 64
HB, QL, KVL, NOPE, ROPE, DV = 8, 256, 128, 64, 32, 64
NBUCK, MAXD = 32, 128
EPS = 1e-6
C_QA, C_KA, C_VA, C_ZA, C_CQ, C_CKV, C_KR, C_ZB, C_GA, C_GB, C_END = 0, 512, 640, 768, 1280, 1536, 1664, 1696, 2208, 3232, 4256


class Cfg:
    def __init__(self, NC=8, SEQ=16384, DB=16, DSEQ=2048, QP=1024, CH=512):
        self.NC, self.SEQ, self.DB, self.DSEQ, self.QP, self.CH = NC, SEQ, DB, DSEQ, QP, CH
        self.OWNP = SEQ // NC
        self.NS = DB // NC
        self.segs = [(SEQ, self.OWNP)] + [(DSEQ, DSEQ)] * self.NS
        self.passes = []
        for s, (T, own) in enumerate(self.segs):
            assert own % QP == 0 and T % CH == 0 and QP % CH == 0
            for hf in range(own // QP):
                self.passes.append((s, hf))
        self.NPASS = len(self.passes)
        self.NBQ = QP // 128
        self.TK = SEQ + self.NS * DSEQ
        self.TMAX = max(SEQ, DSEQ)
        self.kbase = [0] + [SEQ + i * DSEQ for i in range(self.NS)]


class Prog:
    def __init__(self, nc):
        self.nc = nc
        self.ops = []

    def op(self, eng, fn, reads=(), writes=(), dma=None):
        self.ops.append((eng, fn, tuple(reads), tuple(writes), dma))

    def emit(self, final_wait_eng="sp"):
        nc, ops = self.nc, self.ops
        n = len(ops)
        last_w, readers = {}, {}
        deps = [None] * n
        raw = [None] * n
        for i, (eng, fn, rds, wrs, dma) in enumerate(ops):
            d = set()
            rw = set()
            for r in rds:
                j = last_w.get(r)
                if j is not None:
                    d.add(j)
                    rw.add(j)
            raw[i] = rw
            for w in wrs:
                j = last_w.get(w)
                if j is not None:
                    d.add(j)
                rl = readers.get(w)
                if rl:
                    d.update(rl)
            for r in rds:
                readers.setdefault(r, []).append(i)
            for w in wrs:
                last_w[w] = i
                readers[w] = []
            d.discard(i)
            deps[i] = d
        need_sig = [False] * n
        for i in range(n):
            e = ops[i][0]
            for j in deps[i]:
                if ops[j][4] is None and (ops[j][0] != e or e in SYNC_SAME):
                    need_sig[j] = True
        dma_keys = []
        seenk = set()
        for o in ops:
            if o[4] is not None and o[4] not in seenk:
                seenk.add(o[4])
                dma_keys.append(o[4])
        with contextlib.ExitStack() as st:
            eng_sems = {e: st.enter_context(nc.semaphore("s_" + e)) for e in ENGS}
            dma_sems = {k: st.enter_context(nc.semaphore("d_%d" % i)) for i, k in enumerate(dma_keys)}
            cnt = {e: 0 for e in ENGS}
            dcnt = {k: 0 for k in dma_keys}
            sig = [None] * n
            for i, (eng, fn, rds, wrs, dma) in enumerate(ops):
                if dma is not None:
                    dcnt[dma] += 16
                    sig[i] = (dma_sems[dma], dcnt[dma], ("d", dma))
                elif need_sig[i]:
                    cnt[eng] += 1
                    sig[i] = (eng_sems[eng], cnt[eng], ("e", eng))
            per_eng = {e: [] for e in ENGS}
            for i, o in enumerate(ops):
                per_eng[o[0]].append(i)
            block = st.enter_context(nc.Block())

            def make_body(e):
                def body(eng):
                    seen = {}
                    for i in per_eng[e]:
                        _, fn, _, _, dma = ops[i]
                        need = {}
                        for j in deps[i]:
                            s = sig[j]
                            if s is None:
                                continue
                            if ops[j][4] is None and ops[j][0] == e and e not in SYNC_SAME:
                                continue
                            sem, val, key = s
                            if seen.get(key, 0) >= val:
                                continue
                            if key not in need or need[key][1] < val:
                                need[key] = (sem, val)
                        for key, (sem, val) in need.items():
                            eng.wait_ge(sem, val)
                            seen[key] = val
                        ins = fn(eng)
                        if sig[i] is not None:
                            ins.then_inc(sig[i][0], 16 if dma is not None else 1)
                    if e == final_wait_eng:
                        for k in dma_keys:
                            if seen.get(("d", k), 0) < dcnt[k]:
                                eng.wait_ge(dma_sems[k], dcnt[k])
                        for e2 in ENGS:
                            if e2 != e and cnt[e2] > 0 and seen.get(("e", e2), 0) < cnt[e2]:
                                eng.wait_ge(eng_sems[e2], cnt[e2])
                return body

            block.tensor(make_body("pe"))
            block.scalar(make_body("act"))
            block.vector(make_body("dve"))
            block.gpsimd(make_body("pool"))
            block.sync(make_body("sp"))
        return n


def t5_bucket_np(rel):
    nb = NBUCK // 2
    max_exact = nb // 2
    ret = (rel > 0).astype(np.int32) * nb
    n = np.abs(rel)
    nf = np.maximum(n, 1).astype(np.float32)
    large = max_exact + (np.log(nf / max_exact) / math.log(MAXD / max_exact) * (nb - max_exact)).astype(np.int32)
    large = np.minimum(large, nb - 1)
    return ret + np.where(n < max_exact, n, large)


def build_program(cfg):
    nc = bass.Bass("TRN2", target_bir_lowering=False)
    P = Prog(nc)
    CH, QP, NBQ, NPASS, TK, TMAX = cfg.CH, cfg.QP, cfg.NBQ, cfg.NPASS, cfg.TK, cfg.TMAX
    NBW = NBQ + 2
    NGQ = QP // CH
    NBC = CH // 128
    TREG = max(cfg.SEQ, cfg.DSEQ)

    def din(name, shape, dt=F32):
        return nc.dram_tensor(name, list(shape), dt, kind="ExternalInput").ap()

    xp = din("xp", [cfg.SEQ, D])
    xs = din("xs", [cfg.NS * cfg.DSEQ, D])
    xq = din("xq", [NPASS * NBW * 128, D])
    flags = din("flags", [NPASS * 2])
    ropek = din("ropek", [64, TMAX])
    ropeq = din("ropeq", [NPASS, 64, QP])
    w_in = din("w_in", [D, C_END])
    ln_g = din("ln_g", [128, 8])
    b_gate = din("b_gate", [128, 16])
    sink_a = din("sink_a", [8])
    q_norm_g = din("q_norm_g", [128, 2])
    kv_norm_g = din("kv_norm_g", [128, 1])
    w_uq = din("w_uq", [QL, HB * 96])
    w_ukv = din("w_ukv", [KVL, HB * 128])
    w_proj_a = din("w_proj_a", [512, D])
    w_proj_b = din("w_proj_b", [512, D])
    w_out = din("w_out", [D, D])
    rel_bias = din("rel_bias", [NBUCK, HA])
    final_g = din("final_g", [D])
    onehot = din("onehot", [NBUCK, 512])
    validv = din("validv", [8, 512])
    yout = nc.dram_tensor("y", [NPASS * QP, D], F32, kind="ExternalOutput").ap()

    NWD = 24576 + 4096 * 3 + 8192
    wd_img = nc.dram_tensor("wd_img", [128, NWD], BF16, kind="Internal").ap()
    kvc = nc.dram_tensor("kvc", [160, TK], BF16, kind="Internal").ap()
    evec_t = nc.dram_tensor("evec", [8, 512], F32, kind="Internal")
    etoe_t = nc.dram_tensor("etoe", [8 * 128 * 512], F32, kind="Internal")

    sb = nc.alloc_sbuf_tensor
    O_CN, O_KT, O_V = 0, TREG, 2 * TREG
    NV = (TREG // 128) * 96
    O_QT = O_V + NV
    NREG = max(O_QT + 2 * QP, NWD)
    RG = sb("RG", [128, NREG], BF16)
    cnT = RG[:, O_CN:O_CN + TREG]
    KT = RG[:, O_KT:O_KT + TREG]
    Vb = RG[:, O_V:O_V + NV].rearrange("p (t c) -> p t c", c=96)
    QTs = [RG[:, O_QT + i * QP:O_QT + (i + 1) * QP] for i in range(2)]
    WQA = RG[:, 0:4096].rearrange("p (k c) -> p k c", k=8)
    WG = RG[:, 4096:28672].rearrange("p (k c) -> p k c", k=8)
    WPA = RG[:, 28672:32768].rearrange("p (k c) -> p k c", k=4)
    WPB = RG[:, 32768:36864].rearrange("p (k c) -> p k c", k=4)
    WO = RG[:, 36864:45056].rearrange("p (k c) -> p k c", k=8)

    hTo = sb("hTo", [128, 8 * QP], BF16)
    hTo3 = hTo[:, :].rearrange("p (k c) -> p k c", k=8)
    HR = sb("HR", [128, max(2048, 2 * QP)], BF16)
    hTh = [HR[:, i * 1024:(i + 1) * 1024] for i in range(2)]
    cqnT = sb("cqnT", [128, 2 * QP], BF16)
    cqn3 = cqnT[:, :].rearrange("p (k c) -> p k c", k=2)
    ybT = sb("ybT", [128, 4 * QP], BF16)
    yb3 = ybT[:, :].rearrange("p (k c) -> p k c", k=4)
    kaT = sb("kaT", [128, NBW * 128], BF16)
    vaw = sb("vaw", [128, NBW * 2 * 128], BF16)
    vaw4 = vaw[:, :].rearrange("p (b g c) -> p b g c", b=NBW, g=2)
    WA = sb("WA", [128, 8 * 192], BF16)
    WA3 = WA[:, :].rearrange("p (k c) -> p k c", k=8)
    WB = sb("WB", [128, 8 * 512], BF16)
    WB3 = WB[:, :].rearrange("p (k c) -> p k c", k=8)
    WUQ = sb("WUQ", [128, 2 * 8 * 128], BF16)
    WUQ4 = WUQ[:, :].rearrange("p (k h c) -> p k h c", k=2, h=8)
    WUKV = sb("WUKV", [128, 8 * 128], BF16)
    WUKV3 = WUKV[:, :].rearrange("p (h c) -> p h c", h=8)
    gcol = sb("gcol", [128, 8], F32)
    bg = sb("bg", [128, 16], F32)
    gq = sb("gq", [128, 2], F32)
    gkv = sb("gkv", [128, 1], F32)
    fgB = sb("fgB", [128, D], F32)
    epsT = sb("epsT", [128, 1], F32)
    ident = sb("ident", [128, 128], BF16)
    onesf = sb("onesf", [128, 128], F32)
    esk = sb("esk", [128, 8], F32)
    esinkT = sb("esinkT", [128, 2 * 512], BF16)
    esink3 = esinkT[:, :].rearrange("p (g c) -> p g c", g=2)
    Etab = sb("Etab", [128, 6 * 512], BF16)
    Etab4 = Etab[:, :].rearrange("p (g t c) -> p g t c", g=2, t=3)
    flg = sb("flg", [128, NPASS * 2], F32)
    ropq = HR[:, 0:2 * QP].bitcast(F32)
    NXT = 4
    xt = [sb("xt%d" % i, [128, D], F32) for i in range(NXT)]
    xsb = [sb("xsb%d" % i, [128, D], BF16) for i in range(2)]
    ss = [sb("ss%d" % i, [128, 4], F32) for i in range(NXT)]
    NF = 5
    f32t = [sb("f32t%d" % i, [128, 512], F32) for i in range(NF)]
    b16t = [sb("b16t%d" % i, [128, 512], BF16) for i in range(4)]
    QY = sb("QY", [128, 4096], BF16)
    qaT3 = QY[:, 0:2048].rearrange("p (j c) -> p j c", j=4)
    yaT3 = QY[:, 2048:4096].rearrange("p (j c) -> p j c", j=4)
    rpf = QY[:, 0:2048].bitcast(F32)
    rpk = [rpf[0:64, i * 512:(i + 1) * 512] for i in range(2)]
    cng = [QY[:, 2048 + i * 512:2048 + (i + 1) * 512] for i in range(2)]
    krg = [QY[0:32, 3072 + i * 512:3072 + (i + 1) * 512] for i in range(2)]
    mTb = sb("mTb", [128, 4 * 512], BF16)
    mTl = [QY[:, m * 512:(m + 1) * 512] for m in range(4)] + [mTb[:, m * 512:(m + 1) * 512] for m in range(4)]
    fence_t = sb("fence_t", [128, 2], F32)
    pb = [nc.alloc_psum_tensor("pb%d" % i, [128, 512], F32) for i in range(8)]
    print("sbuf bytes remaining:", nc.sbuf_bytes_remaining)

    st = {"bank": 0, "nb": 7, "f": 0, "b": 0, "xt": 0}
    NWARM = 0

    live_banks = set()

    def bank(hold=False):
        for _ in range(16):
            i = st["bank"] % st["nb"]
            st["bank"] += 1
            if "pb%d" % i not in live_banks:
                break
        else:
            raise RuntimeError("no free PSUM bank")
        if hold:
            live_banks.add("pb%d" % i)
        return pb[i], "pb%d" % i

    def release(k):
        live_banks.discard(k)

    def ftmp():
        i = st["f"] % NF
        st["f"] += 1
        return f32t[i], "f32t%d" % i

    def btmp():
        i = st["b"] % 4
        st["b"] += 1
        return b16t[i], "b16t%d" % i

    def I(eng, name, *args, reads=(), writes=(), dma=None, **kw):
        P.op(eng, lambda e: getattr(e, name)(*args, **kw), reads, writes, dma)

    def DMA(eng, out, in_, reads=(), writes=(), key=None):
        I(eng, "dma_start", out=out, in_=in_, reads=reads, writes=writes, dma=key)

    def MM(out, lhsT, rhs, start, stop, reads, writes):
        I("pe", "matmul", out, lhsT=lhsT, rhs=rhs, start=start, stop=stop, reads=reads, writes=writes)

    def ACT(out, in_, func, reads, writes, **kw):
        I("act", "activation", out=out, in_=in_, func=func, reads=reads, writes=writes, **kw)

    def TT(eng, out, in0, in1, op_, reads, writes):
        I(eng, "tensor_tensor", out=out, in0=in0, in1=in1, op=op_, reads=reads, writes=writes)

    def TS(eng, out, in0, s1, s2, op0, op1, reads, writes):
        if s2 is None:
            I(eng, "tensor_scalar", out=out, in0=in0, scalar1=s1, scalar2=None, op0=op0, reads=reads, writes=writes)
        else:
            I(eng, "tensor_scalar", out=out, in0=in0, scalar1=s1, scalar2=s2, op0=op0, op1=op1, reads=reads, writes=writes)

    def STT(eng, out, in0, scalar, in1, reads, writes):
        I(eng, "scalar_tensor_tensor", out=out, in0=in0, scalar=scalar, in1=in1, op0=ALU.mult, op1=ALU.mult, reads=reads, writes=writes)

    def CP(eng, out, in_, reads, writes):
        I(eng, "tensor_copy", out=out, in_=in_, reads=reads, writes=writes)

    def RCP(out, in_, reads, writes):
        I("dve", "reciprocal", out=out, in_=in_, reads=reads, writes=writes)

    I("pool", "memset", epsT[:], EPS, writes=["epsT"])
    I("pool", "memset", onesf[:], 1.0, writes=["onesf"])
    idf, idk = f32t[0], "f32t0"
    I("pool", "memset", idf[:, 0:128], 1.0, writes=[idk])
    I("pool", "affine_select", out=idf[:, 0:128], in_=idf[:, 0:128], pattern=[[-1, 128]], compare_op=ALU.is_equal,
      fill=0.0, base=0, channel_multiplier=1, reads=[idk], writes=[idk])
    CP("dve", ident[:], idf[:, 0:128], [idk], ["ident"])
    I("pool", "memset", vaw[:], 1.0, writes=["vaw_init"])
    for t_, d_, k_ in [(gcol, ln_g, "gcol"), (bg, b_gate, "bg"), (gq, q_norm_g, "gq"), (gkv, kv_norm_g, "gkv")]:
        DMA("sp", t_[:], d_[:, :], writes=[k_], key="ld_" + k_)
    DMA("sp", fgB[:], bass.AP(final_g.tensor, 0, [[0, 128], [1, D]]), writes=["fgB"], key="ld_fgB")
    DMA("sp", esk[:], bass.AP(sink_a.tensor, 0, [[0, 128], [1, 8]]), writes=["esk"], key="ld_esk")
    DMA("sp", flg[:], bass.AP(flags.tensor, 0, [[0, 128], [1, NPASS * 2]]), writes=["flg"], key="ld_flg")
    ACT(esk[:], esk[:], AF.Exp, ["esk"], ["esk"])
    for h in range(8):
        g, i = h // 4, h % 4
        CP("dve", esink3[:, g, i * 128:(i + 1) * 128], esk[:, h:h + 1].to_broadcast([128, 128]), ["esk"], ["esinkT"])

    rbt, oht, vvt, ev = f32t[1], f32t[2], f32t[3], f32t[4]
    DMA("sp", rbt[0:NBUCK, 0:8], rel_bias[:, :], writes=["f32t1"], key="ld_rb")
    DMA("sp", oht[0:NBUCK, :], onehot[:, :], writes=["f32t2"], key="ld_oh")
    DMA("sp", vvt[0:8, :], validv[:, :], writes=["f32t3"], key="ld_vv")
    MM(pb[0][0:8, :], rbt[0:NBUCK, 0:8], oht[0:NBUCK, :], True, True, ["f32t1", "f32t2"], ["pb0"])
    ACT(ev[0:8, :], pb[0][0:8, :], AF.Exp, ["pb0"], ["f32t4"])
    TT("dve", ev[0:8, :], ev[0:8, :], vvt[0:8, :], ALU.mult, ["f32t4", "f32t3"], ["f32t4"])
    DMA("sp", evec_t.ap()[:, :], ev[0:8, :], reads=["f32t4"], writes=["evec"], key="st_evec")
    DMA("sp", bass.AP(etoe_t, 0, [[128 * 512, 8], [512, 128], [1, 511]]), bass.AP(evec_t, 0, [[512, 8], [0, 128], [1, 511]]),
        reads=["evec"], writes=["etoe"], key="st_etoe")
    CT = [383, 255, 127]
    st["f"] = 0
    for g in range(2):
        for t in range(3):
            ft, fk = ftmp()
            DMA("sp", ft[:, :].rearrange("p (h c) -> p h c", h=4),
                bass.AP(etoe_t, 4 * g * 128 * 512 + CT[t], [[511, 128], [128 * 512, 4], [1, 128]]),
                reads=["etoe"], writes=[fk], key="ld_E%d%d" % (g, t))
            CP("dve", Etab4[:, g, t, :], ft[:, :], [fk], ["Etab"])

    def stage(src_ap, ncols):
        i = st["xt"] % NXT
        st["xt"] += 1
        DMA("sp", xt[i][:, 0:ncols], src_ap, writes=["xt%d" % i], key="ld_xt%d" % i)
        return xt[i], ["xt%d" % i, "gcol", "RG"]

    for k in range(8):
        rows = slice(k * 128, (k + 1) * 128)
        gk = gcol[:, k:k + 1]
        x_, rd = stage(w_in[rows, 0:768], 768)
        for hf in range(2):
            TS("dve", WQA[:, k, :].rearrange("p (j a c) -> p j a c", j=4, a=2)[:, :, hf, :],
               x_[:, hf * 256:(hf + 1) * 256].rearrange("p (j c) -> p j c", j=4), gk, None, ALU.mult, None, rd, ["WQA"])
        TS("dve", WB3[:, k, 256:512], x_[:, 512:768], gk, None, ALU.mult, None, rd, ["WB"])
        x_, rd = stage(w_in[rows, 768:1696], 928)
        TS("dve", WG[:, k, 0:512], x_[:, 0:512], gk, None, ALU.mult, None, rd, ["WG"])
        TS("dve", WB3[:, k, 0:256], x_[:, 512:768], gk, None, ALU.mult, None, rd, ["WB"])
        TS("dve", WA3[:, k, 0:160], x_[:, 768:928], gk, None, ALU.mult, None, rd, ["WA"])
        TS("dve", WA3[:, k, 176:192], x_[:, 896:912], gk, None, ALU.mult, None, rd, ["WA"])
        TS("dve", WA3[:, k, 160:176], x_[:, 912:928], gk, -1.0, ALU.mult, ALU.mult, rd, ["WA"])
        x_, rd = stage(w_in[rows, 1696:2720], 1024)
        TS("dve", WG[:, k, 512:1536], x_[:, 0:1024], gk, None, ALU.mult, None, rd, ["WG"])
        x_, rd = stage(w_in[rows, 2720:3744], 1024)
        TS("dve", WG[:, k, 1536:2560], x_[:, 0:1024], gk, None, ALU.mult, None, rd, ["WG"])
        x_, rd = stage(w_in[rows, 3744:4256], 512)
        TS("dve", WG[:, k, 2560:3072], x_[:, 0:512], gk, None, ALU.mult, None, rd, ["WG"])
    for k in range(2):
        x_, rd = stage(w_uq[k * 128:(k + 1) * 128, :], 768)
        x3 = x_[:, 0:768].rearrange("p (h c) -> p h c", h=8)
        CP("dve", WUQ4[:, k, :, 0:96], x3, rd, ["WUQ"])
        TS("dve", WUQ4[:, k, :, 96:112], x3[:, :, 80:96], -1.0, None, ALU.mult, None, rd, ["WUQ"])
        CP("dve", WUQ4[:, k, :, 112:128], x3[:, :, 64:80], rd, ["WUQ"])
    x_, rd = stage(w_ukv[:, :], 1024)
    CP("dve", WUKV[:, :], x_[:, :], rd, ["WUKV"])
    for k in range(4):
        x_, rd = stage(w_proj_a[k * 128:(k + 1) * 128, :], 1024)
        CP("dve", WPA[:, k, :], x_[:, :], rd, ["WPA"])
        x_, rd = stage(w_proj_b[k * 128:(k + 1) * 128, :], 1024)
        CP("dve", WPB[:, k, :], x_[:, :], rd, ["WPB"])
    for k in range(8):
        x_, rd = stage(w_out[k * 128:(k + 1) * 128, :], 1024)
        CP("dve", WO[:, k, :], x_[:, :], rd, ["WO"])
    wpieces = [(0, 4096), (4096, 16384), (16384, 28672), (28672, 36864), (36864, NWD)]
    NWP = len(wpieces)
    EARLY = [i for i in (0, 1) if wpieces[i][1] <= TREG]
    for i, (a, b) in enumerate(wpieces):
        DMA("sp", wd_img[:, a:b], RG[:, a:b], reads=["WG", "WQA", "WPA", "WPB", "WO", "RG"], writes=["wd_img%d" % i], key="st_wd%d" % i)

    dbg = getattr(cfg, "debug", False)

    def dump(name, ap, shape, dt, reads):
        if not dbg:
            return
        o = nc.dram_tensor(name, list(shape), dt, kind="ExternalOutput").ap()
        DMA("sp", o, ap, reads=reads, key="dbg_" + name)

    def fence():
        I("pool", "memset", fence_t[:], 0.0, writes=["RG", "fence_t"])

    def pipeline(tasks):
        active = []
        tasks = list(tasks)
        while tasks or active:
            if tasks:
                active.append(tasks.pop(0))
            nxt = []
            for g_ in active:
                try:
                    next(g_)
                    nxt.append(g_)
                except StopIteration:
                    pass
            active = nxt

    def load_norm_transpose(src_ap, dst_view, dst_keys, extra_reads=(), tail=None):
        i = st["xt"] % NXT
        st["xt"] += 1
        xtile, xk, sst, sk = xt[i], "xt%d" % i, ss[i], "ss%d" % i
        j = st["xt"] % 2
        xb, xbk = xsb[j], "xsb%d" % j
        DMA("sp", xtile[:, :], src_ap, writes=[xk], key="ld_" + xk)
        ACT(xb[:, :], xtile[:, :], AF.Square, [xk], [xbk, sk], accum_out=sst[:, 0:1])
        ACT(sst[:, 1:2], sst[:, 0:1], AF.Ln, [sk, "epsT"], [sk], scale=1.0 / D, bias=epsT[:, 0:1])
        ACT(sst[:, 2:3], sst[:, 1:2], AF.Exp, [sk], [sk], scale=-0.5)
        TS("dve", xb[:, :], xtile[:, :], sst[:, 2:3], None, ALU.mult, None, [xk, sk], [xbk])
        yield
        bk, bkk = bank(hold=True)
        bkb = bk[:, :].bitcast(BF16)
        for k in range(8):
            I("pe", "transpose", out=bkb[:, k * 128:(k + 1) * 128], in_=xb[:, k * 128:(k + 1) * 128], identity=ident[:],
              reads=[xbk, "ident"], writes=[bkk])
        for _ in range(NWARM):
            MM(pb[7][:, 0:512], ident[:], WB[:, 0:512], True, True, ["ident", "WB"], ["pb7"])
        yield
        CP("dve", dst_view, bkb[:, :].rearrange("p (k c) -> p k c", k=8), [bkk] + list(extra_reads), dst_keys)
        release(bkk)
        if tail is not None:
            yield
            r_ = tail()
            if r_ is not None:
                yield from r_

    if QP >= 2 * CH:
        hA = [hTo[:, i * 8 * CH:(i + 1) * 8 * CH].rearrange("p (k c) -> p k c", k=8) for i in range(2)]
    else:
        hAt = [sb("hA%d" % i, [128, 8 * CH], BF16) for i in range(2)]
        hA = [t_[:, :].rearrange("p (k c) -> p k c", k=8) for t_ in hAt]
    _ob = onesf[:, :].bitcast(BF16)
    onesb = bass.AP(_ob.tensor, _ob.offset + 1, [[_ob.ap[0][0], 128], [2, 128]])

    def phaseA_group_tail(grp, slot, tok0, pos0, hks):
        b1, b1k = bank(hold=True)
        for k in range(8):
            MM(b1[:, 0:CH], WA3[:, k, 0:128], hA[slot][:, k, :], k == 0, k == 7, hks + ["WA", "RG"], [b1k])
        b2, b2k = bank(hold=True)
        for k in range(8):
            MM(b2[0:64, 0:CH], WA3[:, k, 128:192], hA[slot][:, k, :], k == 0, k == 7, hks + ["WA", "RG"], [b2k])
        rp, rpkk = rpk[slot], "rpk%d" % slot
        DMA("sp", rp[:, 0:CH], ropek[:, pos0:pos0 + CH], reads=["RG"], writes=[rpkk], key="ld_" + rpkk)
        yield
        sq, sqk = btmp()
        ACT(sq[:, 0:CH], b1[:, 0:CH], AF.Square, [b1k], [sqk])
        b3, b3k = bank(hold=True)
        MM(b3[:, 0:CH], onesb, sq[:, 0:CH], True, True, [sqk, "onesf"], [b3k])
        ta, tak = ftmp()
        tb, tbk = ftmp()
        TT("dve", ta[0:32, 0:CH], b2[0:32, 0:CH], rp[0:32, 0:CH], ALU.mult, [b2k, rpkk, "RG"], [tak])
        TT("dve", tb[0:32, 0:CH], b2[32:64, 0:CH], rp[32:64, 0:CH], ALU.mult, [b2k, rpkk, "RG"], [tbk])
        release(b2k)
        yield
        rc, rck = ftmp()
        ACT(rc[:, 0:CH], b3[:, 0:CH], AF.Ln, [b3k, "epsT"], [rck], scale=1.0 / KVL, bias=epsT[:, 0:1])
        release(b3k)
        ACT(rc[:, 0:CH], rc[:, 0:CH], AF.Exp, [rck], [rck], scale=-0.5)
        kg, kgk = krg[slot], "krg%d" % slot
        TT("dve", kg[:, 0:CH], ta[0:32, 0:CH], tb[0:32, 0:CH], ALU.add, [tak, tbk, "RG"], [kgk])
        DMA("pool", kvc[128:160, tok0:tok0 + CH], kg[:, 0:CH], reads=[kgk, "RG"], writes=["kvck%d" % grp], key="st_" + kgk)
        yield
        cg, cgk = cng[slot], "cng%d" % slot
        STT("dve", cg[:, 0:CH], b1[:, 0:CH], gkv[:, 0:1], rc[:, 0:CH], [b1k, rck, "gkv", "RG"], [cgk])
        release(b1k)
        DMA("pool", kvc[0:128, tok0:tok0 + CH], cg[:, 0:CH], reads=[cgk, "RG"], writes=["kvcc%d" % grp], key="st_" + cgk)

    tasksA = []
    for grp in range(TK // CH):
        slot = grp % 2
        tok0 = grp * CH
        if tok0 < cfg.SEQ:
            src, r0, pos0 = xp, tok0, tok0
        else:
            src, r0 = xs, tok0 - cfg.SEQ
            pos0 = r0 % cfg.DSEQ
        hks = ["hA%d_%d" % (slot, tl) for tl in range(NBC)]
        for tl in range(NBC):
            tail = None
            if tl == NBC - 1:
                tail = (lambda grp=grp, slot=slot, tok0=tok0, pos0=pos0, hks=hks: phaseA_group_tail(grp, slot, tok0, pos0, hks))
            tasksA.append(load_norm_transpose(src[r0 + tl * 128: r0 + (tl + 1) * 128, :], hA[slot][:, :, tl * 128:(tl + 1) * 128],
                                              [hks[tl]], ["RG"], tail))
    pipeline(tasksA)

    dump("d_kvc", kvc[:, :], [160, TK], BF16, ["kvcc%d" % g_ for g_ in range(TK // CH)] + ["kvck%d" % g_ for g_ in range(TK // CH)])
    dump("d_E", Etab[:, :], [128, 6 * 512], BF16, ["Etab"])
    SC_A = DH ** -0.5
    SC_B = (NOPE + ROPE) ** -0.5
    ocnt = 0
    B_done = set()
    for p, (seg, hf) in enumerate(cfg.passes):
        T, own = cfg.segs[seg]
        kb = cfg.kbase[seg]
        NJ = T // CH
        NT = T // 128
        st["nb"] = 7
        fence()
        NLD = min(4, NJ)
        for i in range(NLD):
            j0, j1 = i * NJ // NLD, (i + 1) * NJ // NLD
            a, b = j0 * CH, j1 * CH
            kvr = ["kvcc%d" % ((kb + jj * CH) // CH) for jj in range(j0, j1)] + ["kvck%d" % ((kb + jj * CH) // CH) for jj in range(j0, j1)]
            DMA("pool", cnT[:, a:b], kvc[0:128, kb + a:kb + b], reads=["RG"] + kvr, writes=["cn%d" % j for j in range(j0, j1)], key="ld_cn%d" % i)
            DMA("pool", KT[64:96, a:b], kvc[128:160, kb + a:kb + b], reads=["RG"] + kvr, writes=["KTr%d" % j for j in range(j0, j1)], key="ld_kr%d" % i)
        I("pool", "memset", Vb[:, 0:NT, 64:96], 1.0, reads=["RG"], writes=["Vones"])
        def phaseB_tail(blk, hv, hkeys):
            bka, bkak = bank(hold=True)
            for k in range(8):
                MM(bka[:, 0:128], WB3[:, k, 256:384], hv[:, k, :], k == 0, k == 7, hkeys + ["WB"], [bkak])
            bkv, bkvk = bank(hold=True)
            for k in range(8):
                MM(bkv[:, 0:128], hv[:, k, :], WB3[:, k, 384:512], k == 0, k == 7, hkeys + ["WB"], [bkvk])
            yield
            CP("dve", kaT[:, blk * 128:(blk + 1) * 128], bka[:, 0:128], [bkak], ["kaT%d" % blk])
            CP("dve", vaw4[:, blk, :, 0:64], bkv[:, 0:128].rearrange("p (g c) -> p g c", g=2), [bkvk, "vaw_init"], ["vaw%d" % blk])
            release(bkak)
            release(bkvk)

        def make_B_tasks(pp):
          tasksB = []
          for blk in range(NBW):
            row0 = (pp * NBW + blk) * 128
            if blk == 0 or blk == NBW - 1:
                hi = 0 if blk == 0 else 1
                hv = hTh[hi].rearrange("p (k c) -> p k c", k=8)
                hkeys = ["hTh%d" % hi, "ropq"]
            else:
                hv = hTo3[:, :, (blk - 1) * 128: blk * 128]
                hkeys = ["hTo%d" % (blk - 1)]
            tasksB.append(load_norm_transpose(xq[row0:row0 + 128, :], hv, hkeys, ["RG"] if pp == 0 else [],
                                              (lambda blk=blk, hv=hv, hkeys=hkeys: phaseB_tail(blk, hv, hkeys))))
          return tasksB

        def emit_B_cq():
          for grp in range(NGQ):
              cols = slice(grp * CH, (grp + 1) * CH)
              hk = ["hTo%d" % b for b in range(grp * NBC, (grp + 1) * NBC)]
              bq = []
              for m in range(2):
                  bk, bkk = bank()
                  bq.append((bk, bkk))
                  for k in range(8):
                      MM(bk[:, 0:CH], WB3[:, k, m * 128:(m + 1) * 128], hTo3[:, k, cols], k == 0, k == 7, hk + ["WB"], [bkk])
              b3, b3k = bank()
              for m in range(2):
                  sq, sqk = btmp()
                  ACT(sq[:, 0:CH], bq[m][0][:, 0:CH], AF.Square, [bq[m][1]], [sqk])
                  MM(b3[:, 0:CH], onesb, sq[:, 0:CH], m == 0, m == 1, [sqk, "onesf"], [b3k])
              rc, rck = ftmp()
              ACT(rc[:, 0:CH], b3[:, 0:CH], AF.Ln, [b3k, "epsT"], [rck], scale=1.0 / QL, bias=epsT[:, 0:1])
              ACT(rc[:, 0:CH], rc[:, 0:CH], AF.Exp, [rck], [rck], scale=-0.5)
              for m in range(2):
                  STT("dve", cqn3[:, m, cols], bq[m][0][:, 0:CH], gq[:, m:m + 1], rc[:, 0:CH], [bq[m][1], rck, "gq"], ["cqn%d_%d" % (grp, m)])

        if p not in B_done:
            pipeline(make_B_tasks(p))
            emit_B_cq()
            B_done.add(p)

        if p == 0:
            dump("d_hTo", hTo[:, :], [128, 8 * QP], BF16, ["hTo%d" % b_ for b_ in range(NBQ)])
            dump("d_cqn", cqnT[:, :], [128, 2 * QP], BF16, ["cqn%d_%d" % (g_, m_) for g_ in range(NGQ) for m_ in range(2)])
            dump("d_ka", kaT[:, :], [128, NBW * 128], BF16, ["kaT%d" % b_ for b_ in range(NBW)])
            dump("d_va", vaw[:, :], [128, NBW * 2 * 128], BF16, ["vaw%d" % b_ for b_ in range(NBW)])
        DMA("sp", ropq[64:128, :], ropeq[p, :, :], writes=["ropq", "hTh0", "hTh1"], key="ld_ropq")
        st["nb"] = 2
        st["bank"] = 0
        ocount = 0
        def expK(h, j):
            bk, bkk = bank()
            MM(bk[0:64, 0:CH], WUKV3[:, h, 0:64], cnT[:, j * CH:(j + 1) * CH], True, True, ["cn%d" % j, "WUKV", "RG"], [bkk])
            CP("dve", KT[0:64, j * CH:(j + 1) * CH], bk[0:64, 0:CH], [bkk, "RG"], ["KTn%d" % j])

        def expV(h, t0):
            nt = min(8, NT - t0)
            bk, bkk = bank()
            for i in range(nt):
                MM(bk[:, i * 64:(i + 1) * 64], cnT[:, (t0 + i) * 128:(t0 + i + 1) * 128], WUKV3[:, h, 64:128], True, True,
                   ["cn%d" % ((t0 + i) * 128 // CH), "WUKV", "RG"], [bkk])
            CP("dve", Vb[:, t0:t0 + nt, 0:64], bk[:, 0:nt * 64].rearrange("p (t c) -> p t c", c=64), [bkk, "RG", "Vones"], ["V%d" % (t0 // 8)])

        def expQ(h, grp):
            QT = QTs[h % 2]
            qk_ = "QT%d" % (h % 2)
            cols = slice(grp * CH, (grp + 1) * CH)
            bk, bkk = bank()
            for k in range(2):
                MM(bk[:, 0:CH], WUQ4[:, k, h, :], cqn3[:, k, cols], k == 0, k == 1, ["cqn%d_%d" % (grp, k), "WUQ"], [bkk])
            CP("dve", QT[0:64, cols], bk[0:64, 0:CH], [bkk, "RG"], [qk_ + "n%d" % grp])
            ta, tak = ftmp()
            tb, tbk = ftmp()
            TT("dve", ta[64:96, 0:CH], bk[64:96, 0:CH], ropq[64:96, cols], ALU.mult, [bkk, "ropq"], [tak])
            TT("dve", tb[64:96, 0:CH], bk[96:128, 0:CH], ropq[96:128, cols], ALU.mult, [bkk, "ropq"], [tbk])
            TT("dve", QT[64:96, cols], ta[64:96, 0:CH], tb[64:96, 0:CH], ALU.add, [tak, tbk, "RG"], [qk_ + "r%d" % grp])

        for grp in range(NGQ):
            expQ(0, grp)
        for j in range(NJ):
            expK(0, j)
        for t0 in range(0, NT, 8):
            expV(0, t0)
        units = [(h, qc, t) for h in range(HB) for qc in range(NGQ) for t in range(NT)]
        LOOK = min(3, NT * NGQ)

        def qk(u):
            h, qc, t = units[u]
            QT = QTs[h % 2]
            qk_ = "QT%d" % (h % 2)
            s_, sk_ = pb[4 + (u % 4)], "pb%d" % (4 + u % 4)
            j = t * 128 // CH
            MM(s_[:, 0:CH], KT[0:96, t * 128:(t + 1) * 128], QT[0:96, qc * CH:(qc + 1) * CH], True, True,
               ["KTn%d" % j, "KTr%d" % j, qk_ + "n%d" % qc, qk_ + "r%d" % qc, "RG"], [sk_])

        for u in range(min(LOOK, len(units))):
            qk(u)
        O, Ok = None, None
        pendK = []
        deferred = []
        for u, (h, qc, t) in enumerate(units):
            if t == 0:
                O, Ok = pb[2 + ocount % 2], "pb%d" % (2 + ocount % 2)
                ocount += 1
            if h == HB - 1 and qc == 0 and t == 0:
                for i in EARLY:
                    a, b = wpieces[i]
                    DMA("sp", RG[:, a:b], wd_img[:, a:b], reads=["wd_img%d" % i],
                        writes=["WD%d" % i] + ["cn%d" % j for j in range(a // CH, (b + CH - 1) // CH)], key="ld_wd%d" % i)
            s_, sk_ = pb[4 + (u % 4)], "pb%d" % (4 + u % 4)
            pt, ptk = btmp()
            ACT(pt[:, 0:CH], s_[:, 0:CH], AF.Exp, [sk_], [ptk], scale=SC_B)
            while deferred and deferred[0][0] <= u:
                deferred.pop(0)[1]()
            last_sweep = (qc == NGQ - 1) and (h + 1 < HB)
            if u + LOOK < len(units):
                h2, qc2, t2 = units[u + LOOK]
                if h2 == h or (qc == NGQ - 1 and (t2 // NBC + 1) * NBC - 1 < t):
                    qk(u + LOOK)
                else:
                    pendK.append(u + LOOK)
            MM(O[0:96, 0:CH], Vb[:, t, :], pt[:, 0:CH], t == 0, t == NT - 1, ["V%d" % (t // 8), "Vones", ptk, "RG"], [Ok])
            if qc == 0 and t == 0 and h + 1 < HB:
                for grp in range(NGQ):
                    expQ(h + 1, grp)
            if last_sweep:
                if (t + 1) % NBC == 0:
                    expK(h + 1, t // NBC)
                if (t + 1) % 8 == 0 or t == NT - 1:
                    expV(h + 1, (t // 8) * 8)
            if t == NT - 1:
                def fin(O=O, Ok=Ok, h=h, qc=qc):
                    r, rk = ftmp()
                    ACT(r[64:96, 0:CH], O[64:96, 0:CH], AF.Ln, [Ok], [rk + "l"])
                    ACT(r[0:32, 0:CH], r[64:96, 0:CH], AF.Exp, [rk + "l"], [rk], scale=-1.0)
                    r0_ = (h % 2) * 64
                    for hh_ in range(2):
                        TT("dve", yb3[r0_ + 32 * hh_:r0_ + 32 * hh_ + 32, h // 2, qc * CH:(qc + 1) * CH],
                           O[32 * hh_:32 * hh_ + 32, 0:CH], r[0:32, 0:CH], ALU.mult, [Ok, rk], ["yb%d_%d_%d" % (h // 2, qc, h % 2)])
                deferred.append((u + 2, fin))
                if qc == NGQ - 1:
                    for u2 in pendK:
                        qk(u2)
                    pendK = []

        while deferred:
            deferred.pop(0)[1]()
        if p == 0:
            dump("d_yb", ybT[:, :], [128, 4 * QP], BF16, ["yb%d_%d_%d" % (j_, c_, h_) for j_ in range(4) for c_ in range(NGQ) for h_ in range(2)])
        st["nb"] = 8
        fence()
        for i, (a, b) in enumerate(wpieces):
            if i not in EARLY:
                DMA("sp" if i % 2 == 0 else "pool", RG[:, a:b], wd_img[:, a:b], reads=["RG", "wd_img%d" % i], writes=["WD%d" % i], key="ld_wd%d" % i)
        K_QA, K_G, K_P, K_O = ["WD0", "RG"], ["WD1", "WD2", "RG"], ["WD3", "RG"], ["WD4", "RG"]
        for ch in range(NGQ):
            cols = slice(ch * CH, (ch + 1) * CH)
            hk = ["hTo%d" % b for b in range(ch * NBC, (ch + 1) * NBC)]
            for j in range(4):
                bk, bkk = bank()
                for k in range(8):
                    MM(bk[:, 0:CH], WQA[:, k, j * 128:(j + 1) * 128], hTo3[:, k, cols], k == 0, k == 7, hk + K_QA, [bkk])
                ACT(qaT3[:, j, 0:CH], bk[:, 0:CH], AF.Copy, [bkk, "RG"], ["qaT", "mT0", "mT1", "mT2", "mT3"])
            def win_task(b, g, ch=ch, p=p):
                gbk = ch * NBC + b
                rows = slice(g * 64, g * 64 + 64)
                sl = []
                for t in range(3):
                    kblk = gbk + t
                    s_, sk_ = bank()
                    sl.append((s_, sk_))
                    MM(s_[:, 0:512], kaT[rows, kblk * 128:(kblk + 1) * 128], qaT3[rows, :, b * 128:(b + 1) * 128], True, True,
                       ["kaT%d" % kblk, "qaT"], [sk_])
                yield
                pl = []
                for t in range(3):
                    s_, sk_ = sl[t]
                    pf, pfk = ftmp()
                    ACT(pf[:, :], s_[:, 0:512], AF.Exp, [sk_], [pfk], scale=SC_A)
                    if (gbk == 0 and t == 0) or (gbk == NBQ - 1 and t == 2):
                        fcol = 2 * p if t == 0 else 2 * p + 1
                        Ef, Ek = ftmp()
                        Efv = Ef[:, :].bitcast(BF16)[:, 0:512]
                        TS("dve", Efv, Etab4[:, g, t, :], flg[:, fcol:fcol + 1], None, ALU.mult, None, ["Etab", "flg"], [Ek])
                        Et = Efv
                    else:
                        Et, Ek = Etab4[:, g, t, :], "Etab"
                    pw, pwk = btmp()
                    pl.append((pw, pwk))
                    TT("dve", pw[:, :], pf[:, :], Et, ALU.mult, [pfk, Ek], [pwk])
                yield
                O, Ok = bank()
                for t in range(3):
                    kblk = gbk + t
                    pw, pwk = pl[t]
                    MM(O[:, 0:512], vaw4[:, kblk, g, :], pw[:, :], t == 0, t == 2, ["vaw%d" % kblk, "vaw_init", pwk], [Ok])
                dn, dnk = ftmp()
                TT("dve", dn[64:128, :], O[64:128, 0:512], esink3[64:128, g, :], ALU.add, [Ok, "esinkT"], [dnk])
                yield
                r, rk = ftmp()
                ACT(dn[64:128, :], dn[64:128, :], AF.Ln, [dnk], [dnk])
                ACT(r[0:64, :], dn[64:128, :], AF.Exp, [dnk], [rk], scale=-1.0)
                for half in range(2):
                    orow = slice(half * 64, half * 64 + 64)
                    TT("dve", yaT3[orow, 2 * g:2 * g + 2, b * 128:(b + 1) * 128],
                       O[0:64, 0:512].rearrange("p (i a c) -> p i a c", i=2, a=2)[:, :, half, :],
                       r[0:64, :].rearrange("p (i a c) -> p i a c", i=2, a=2)[:, :, half, :], ALU.mult, [Ok, rk, "RG"], ["yaT"])

            pipeline([win_task(b, g) for b in range(NBC) for g in range(2)])
            if p == 0 and ch == 0:
                dump("d_ya", QY[:, 2048:4096], [128, 2048], BF16, ["yaT"])
                dump("d_qa", QY[:, 0:2048], [128, 2048], BF16, ["qaT"])
            for br in range(2):
                for j in range(4):
                    bk, bkk = bank()
                    c0 = br * 512 + j * 128
                    for k in range(8):
                        MM(bk[:, 0:CH], WG[:, k, c0:c0 + 128], hTo3[:, k, cols], k == 0, k == 7, hk + K_G, [bkk])
                    sz, szk = ftmp()
                    ACT(sz[:, 0:CH], bk[:, 0:CH], AF.Silu, [bkk], [szk])
                    if br == 0:
                        yv, yk = yaT3[:, j, 0:CH], ["yaT"]
                    else:
                        yv, yk = yb3[:, j, cols], ["yb%d_%d_0" % (j, ch), "yb%d_%d_1" % (j, ch)]
                    TT("dve", yv, sz[:, 0:CH], yv, ALU.mult, [szk] + yk, yk + ["uT%d" % (br * 4 + j)])
            for m in range(8):
                ba, bak = bank()
                for j in range(4):
                    MM(ba[:, 0:CH], WPA[:, j, m * 128:(m + 1) * 128], yaT3[:, j, 0:CH], j == 0, j == 3, ["uT%d" % j, "yaT"] + K_P, [bak])
                bb, bbk = bank()
                for j in range(4):
                    MM(bb[:, 0:CH], WPB[:, j, m * 128:(m + 1) * 128], yb3[:, j, cols], j == 0, j == 3,
                       ["uT%d" % (4 + j), "yb%d_%d_0" % (j, ch), "yb%d_%d_1" % (j, ch)] + K_P, [bbk])
                sg = []
                for br in range(2):
                    bk, bkk = bank()
                    c0 = 1024 + br * 1024 + m * 128
                    for k in range(8):
                        MM(bk[:, 0:CH], WG[:, k, c0:c0 + 128], hTo3[:, k, cols], k == 0, k == 7, hk + K_G, [bkk])
                    s_, sk_ = ftmp()
                    ACT(s_[:, 0:CH], bk[:, 0:CH], AF.Sigmoid, [bkk, "bg"], [sk_], bias=bg[:, br * 8 + m:br * 8 + m + 1])
                    sg.append((s_, sk_))
                t1, t1k = ftmp()
                TT("dve", t1[:, 0:CH], ba[:, 0:CH], sg[0][0][:, 0:CH], ALU.mult, [bak, sg[0][1]], [t1k])
                t2, t2k = ftmp()
                TT("dve", t2[:, 0:CH], bb[:, 0:CH], sg[1][0][:, 0:CH], ALU.mult, [bbk, sg[1][1]], [t2k])
                TT("pool", mTl[m][:, 0:CH], t1[:, 0:CH], t2[:, 0:CH], ALU.add, [t1k, t2k, "RG"], ["mT%d" % m] + (["qaT"] if m < 4 else []))
            if p == 0 and ch == 0:
                dump("d_m0", QY[:, 0:2048], [128, 2048], BF16, ["mT%d" % m_ for m_ in range(4)])
                dump("d_m1", mTb[:, :], [128, 2048], BF16, ["mT%d" % m_ for m_ in range(4, 8)])
                dump("d_u", QY[:, 2048:4096], [128, 2048], BF16, ["yaT"])
            def outproj_task(tb_, ch=ch, p=p):
                nonlocal ocnt
                blk = ch * NBC + tb_
                row0 = (p * NBW + 1 + blk) * 128
                oi = ocnt % 2
                ocnt += 1
                i = st["xt"] % NXT
                st["xt"] += 1
                xtile, xk, sst, sk = xt[i], "xt%d" % i, ss[i], "ss%d" % i
                DMA("sp", xtile[:, :], xq[row0:row0 + 128, :], writes=[xk], key="ld_" + xk)
                rs, rsk = xtile, xk
                for hf2 in range(2):
                    bk, bkk = bank()
                    for k in range(8):
                        MM(bk[:, 0:512], mTl[k][:, tb_ * 128:(tb_ + 1) * 128], WO[:, k, hf2 * 512:(hf2 + 1) * 512], k == 0, k == 7,
                           ["mT%d" % k] + K_O, [bkk])
                    TT("dve", rs[:, hf2 * 512:(hf2 + 1) * 512], bk[:, 0:512], xtile[:, hf2 * 512:(hf2 + 1) * 512], ALU.add, [bkk, xk], [rsk])
                yield
                jx, jxk = xsb[oi], "xsb%d" % oi
                ACT(jx[:, :], rs[:, :], AF.Square, [rsk], [jxk, sk], accum_out=sst[:, 0:1])
                ACT(sst[:, 1:2], sst[:, 0:1], AF.Ln, [sk, "epsT"], [sk], scale=1.0 / D, bias=epsT[:, 0:1])
                ACT(sst[:, 2:3], sst[:, 1:2], AF.Exp, [sk], [sk], scale=-0.5)
                yo, yok = xtile, xk
                STT("dve", yo[:, :], rs[:, :], sst[:, 2:3], fgB[:, :], [rsk, sk, "fgB"], [yok])
                orow = p * QP + blk * 128
                DMA("pool", yout[orow:orow + 128, :], yo[:, :], reads=[yok], key="st_xt%d" % i)

            optasks = [outproj_task(tb_) for tb_ in range(NBC)]
            if ch == NGQ - 1 and p + 1 < NPASS:
                st["nb"] = 7
                btasks = make_B_tasks(p + 1)
                merged = []
                for k_ in range(max(len(optasks), len(btasks))):
                    if k_ < len(optasks):
                        merged.append(optasks[k_])
                    if k_ < len(btasks):
                        merged.append(btasks[k_])
                pipeline(merged)
                B_done.add(p + 1)
                pend_cq = True
            else:
                pipeline(optasks)
                pend_cq = False
        if pend_cq:
            emit_B_cq()
    n = P.emit()
    print("ops:", n)
    return nc


def host_inputs(cfg, x_prompt, x_sample, ln_g, w_in, b_gate, sink_a, q_norm_g, kv_norm_g, w_uq, w_ukv,
                w_proj_a, w_proj_b, w_out, rel_bias, final_g):
    f = np.float32
    x_prompt = np.asarray(x_prompt, f)
    x_sample = np.asarray(x_sample, f)
    xp = np.ascontiguousarray(x_prompt[0])
    NBW = cfg.NBQ + 2
    QP = cfg.QP
    half = ROPE // 2
    inv = np.power(np.float32(10000.0), -np.arange(half, dtype=f) / half).astype(f)
    pos = np.arange(cfg.TMAX, dtype=f)
    ang = pos[None, :] * inv[:, None]
    cos2 = np.concatenate([np.cos(ang), np.cos(ang)], 0).astype(f)
    sin2 = np.concatenate([np.sin(ang), np.sin(ang)], 0).astype(f)
    ropek = np.ascontiguousarray(np.concatenate([cos2, sin2], 0))
    j = np.arange(512)
    rel = 255 - j
    bucket = t5_bucket_np(rel.astype(np.int32))
    onehot = np.zeros((NBUCK, 512), f)
    onehot[bucket[:511], np.arange(511)] = 1.0
    valid = (np.abs(rel) <= 128).astype(f)
    valid[511] = 0.0
    validv = np.ascontiguousarray(np.broadcast_to(valid[None, :], (8, 512))).astype(f)
    shared = dict(
        xp=xp, ropek=ropek, onehot=onehot, validv=validv,
        w_in=np.ascontiguousarray(np.asarray(w_in, f)[0]),
        ln_g=np.ascontiguousarray(np.asarray(ln_g, f)[0].reshape(8, 128).T),
        b_gate=np.ascontiguousarray(np.asarray(b_gate, f)[0].reshape(16, 128).T),
        sink_a=np.ascontiguousarray(np.asarray(sink_a, f)[0]),
        q_norm_g=np.ascontiguousarray(np.asarray(q_norm_g, f)[0].reshape(2, 128).T),
        kv_norm_g=np.ascontiguousarray(np.asarray(kv_norm_g, f)[0].reshape(1, 128).T),
        w_uq=np.ascontiguousarray(np.asarray(w_uq, f)[0]),
        w_ukv=np.ascontiguousarray(np.asarray(w_ukv, f)[0]),
        w_proj_a=np.ascontiguousarray(np.asarray(w_proj_a, f)[0]),
        w_proj_b=np.ascontiguousarray(np.asarray(w_proj_b, f)[0]),
        w_out=np.ascontiguousarray(np.asarray(w_out, f)[0]),
        rel_bias=np.ascontiguousarray(np.asarray(rel_bias, f)),
        final_g=np.ascontiguousarray(np.asarray(final_g, f)),
    )
    in_maps = []
    for c in range(cfg.NC):
        xs = np.ascontiguousarray(x_sample[c * cfg.NS:(c + 1) * cfg.NS].reshape(cfg.NS * cfg.DSEQ, D))
        xq = np.zeros((cfg.NPASS, NBW * 128, D), f)
        flags = np.zeros((cfg.NPASS, 2), f)
        ropeq = np.zeros((cfg.NPASS, 64, QP), f)
        for p, (seg, hf) in enumerate(cfg.passes):
            if seg == 0:
                seq, S = xp, cfg.SEQ
                start = c * cfg.OWNP + hf * QP
            else:
                seq, S = x_sample[c * cfg.NS + seg - 1], cfg.DSEQ
                start = hf * QP
            lo, hi = start - 128, start + QP + 128
            a, b = max(lo, 0), min(hi, S)
            xq[p, a - lo:b - lo] = seq[a:b]
            flags[p, 0] = 1.0 if lo >= 0 else 0.0
            flags[p, 1] = 1.0 if hi <= S else 0.0
            ropeq[p] = ropek[:, start:start + QP]
        m = dict(shared)
        m.update(xs=xs, xq=xq.reshape(cfg.NPASS * NBW * 128, D), flags=flags.reshape(-1), ropeq=ropeq)
        in_maps.append(m)
    return in_maps


def assemble(cfg, results):
    yp = np.zeros((1, cfg.SEQ, D), np.float32)
    ysm = np.zeros((cfg.DB, cfg.DSEQ, D), np.float32)
    for c in range(cfg.NC):
        y = np.asarray(results[c]["y"]).reshape(cfg.NPASS, cfg.QP, D)
        for p, (seg, hf) in enumerate(cfg.passes):
            if seg == 0:
                s0 = c * cfg.OWNP + hf * cfg.QP
                yp[0, s0:s0 + cfg.QP] = y[p]
            else:
                ysm[c * cfg.NS + seg - 1, hf * cfg.QP:(hf + 1) * cfg.QP] = y[p]
    return yp, ysm


def run(cfg, inputs, trace=False):
    nc = build_program(cfg)
    in_maps = host_inputs(cfg, **inputs)
    res = run_bass_kernel_spmd(nc, in_maps, core_ids=list(range(cfg.NC)), **({"trace": True} if trace else {}))
    return assemble(cfg, res.results), res


def kernel(**inputs):
    cfg = Cfg()
    (yp, ysm), _ = run(cfg, inputs)
    return yp, ysm
```
